# Optimizing a Trainium2 kernel written in Bass

```python
import math
import jax, jax.numpy as jnp
from jax import lax
import numpy as np

D_MODEL = 1024
BATCH = 8
SEQ = 2048
DEPTH = 4
DEC_BATCH = 128
DEC_SEQ = 8
PAST_LEN = 16384
PAGE_SIZE = 128

A_HEADS = 4
A_KEY = 128
A_VAL = 128
B_HEADS = 4
B_QK = 64
B_VAL = 128
C_HEADS = 4
C_KEY = 128
C_VAL = 128
CONV_W = 4
CONV_CH = C_HEADS * (2 * C_KEY + C_VAL)
N_BRANCH = 3
BRANCH_W = 512
FFN_HIDDEN = -(-(8 * D_MODEL) // (3 * 256)) * 256
CHUNK = 64
ALPHA = (2 * DEPTH) ** 0.25
BETA_INIT = (8 * DEPTH) ** -0.25
LN_EPS = 1e-5
NORM_EPS = 1e-6
NEG_BIG = -1e30

SPLITS = (A_HEADS * A_KEY, A_HEADS * A_KEY, A_HEADS * A_VAL, A_HEADS * A_VAL,
          B_HEADS * B_QK, B_HEADS * B_QK, B_HEADS * B_VAL, B_HEADS * B_VAL, B_HEADS, B_HEADS,
          CONV_CH, C_HEADS * C_VAL, C_HEADS, C_HEADS,
          N_BRANCH * D_MODEL)
N_IN = sum(SPLITS)
SPLIT_IDX = tuple(int(v) for v in np.cumsum(SPLITS)[:-1])

kernel_name = 'hybrid_hgrn2_mlstm_gdn_step'


def _layer_norm(x, g, b):
    mu = jnp.mean(x, -1, keepdims=True)
    xc = x - mu
    var = jnp.mean(xc * xc, -1, keepdims=True)
    return xc * lax.rsqrt(var + LN_EPS) * g.astype(jnp.float32) + b.astype(jnp.float32)


def _heads(t, h):
    B, L, _ = t.shape
    return t.reshape(B, L, h, -1).transpose(0, 2, 1, 3)


def _head_rms_merge(o, g):
    o = o * lax.rsqrt(jnp.mean(o * o, -1, keepdims=True) + NORM_EPS)
    B, H, L, d = o.shape
    return o.transpose(0, 2, 1, 3).reshape(B, L, H * d) * g.astype(jnp.float32)


def _l2norm(t):
    return t * lax.rsqrt(jnp.sum(t * t, -1, keepdims=True) + NORM_EPS)


def _masked_exp(mask, t):
    return jnp.where(mask, jnp.exp(jnp.where(mask, t, 0.0)), 0.0)


def _chunk(t, c):
    B, H, L = t.shape[:3]
    t = t.reshape((B, H, L // c, c) + t.shape[3:])
    return jnp.moveaxis(t, 2, 0)


def _unchunk(o):
    n, B, H, c = o.shape[:4]
    o = jnp.moveaxis(o, 0, 2)
    return o.reshape((B, H, n * c) + o.shape[4:])


def _hgrn2_scan(q, k, v, logf, S0):
    c = math.gcd(q.shape[2], CHUNK)
    incl = jnp.tril(jnp.ones((c, c), bool))

    def step(S, blk):
        qc, kc, vc, fc = blk
        b = jnp.cumsum(fc, axis=2)
        diff = b[:, :, :, None, :] - b[:, :, None, :, :]
        dec = _masked_exp(incl[:, :, None], diff)
        att = jnp.einsum('bhtk,bhsk,bhtsk->bhts', qc, kc, dec)
        o = jnp.einsum('bhtk,bhkv->bhtv', qc * jnp.exp(b), S) + jnp.einsum('bhts,bhsv->bhtv', att, vc)
        bl = b[:, :, -1, :]
        S_new = jnp.exp(bl)[..., None] * S + jnp.einsum('bhsk,bhsv->bhkv', kc * jnp.exp(bl[:, :, None, :] - b), vc)
        return S_new, o

    S, o = lax.scan(step, S0, (_chunk(q, c), _chunk(k, c), _chunk(v, c), _chunk(logf, c)))
    return _unchunk(o), S


def _mlstm_scan(q, k, v, ig, lf, C0, n0, m0):
    c = math.gcd(q.shape[2], CHUNK)
    incl = jnp.tril(jnp.ones((c, c), bool))

    def step(carry, blk):
        Cm, n, m = carry
        qc, kc, vc, ic, fc = blk
        b = jnp.cumsum(fc, axis=-1)
        logd = jnp.where(incl, b[..., :, None] - b[..., None, :] + ic[..., None, :], NEG_BIG)
        m_inter = b + m[..., None]
        m_t = jnp.maximum(m_inter, jnp.max(logd, -1))
        w_inter = jnp.exp(m_inter - m_t)
        sc = jnp.einsum('bhtk,bhsk->bhts', qc, kc) * _masked_exp(incl, logd - m_t[..., None])
        num = w_inter[..., None] * jnp.einsum('bhvk,bhtk->bhtv', Cm, qc) + jnp.einsum('bhts,bhsv->bhtv', sc, vc)
        den = w_inter * jnp.einsum('bhk,bhtk->bht', n, qc) + jnp.sum(sc, -1)
        h = num / jnp.maximum(jnp.abs(den), jnp.exp(-m_t))[..., None]
        g = b[..., -1:] - b + ic
        m_new = jnp.maximum(b[..., -1] + m, jnp.max(g, -1))
        w_old = jnp.exp(b[..., -1] + m - m_new)
        wk = jnp.exp(g - m_new[..., None])
        C_new = w_old[..., None, None] * Cm + jnp.einsum('bhs,bhsv,bhsk->bhvk', wk, vc, kc)
        n_new = w_old[..., None] * n + jnp.einsum('bhs,bhsk->bhk', wk, kc)
        return (C_new, n_new, m_new), h

    (Cm, n, m), o = lax.scan(step, (C0, n0, m0),
                             (_chunk(q, c), _chunk(k, c), _chunk(v, c), _chunk(ig, c), _chunk(lf, c)))
    return _unchunk(o), Cm, n, m


def _gdn_scan(q, k, v, beta, g, S0):
    c = math.gcd(q.shape[2], CHUNK)
    strict = jnp.tril(jnp.ones((c, c), bool), -1)
    incl = jnp.tril(jnp.ones((c, c), bool))
    eye = jnp.eye(c, dtype=jnp.float32)
    dv = v.shape[-1]

    def step(S, blk):
        qc, kc, vc, bc, gc = blk
        b = jnp.cumsum(gc, axis=-1)
        diff = b[..., :, None] - b[..., None, :]
        a = bc[..., :, None] * jnp.einsum('bhtk,bhsk->bhts', kc, kc) * _masked_exp(strict, diff)
        rhs = jnp.concatenate([vc * bc[..., None], kc * (bc * jnp.exp(b))[..., None]], axis=-1)
        sol = lax.linalg.triangular_solve(eye + a, rhs, left_side=True, lower=True, unit_diagonal=True)
        v_new = sol[..., :dv] - jnp.einsum('bhtk,bhkv->bhtv', sol[..., dv:], S)
        att = jnp.einsum('bhtk,bhsk->bhts', qc, kc) * _masked_exp(incl, diff)
        o = jnp.einsum('bhtk,bhkv->bhtv', qc * jnp.exp(b)[..., None], S) + jnp.einsum('bhts,bhsv->bhtv', att, v_new)
        bl = b[..., -1]
        S_new = jnp.exp(bl)[..., None, None] * S + jnp.einsum('bhsk,bhsv->bhkv', kc * jnp.exp(bl[..., None] - b)[..., None], v_new)
        return S_new, o

    S, o = lax.scan(step, S0, (_chunk(q, c), _chunk(k, c), _chunk(v, c), _chunk(beta, c), _chunk(g, c)))
    return _unchunk(o), S


def _causal_conv(xc, buf, w):
    L = xc.shape[1]
    xp = jnp.concatenate([buf, xc], axis=1)
    out = xp[:, 0:L] * w[0]
    for j in range(1, CONV_W):
        out = out + xp[:, j:j + L] * w[j]
    return out, xp[:, L:]


def _trunk(x, st_a, st_c, st_n, st_m, st_g, st_conv,
           w_in, lb_logits, a_norm_g, b_mi, b_mf, b_norm_g, conv_w, a_log, dt_bias, c_norm_g,
           w_branch, w_out, ln1_g, ln1_b, w_ffn_in, w_ffn_out, ln2_g, ln2_b):
    f32 = jnp.float32
    dt = x.dtype
    B, L, _ = x.shape
    lb_w = jax.nn.softmax(lb_logits.astype(f32), axis=0)
    lb_all = jnp.cumsum(lb_w, axis=0) - lb_w[0]
    na, nc, nn, nm, ng, nv = [], [], [], [], [], []
    h = x
    for l in range(DEPTH):
        proj = jnp.einsum('bld,dn->bln', h, w_in[l])
        (aq, af, ai, ag, bq, bk, bv, bo, bi, bf, cx, cg, cb, ca, mg) = jnp.split(proj, SPLIT_IDX, axis=-1)

        lb = lb_all[l]
        af32 = af.astype(f32)
        a_logf = jnp.log(lb + (1.0 - lb) * jax.nn.sigmoid(af32))
        a_k = (1.0 - lb) * jax.nn.sigmoid(-af32)
        a_o, a_S = _hgrn2_scan(_heads(jax.nn.silu(aq.astype(f32)), A_HEADS), _heads(a_k, A_HEADS),
                               _heads(ai.astype(f32), A_HEADS), _heads(a_logf, A_HEADS), st_a[l].astype(f32))
        y_a = _head_rms_merge(a_o, a_norm_g[l]) * jax.nn.silu(ag.astype(f32))

        b_ig = (bi.astype(f32) + b_mi[l].astype(f32)).transpose(0, 2, 1)
        b_lf = jax.nn.log_sigmoid(bf.astype(f32) + b_mf[l].astype(f32)).transpose(0, 2, 1)
        b_o, b_C, b_n, b_m = _mlstm_scan(_heads(bq.astype(f32), B_HEADS) * B_QK ** -0.5,
                                         _heads(bk.astype(f32), B_HEADS), _heads(bv.astype(f32), B_HEADS),
                                         b_ig, b_lf, st_c[l].astype(f32), st_n[l].astype(f32), st_m[l].astype(f32))
        y_b = _head_rms_merge(b_o, b_norm_g[l]) * jax.nn.sigmoid(bo.astype(f32))

        cconv, cbuf = _causal_conv(cx.astype(f32), st_conv[l].astype(f32), conv_w[l].astype(f32))
        cconv = jax.nn.silu(cconv)
        cq, ck, cv = jnp.split(cconv, [C_HEADS * C_KEY, 2 * C_HEADS * C_KEY], axis=-1)
        cq = _l2norm(_heads(cq, C_HEADS)) * C_KEY ** -0.5
        ck = _l2norm(_heads(ck, C_HEADS))
        c_beta = jax.nn.sigmoid(cb.astype(f32)).transpose(0, 2, 1)
        c_g = (-jnp.exp(a_log[l].astype(f32)) * jax.nn.softplus(ca.astype(f32) + dt_bias[l].astype(f32))).transpose(0, 2, 1)
        c_o, c_S = _gdn_scan(cq, ck, _heads(cv, C_HEADS), c_beta, c_g, st_g[l].astype(f32))
        y_c = _head_rms_merge(c_o, c_norm_g[l]) * jax.nn.silu(cg.astype(f32))

        ys = jnp.stack([y_a, y_b, y_c], axis=2).astype(dt)
        z = jnp.einsum('blnc,ncd->blnd', ys, w_branch[l])
        gates = jax.nn.sigmoid(mg.reshape(B, L, N_BRANCH, D_MODEL).astype(f32))
        merged = jnp.sum(gates * z.astype(f32), axis=2).astype(dt)
        mix = merged @ w_out[l]
        h = _layer_norm(ALPHA * h.astype(f32) + mix.astype(f32), ln1_g[l], ln1_b[l]).astype(dt)

        gate, up = jnp.split(h @ w_ffn_in[l], 2, axis=-1)
        ff = (jax.nn.silu(gate) * up) @ w_ffn_out[l]
        h = _layer_norm(ALPHA * h.astype(f32) + ff.astype(f32), ln2_g[l], ln2_b[l]).astype(dt)

        na.append(a_S); nc.append(b_C); nn.append(b_n); nm.append(b_m); ng.append(c_S); nv.append(cbuf)
    return (h, jnp.stack(na), jnp.stack(nc), jnp.stack(nn), jnp.stack(nm), jnp.stack(ng), jnp.stack(nv))


def setup_inputs(seed: int = 0) -> dict:
    key = jax.random.key(seed)
    ks = jax.random.split(key, 32)
    f32 = jnp.float32

    def nrm(k, shape, s):
        return s * jax.random.normal(k, shape, f32)

    dt0 = jnp.exp(jax.random.uniform(ks[14], (DEPTH, C_HEADS), f32, math.log(1e-3), math.log(1e-1)))
    return {
        'x_prompt': nrm(ks[0], (BATCH, SEQ, D_MODEL), 1.0),
        'x_sample': nrm(ks[1], (DEC_BATCH, DEC_SEQ, D_MODEL), 1.0),
        'state_hgrn': nrm(ks[2], (DEPTH, DEC_BATCH, A_HEADS, A_KEY, A_VAL), 0.5),
        'state_mlstm_c': nrm(ks[3], (DEPTH, DEC_BATCH, B_HEADS, B_VAL, B_QK), 0.3),
        'state_mlstm_n': nrm(ks[4], (DEPTH, DEC_BATCH, B_HEADS, B_QK), 0.3),
        'state_mlstm_m': 1.0 + nrm(ks[5], (DEPTH, DEC_BATCH, B_HEADS), 0.5),
        'state_gdn': nrm(ks[6], (DEPTH, DEC_BATCH, C_HEADS, C_KEY, C_VAL), 0.2),
        'state_gdn_conv': nrm(ks[7], (DEPTH, DEC_BATCH, CONV_W - 1, CONV_CH), 1.0),
        'w_in': nrm(ks[8], (DEPTH, D_MODEL, N_IN), D_MODEL ** -0.5),
        'lb_logits': nrm(ks[9], (DEPTH, A_HEADS * A_KEY), 0.5),
        'a_norm_g': 1.0 + nrm(ks[10], (DEPTH, A_HEADS * A_VAL), 0.02),
        'b_mi': nrm(ks[11], (DEPTH, B_HEADS), 0.1),
        'b_mf': jnp.linspace(3.0, 6.0, B_HEADS, dtype=f32)[None, :] + nrm(ks[12], (DEPTH, B_HEADS), 0.1),
        'b_norm_g': 1.0 + nrm(ks[13], (DEPTH, B_HEADS * B_VAL), 0.02),
        'conv_w': nrm(ks[15], (DEPTH, CONV_W, CONV_CH), CONV_W ** -0.5),
        'a_log': jnp.log(jax.random.uniform(ks[16], (DEPTH, C_HEADS), f32, 1.0, 16.0)),
        'dt_bias': dt0 + jnp.log(-jnp.expm1(-dt0)),
        'c_norm_g': 1.0 + nrm(ks[17], (DEPTH, C_HEADS * C_VAL), 0.02),
        'w_branch': nrm(ks[18], (DEPTH, N_BRANCH, BRANCH_W, D_MODEL), BRANCH_W ** -0.5),
        'w_out': nrm(ks[19], (DEPTH, D_MODEL, D_MODEL), BETA_INIT * D_MODEL ** -0.5),
        'ln1_g': 1.0 + nrm(ks[20], (DEPTH, D_MODEL), 0.02),
        'ln1_b': nrm(ks[21], (DEPTH, D_MODEL), 0.02),
        'w_ffn_in': nrm(ks[22], (DEPTH, D_MODEL, 2 * FFN_HIDDEN), D_MODEL ** -0.5),
        'w_ffn_out': nrm(ks[23], (DEPTH, FFN_HIDDEN, D_MODEL), BETA_INIT * FFN_HIDDEN ** -0.5),
        'ln2_g': 1.0 + nrm(ks[24], (DEPTH, D_MODEL), 0.02),
        'ln2_b': nrm(ks[25], (DEPTH, D_MODEL), 0.02),
    }


def reference(x_prompt, x_sample, state_hgrn, state_mlstm_c, state_mlstm_n, state_mlstm_m, state_gdn,
              state_gdn_conv, w_in, lb_logits, a_norm_g, b_mi, b_mf, b_norm_g, conv_w, a_log, dt_bias,
              c_norm_g, w_branch, w_out, ln1_g, ln1_b, w_ffn_in, w_ffn_out, ln2_g, ln2_b):
    weights = (w_in, lb_logits, a_norm_g, b_mi, b_mf, b_norm_g, conv_w, a_log, dt_bias, c_norm_g,
               w_branch, w_out, ln1_g, ln1_b, w_ffn_in, w_ffn_out, ln2_g, ln2_b)
    f32 = jnp.float32
    Bp = x_prompt.shape[0]
    y_p, pa, pc, pn, pm, pg, pv = _trunk(
        x_prompt,
        jnp.zeros((DEPTH, Bp, A_HEADS, A_KEY, A_VAL), f32),
        jnp.zeros((DEPTH, Bp, B_HEADS, B_VAL, B_QK), f32),
        jnp.zeros((DEPTH, Bp, B_HEADS, B_QK), f32),
        jnp.zeros((DEPTH, Bp, B_HEADS), f32),
        jnp.zeros((DEPTH, Bp, C_HEADS, C_KEY, C_VAL), f32),
        jnp.zeros((DEPTH, Bp, CONV_W - 1, CONV_CH), f32),
        *weights)
    y_s, sa, sc, sn, sm, sg, sv = _trunk(
        x_sample, state_hgrn, state_mlstm_c, state_mlstm_n, state_mlstm_m, state_gdn, state_gdn_conv, *weights)
    return (y_p, y_s,
            pa.astype(state_hgrn.dtype), sa.astype(state_hgrn.dtype),
            pc.astype(state_mlstm_c.dtype), sc.astype(state_mlstm_c.dtype),
            pn.astype(state_mlstm_n.dtype), sn.astype(state_mlstm_n.dtype),
            pm.astype(state_mlstm_m.dtype), sm.astype(state_mlstm_m.dtype),
            pg.astype(state_gdn.dtype), sg.astype(state_gdn.dtype),
            pv.astype(state_gdn_conv.dtype), sv.astype(state_gdn_conv.dtype))
```

```python
import numpy as np
from contextlib import ExitStack
import concourse.bass as bass
import concourse.mybir as mybir
from concourse.bass_utils import run_bass_kernel_spmd

F32 = mybir.dt.float32
BF16 = mybir.dt.bfloat16
ALU = mybir.AluOpType
AF = mybir.ActivationFunctionType
AX = mybir.AxisListType

D = 1024
NIN = 8720
FH = 2816
DEPTH = 4
ALPHA = (2 * DEPTH) ** 0.25
LN_EPS = 1e-5
NORM_EPS = 1e-6
BIG = 30000.0

ENGS = ("pe", "act", "dve", "pool", "sp")
N_DMA_SEMS = 8
SAME_ENG_SYNC = True


class Res:
    def __init__(self, name=""):
        self.name = name
        self.last_write = None
        self.reads = {}
        self.excl = False


class Tl:
    def __init__(self, t, name):
        self.t = t
        self.res = Res(name)
        self.name = name

    def __getitem__(self, idx):
        return self.t[idx]


class Ctx:
    def __init__(self, nc, stack):
        self.nc = nc
        self.stack = stack
        self.ops = {e: [] for e in ENGS}
        self.seq = {e: 0 for e in ENGS}
        self.esem = {e: stack.enter_context(nc.semaphore("s_" + e)) for e in ENGS}
        self.dsem, self.dcount, self.dnext = {}, {}, {}
        for q in ("sp", "act", "pool"):
            self.dsem[q] = [stack.enter_context(nc.semaphore(f"d_{q}{i}")) for i in range(N_DMA_SEMS)]
            self.dcount[q] = [0] * N_DMA_SEMS
            self.dnext[q] = 0
        self.waited = {e: {} for e in ENGS}
        self.semobj = {("e", e): self.esem[e] for e in ENGS}
        for q in self.dsem:
            for i, s in enumerate(self.dsem[q]):
                self.semobj[("d", q, i)] = s
        self.n_inst = 0
        self.out_tokens = []

    def sb(self, name, shape, dtype=F32):
        t = self.stack.enter_context(self.nc.sbuf_tensor(name, list(shape), dtype))
        return Tl(t, name)

    def ps(self, name, shape, dtype=F32):
        t = self.stack.enter_context(self.nc.psum_tensor(name, list(shape), dtype))
        tl = Tl(t, name)
        tl.res.excl = True
        return tl

    def _collect(self, eng, reads, writes):
        deps = {}

        def need(tok):
            if tok is None:
                return
            key, val = tok
            if key == ("e", eng) and (eng == "pe" or not SAME_ENG_SYNC):
                return
            if val > deps.get(key, 0):
                deps[key] = val
        for r in reads:
            r = getattr(r, "res", r)
            need(r.last_write)
            if r.excl:
                for k, v in r.reads.items():
                    need((k, v))
        for w in writes:
            w = getattr(w, "res", w)
            need(w.last_write)
            for k, v in w.reads.items():
                need((k, v))
        out = []
        wd = self.waited[eng]
        for k, v in deps.items():
            if wd.get(k, 0) >= v:
                continue
            wd[k] = v
            out.append((k, v))
        return out

    def _commit(self, reads, writes, token):
        key, val = token
        for r in reads:
            r = getattr(r, "res", r)
            if r.reads.get(key, 0) < val:
                r.reads[key] = val
        for w in writes:
            w = getattr(w, "res", w)
            w.last_write = token
            w.reads = {}

    def op(self, eng, fn, reads=(), writes=()):
        waits = self._collect(eng, reads, writes)
        self.seq[eng] += 1
        token = (("e", eng), self.seq[eng])
        self.ops[eng].append(("op", fn, waits))
        self._commit(reads, writes, token)
        self.n_inst += 1

    def dma(self, q, out, in_, reads=(), writes=(), is_output=False, **kw):
        i = self.dnext[q]
        self.dnext[q] = (i + 1) % N_DMA_SEMS
        waits = self._collect(q, reads, writes)
        key = ("d", q, i)
        prev = self.dcount[q][i] * 16
        if prev > 0 and self.waited[q].get(key, 0) < prev:
            self.waited[q][key] = prev
            waits.append((key, prev))
        self.dcount[q][i] += 1
        token = (key, self.dcount[q][i] * 16)
        self.ops[q].append(("dma", (out, in_, kw, self.dsem[q][i]), waits))
        self._commit(reads, writes, token)
        self.n_inst += 1
        return token

    def finish(self):
        toks = []
        for q in self.dsem:
            for i in range(N_DMA_SEMS):
                if self.dcount[q][i] > 0:
                    toks.append((("d", q, i), self.dcount[q][i] * 16))
        for e in ENGS:
            if self.seq[e] > 0:
                toks.append((("e", e), self.seq[e]))
        self.ops["sp"].append(("wait", None, toks))
        nc = self.nc
        block = self.stack.enter_context(nc.Block())
        ctx = self

        def run(eng_name, engobj):
            esem = ctx.esem[eng_name]
            for kind, payload, waits in ctx.ops[eng_name]:
                for key, val in waits:
                    engobj.wait_ge(ctx.semobj[key], val)
                if kind == "op":
                    payload(engobj).then_inc(esem, 1)
                elif kind == "dma":
                    out, in_, kw, sem = payload
                    engobj.dma_start(out=out, in_=in_, **kw).then_inc(sem, 16)

        block.sync(lambda e: run("sp", e))
        block.scalar(lambda e: run("act", e))
        block.vector(lambda e: run("dve", e))
        block.gpsimd(lambda e: run("pool", e))
        block.tensor(lambda e: run("pe", e))


def _const_layout():
    cols = {}
    off = 0

    def add(name, n):
        nonlocal off
        cols[name] = (off, n)
        off += n
    add("ident", 128)
    add("maskA64", 128)
    add("mask8", 128)
    add("mask128", 128)
    add("posP_incl", 128)
    add("posS_incl", 128)
    add("posP_strict_ts", 128)
    add("posS_strict_ts", 128)
    add("sel", 512)
    add("onecol", 16)
    add("i4", 4)
    add("rm2", 2)
    add("rm16", 16)
    add("rst64", 256)
    add("rst8", 256)
    add("rst128", 256)
    add("ones", 256)
    add("nident", 128)
    return cols, off


CL, NCONST = _const_layout()


def _make_consts():
    c = np.zeros((128, NCONST), np.float32)

    def put(name, arr):
        o, n = CL[name]
        c[:arr.shape[0], o:o + n] = arr
    s = np.arange(128)[:, None]
    t = np.arange(128)[None, :]
    put("ident", (s == t).astype(np.float32))
    put("maskA64", ((s // 64 == t // 64) & (s <= t)).astype(np.float32))
    put("mask8", ((s // 8 == t // 8) & (s <= t)).astype(np.float32))
    put("mask128", (s <= t).astype(np.float32))
    put("posP_incl", np.where(s <= t, 0.0, BIG).astype(np.float32))
    put("posS_incl", np.where((s // 8 == t // 8) & (s <= t), 0.0, BIG).astype(np.float32))
    tt = np.arange(128)[:, None]
    ss = np.arange(128)[None, :]
    put("posP_strict_ts", np.where(ss < tt, 0.0, BIG).astype(np.float32))
    put("posS_strict_ts", np.where((ss // 8 == tt // 8) & (ss < tt), 0.0, BIG).astype(np.float32))
    sel = np.zeros((4, 512), np.float32)
    for h in range(4):
        sel[h, h * 128:(h + 1) * 128] = 1.0
    put("sel", sel)
    oc = np.zeros((128, 16), np.float32)
    for h in range(4):
        oc[:, 4 * h + h] = 1.0
    put("onecol", oc)
    put("i4", np.eye(4, dtype=np.float32))
    put("rm2", (s // 64 == np.arange(2)[None, :]).astype(np.float32))
    put("rm16", (s // 8 == np.arange(16)[None, :]).astype(np.float32))
    tr = np.arange(256)[None, :]
    put("rst64", np.broadcast_to((tr % 64 != 0).astype(np.float32), (128, 256)))
    put("rst8", np.broadcast_to((tr % 8 != 0).astype(np.float32), (128, 256)))
    put("rst128", np.broadcast_to((tr % 128 != 0).astype(np.float32), (128, 256)))
    put("ones", np.ones((128, 256), np.float32))
    put("nident", -(s == t).astype(np.float32))
    return c


def _extra_consts(c):
    return c


class WStream:
    SLOT = 2048

    def __init__(self, c, nslot, pf):
        self.c = c
        self.nslot = nslot
        self.pf = pf
        self.slots = [c.sb(f"wslot{i}", [128, self.SLOT], BF16) for i in range(nslot)]
        self.sched = []
        self.blocks = {}
        self.rec = True
        self.pos = 0
        self.issued = 0
        self.scratch = None
        self.bres = []

    def start_real(self, nc):
        self.rec = False
        self.pos = 0
        self.issued = 0
        nb = max(1, len(self.blocks))
        self.scratch = nc.dram_tensor("w_scratch", [nb, 128, self.SLOT], BF16, kind="Internal").ap()
        self.bres = [Res(f"wblk{i}") for i in range(nb)]

    def emit_conversions(self):
        for key, (idx, src, k, n) in self.blocks.items():
            dst = self.scratch[idx][:, 0:k * n].rearrange("p (k n) -> p k n", k=k)
            self.c.dma("pool", dst, src, writes=[self.bres[idx]])

    def _view(self, i, k, n):
        s = self.slots[i % self.nslot]
        return s[:, 0:k * n].rearrange("p (k n) -> p k n", k=k), s

    def get(self, key, src, k, n):
        assert k * n <= self.SLOT
        if self.rec:
            if key not in self.blocks:
                self.blocks[key] = (len(self.blocks), src, k, n)
            self.sched.append((key, k, n))
            return self._view(len(self.sched) - 1, k, n)
        i = self.pos
        self.pos += 1
        lim = min(len(self.sched), i + 1 + self.pf)
        while self.issued < lim:
            j = self.issued
            skey, sk, sn = self.sched[j]
            idx = self.blocks[skey][0]
            s = self.slots[j % self.nslot]
            self.c.dma("sp", s[:, 0:sk * sn], self.scratch[idx][:, 0:sk * sn], reads=[self.bres[idx]], writes=[s])
            self.issued += 1
        return self._view(i, k, n)


class Cfg:
    def __init__(self, depth=DEPTH, n_pg=8, tg=2, sample=True, dbg=False, parts="ABCMF"):
        self.parts = parts
        self.depth = depth
        self.n_pg = n_pg
        self.tg = tg
        self.sample = sample
        self.dbg = dbg
        self.TMAX = tg * 128
        self.seq = n_pg * tg * 128


IN_NAMES = ["xp", "xs", "st_a", "st_c", "st_n", "st_m", "st_g", "st_v",
            "w_in", "lb_logits", "a_norm_g", "b_mi", "b_mf", "b_norm_g", "conv_w", "a_log", "dt_bias",
            "c_norm_g", "w_branch", "w_out", "ln1_g", "ln1_b", "w_ffn_in", "w_ffn_out", "ln2_g", "ln2_b", "consts"]


class Builder:
    def __init__(self, cfg):
        self.cfg = cfg
        L = cfg.depth
        nc = bass.Bass("TRN2", target_bir_lowering=False)
        self.nc = nc

        def din(name, shape):
            return nc.dram_tensor(name, list(shape), F32, kind="ExternalInput").ap()

        def dout(name, shape):
            return nc.dram_tensor(name, list(shape), F32, kind="ExternalOutput").ap()
        S = cfg.seq
        self.d = dict(
            xp=din("xp", [S, D]), xs=din("xs", [128, D]),
            st_a=din("st_a", [L, 16, 4, 128, 128]), st_c=din("st_c", [L, 16, 4, 128, 64]),
            st_n=din("st_n", [L, 16, 4, 64]), st_m=din("st_m", [L, 16, 4]),
            st_g=din("st_g", [L, 16, 4, 128, 128]), st_v=din("st_v", [L, 16, 3, 1536]),
            w_in=din("w_in", [L, D, NIN]), lb_logits=din("lb_logits", [L, 512]),
            a_norm_g=din("a_norm_g", [L, 512]), b_mi=din("b_mi", [L, 4]), b_mf=din("b_mf", [L, 4]),
            b_norm_g=din("b_norm_g", [L, 512]), conv_w=din("conv_w", [L, 4, 1536]),
            a_log=din("a_log", [L, 4]), dt_bias=din("dt_bias", [L, 4]), c_norm_g=din("c_norm_g", [L, 512]),
            w_branch=din("w_branch", [L, 3, 512, D]), w_out=din("w_out", [L, D, D]),
            ln1_g=din("ln1_g", [L, D]), ln1_b=din("ln1_b", [L, D]),
            w_ffn_in=din("w_ffn_in", [L, D, 2 * FH]), w_ffn_out=din("w_ffn_out", [L, FH, D]),
            ln2_g=din("ln2_g", [L, D]), ln2_b=din("ln2_b", [L, D]),
            consts=din("consts", [128, NCONST]),
        )
        self.o = dict(
            yp=dout("yp", [S, D]), ys=dout("ys", [128, D]),
            oa_p=dout("oa_p", [L, 4, 128, 128]), oa_s=dout("oa_s", [L, 16, 4, 128, 128]),
            oc_p=dout("oc_p", [L, 4, 128, 64]), oc_s=dout("oc_s", [L, 16, 4, 128, 64]),
            on_p=dout("on_p", [L, 4, 64]), on_s=dout("on_s", [L, 16, 4, 64]),
            om_p=dout("om_p", [L, 4]), om_s=dout("om_s", [L, 16, 4]),
            og_p=dout("og_p", [L, 4, 128, 128]), og_s=dout("og_s", [L, 16, 4, 128, 128]),
            ov_p=dout("ov_p", [L, 3, 1536]), ov_s=dout("ov_s", [L, 16, 3, 1536]),
        )
        self.dbg_aps = {}
        self.dbg_off = 0
        if cfg.dbg:
            self.dbgt = dout("dbg", [128, 65536])

    def build(self):
        with ExitStack() as st:
            self.st = st
            self.c = Ctx(self.nc, st)
            self.alloc()
            self.c_real = self.c
            self.c = DryCtx()
            self.ws.c = self.c
            self.emit()
            self.c = self.c_real
            self.ws.c = self.c
            self.ws.start_real(self.nc)
            self.emit()
            self.c.finish()
        return self.nc

    def dry(self):
        return isinstance(self.c, DryCtx)

    def alloc(self):
        c, cfg = self.c, self.cfg
        L, TM = cfg.depth, cfg.TMAX
        self.tiles = {}
        self.cst = c.sb("cst", [128, NCONST])
        self.cstb = c.sb("cstb", [128, 144 + 6 * 128], BF16)
        self.ws = WStream(c, 8, 4)
        self.ps_dense = [c.ps(f"psd{i}", [128, 512]) for i in range(2)]
        self.ps_o = [c.ps(f"pso{i}", [128, 512]) for i in range(2)]
        self.ps_small = [c.ps(f"pss{i}", [128, 512]) for i in range(2)]
        self.ps_rows = [c.ps(f"psr{i}", [128, 512]) for i in range(2)]
        self.small_res = [[Res(f"pss{i}q{q}") for q in range(4)] for i in range(2)]
        self.n_dense = 0
        self.n_small = 0
        self.n_o = 0
        self.small_pool = self.ps_small
        self.dense_pool = self.ps_dense
        self.h_tok = c.sb("h_tok", [128, cfg.tg, D])
        self.hT = c.sb("hT", [128, 8, TM], BF16)
        self.yT = [c.sb(f"yT{n}", [128, 4, TM], BF16) for n in range(3)]
        self.big = c.sb("big", [128, 12 * TM])
        self.lbt = c.sb("lbt", [128, L, 4])
        self.omlt = c.sb("omlt", [128, L, 4])
        self.nomlt = c.sb("nomlt", [128, L, 4])
        self.gA = c.sb("gA", [128, L, 4])
        self.gB = c.sb("gB", [128, L, 4])
        self.gC = c.sb("gC", [128, L, 4])
        self.cw = c.sb("cw", [128, L, 12, 4])
        self.prow = c.sb("prow", [4, 8 * L])
        nst = 2 * L * 4
        SL = 16 * 129 + 16 * 64
        self.arena = c.sb("arena", [128, max(nst * 128, 2 * SL)])

        class _View:
            def __init__(s_, ap, name):
                s_.ap = ap
                s_.res = Res(name)

            def __getitem__(s_, idx):
                return s_.ap[idx]
        self._View = _View
        self.SA = [[(_View(self.arena[:, (l * 4 + h) * 128:(l * 4 + h + 1) * 128], f"SAf{l}_{h}"), c.sb(f"SAb{l}_{h}", [128, 128], BF16))
                    for h in range(4)] for l in range(L)]
        self.SC = [[(_View(self.arena[:, (L * 4 + l * 4 + h) * 128:(L * 4 + l * 4 + h + 1) * 128], f"SCf{l}_{h}"), c.sb(f"SCb{l}_{h}", [128, 128], BF16))
                    for h in range(4)] for l in range(L)]
        self.SB = [[(c.sb(f"SBf{l}_{h}", [64, 129]), c.sb(f"SBb{l}_{h}", [64, 128], BF16), c.sb(f"SBn{l}_{h}", [64, 4], BF16))
                    for h in range(4)] for l in range(L)]
        self.carryB = [c.sb(f"carryB{l}", [4, 4]) for l in range(L)]
        self.hist = [c.sb(f"hist{l}", [128, 12, 3]) for l in range(L)]
        if cfg.sample:
            self.SsF = [_View(self.arena[:, i * SL:i * SL + 16 * 129].rearrange("p (s n) -> p s n", n=129), f"SsF{i}") for i in range(2)]
            self.SsB = [_View(self.arena[:, i * SL + 16 * 129:(i + 1) * SL].bitcast(BF16).rearrange("p (s n) -> p s n", n=128), f"SsB{i}") for i in range(2)]
            self.SsN = [c.sb(f"SsN{i}", [64, 16, 4], BF16) for i in range(2)]
            self.m0row = c.sb("m0row", [4, 16])
        self.lnp = [c.sb(f"lnp{i}", [128, D // 2]) for i in range(2)]

    def T32(self, role, n=None, p=128):
        key = ("f", role)
        if key not in self.tiles:
            self.tiles[key] = self.c_real.sb("t_" + role, [p, n or self.cfg.TMAX], F32)
        return self.tiles[key]

    def T16(self, role, n=None, p=128):
        key = ("b", role)
        if key not in self.tiles:
            self.tiles[key] = self.c_real.sb("b_" + role, [p, n or self.cfg.TMAX], BF16)
        return self.tiles[key]

    def W(self, i, par):
        return self.T32(f"w{i}_{par}", self.cfg.TMAX + (64 if i == 0 else 0))

    def V(self, i, par):
        return self.T16(f"v{i}_{par}")

    @staticmethod
    def interleave(gens):
        gens = list(gens)
        while gens:
            for gq in list(gens):
                try:
                    next(gq)
                except StopIteration:
                    gens.remove(gq)

    def R32(self, role, n=None):
        return self.T32("r_" + role, n, p=4)

    def dense_bank(self):
        pool = self.dense_pool
        b = pool[self.n_dense % len(pool)]
        self.n_dense += 1
        return b

    def o_bank(self):
        b = self.ps_o[self.n_o % 2]
        self.n_o += 1
        return b

    def small(self):
        pool = self.small_pool
        bank = pool[self.n_small % len(pool)]
        self.n_small += 1
        return bank[:, 0:128], bank

    def cs(self, name, rows=128, bf=False, c0=0, n=None):
        o, w = CL[name]
        if bf:
            o = {"onecol": 0, "ones": 16, "ident": 144, "nident": 272, "posP_incl": 400, "posS_incl": 528,
                 "posP_strict_ts": 656, "posS_strict_ts": 784}[name]
        t = self.cstb if bf else self.cst
        n = w - c0 if n is None else n
        return t[0:rows, o + c0:o + c0 + n]

    def MM(self, out, lhsT, rhs, start=True, stop=True, reads=(), writes=()):
        self.c.op("pe", lambda e: e.matmul(out, lhsT, rhs, start=start, stop=stop), reads, writes)

    def TR(self, out, in_, ident, reads=(), writes=()):
        self.c.op("pe", lambda e: e.transpose(out, in_, ident), reads, writes)

    def ACT(self, out, in_, func, reads=(), writes=(), bias=None, scale=None, accum=None):
        kw = {}
        if bias is not None:
            kw["bias"] = bias
        if scale is not None:
            kw["scale"] = scale
        if accum is not None:
            kw["accum_out"] = accum
        self.c.op("act", lambda e: e.activation(out, in_, func, **kw), reads, writes)

    def TT(self, eng, out, in0, in1, op, reads=(), writes=()):
        self.c.op(eng, lambda e: e.tensor_tensor(out, in0, in1, op), reads, writes)

    def TS(self, eng, out, in0, s1, s2, op0, op1=None, reads=(), writes=()):
        if op1 is None:
            self.c.op(eng, lambda e: e.tensor_scalar(out, in0, s1, None, op0), reads, writes)
        else:
            self.c.op(eng, lambda e: e.tensor_scalar(out, in0, s1, s2, op0, op1), reads, writes)

    def STT(self, out, in0, scalar, in1, op0, op1, reads=(), writes=()):
        self.c.op("dve", lambda e: e.scalar_tensor_tensor(out, in0, scalar, in1, op0, op1), reads, writes)

    def CP(self, eng, out, in_, reads=(), writes=()):
        if eng == "act":
            self.c.op("act", lambda e: e.activation(out, in_, AF.Copy), reads, writes)
        else:
            self.c.op(eng, lambda e: e.tensor_copy(out, in_), reads, writes)

    def SCAN(self, out, d0, d1, init, op0, op1, reads=(), writes=()):
        self.c.op("dve", lambda e: e.tensor_tensor_scan(out, d0, d1, init, op0, op1), reads, writes)

    def RECIP(self, out, in_, reads=(), writes=()):
        self.c.op("dve", lambda e: e.reciprocal(out, in_), reads, writes)

    def MEMSET(self, eng, ap, val, writes=()):
        self.c.op(eng, lambda e: e.memset(ap, val), (), writes)

    def dump(self, name, tile, ap, p, n, q="sp"):
        if not self.cfg.dbg or self.dry():
            return
        off = self.dbg_off
        self.dbg_aps[name] = (off, p, n)
        self.dbg_off += n
        self.c.dma(q, self.dbgt[0:p, off:off + n], ap, reads=[tile])


class DryCtx:
    def op(self, *a, **k):
        pass

    def dma(self, *a, **k):
        return None


class Grp:
    def __init__(self, kind, T, tok0, first, last, idx):
        self.kind, self.T, self.tok0, self.first, self.last, self.idx = kind, T, tok0, first, last, idx
        self.nt = T // 128


def _emit(self):
    cfg = self.cfg
    self.n_dense = self.n_small = self.n_o = 0
    self.setup()
    groups = []
    for i in range(cfg.n_pg):
        groups.append(Grp("p", cfg.TMAX, i * cfg.TMAX, i == 0, i == cfg.n_pg - 1, i))
    if cfg.sample:
        groups.append(Grp("s", 128, 0, True, True, cfg.n_pg))
    for g in groups:
        if g.kind == "s":
            allst = [t for row in self.SA + self.SC for (t, _) in row]
            self.MEMSET("dve", self.arena[:, 0:1], 0.0, writes=allst + self.SsF + self.SsB)
        self.load_x(g)
        for l in range(cfg.depth):
            hasA, hasB, hasC = ("A" in cfg.parts), ("B" in cfg.parts), ("C" in cfg.parts)
            if hasA:
                self.branch_A(g, l)
            gB = self.branch_B(g, l) if hasB else None
            if gB is not None:
                next(gB)
            if hasA:
                self.rms_finish(g, l, 0)
            if gB is not None:
                for _ in gB:
                    pass
            gC = self.branch_C(g, l) if hasC else None
            if gC is not None:
                next(gC)
            if hasB:
                self.rms_finish(g, l, 1, scale_row=self._b_aden)
            if gC is not None:
                for _ in gC:
                    pass
                self.rms_finish(g, l, 2)
            if "M" in cfg.parts:
                self.merge(g, l)
            if "F" in cfg.parts:
                self.ffn(g, l)
        self.store_y(g)


def _setup(self):
    c, cfg, d = self.c, self.cfg, self.d
    L = cfg.depth
    cst, cstb = self.cst, self.cstb
    c.dma("sp", cst[:], d["consts"], writes=[cst])
    o1, _ = CL["onecol"]
    o2, _ = CL["ones"]
    self.CP("dve", cstb[:, 0:16], cst[:, o1:o1 + 16], reads=[cst], writes=[cstb])
    self.CP("dve", cstb[:, 16:144], cst[:, o2:o2 + 128], reads=[cst], writes=[cstb])
    for i, nm in enumerate(("ident", "nident", "posP_incl", "posS_incl", "posP_strict_ts", "posS_strict_ts")):
        o3, _ = CL[nm]
        self.CP("dve", cstb[:, 144 + i * 128:144 + (i + 1) * 128], cst[:, o3:o3 + 128], reads=[cst], writes=[cstb])
    e = self.T32("su_e", 4 * L)
    ev = e[:, 0:4 * L].rearrange("p (l h) -> p l h", l=L)
    c.dma("sp", ev, d["lb_logits"].rearrange("l (h k) -> k l h", k=128), writes=[e], allow_slow_non_contiguous=True)
    self.ACT(e[:, 0:4 * L], e[:, 0:4 * L], AF.Exp, reads=[e], writes=[e])
    s = self.T32("su_s", 4)
    self.CP("dve", s[:, 0:4], ev[:, 0, :], reads=[e], writes=[s])
    for l in range(1, L):
        self.TT("dve", s[:, 0:4], s[:, 0:4], ev[:, l, :], ALU.add, reads=[s, e], writes=[s])
    self.RECIP(s[:, 0:4], s[:, 0:4], reads=[s], writes=[s])
    lbt, omlt, nomlt = self.lbt, self.omlt, self.nomlt
    self.MEMSET("dve", lbt[:, 0, :], 0.0, writes=[lbt])
    for l in range(1, L):
        w = self.T32("su_w", 4)
        self.TT("dve", w[:, 0:4], ev[:, l, :], s[:, 0:4], ALU.mult, reads=[e, s], writes=[w])
        self.TT("dve", lbt[:, l, :], lbt[:, l - 1, :], w[:, 0:4], ALU.add, reads=[lbt, w], writes=[lbt])
    self.TS("dve", omlt[:].rearrange("p l h -> p (l h)"), lbt[:].rearrange("p l h -> p (l h)"), -1.0, 1.0, ALU.mult, ALU.add, reads=[lbt], writes=[omlt])
    self.TS("dve", nomlt[:].rearrange("p l h -> p (l h)"), omlt[:].rearrange("p l h -> p (l h)"), -1.0, None, ALU.mult, reads=[omlt], writes=[nomlt])
    for t, nm in ((self.gA, "a_norm_g"), (self.gB, "b_norm_g"), (self.gC, "c_norm_g")):
        c.dma("sp", t[:], d[nm].rearrange("l (h k) -> k l h", k=128), writes=[t], allow_slow_non_contiguous=True)
    for l in range(L):
        for j in range(4):
            c.dma("sp", self.cw[:, l, :, j], d["conv_w"][l, j].rearrange("(c p) -> p c", p=128), writes=[self.cw], allow_slow_non_contiguous=True)
    pr = self.prow
    for i, nm in enumerate(("b_mi", "b_mf", "dt_bias", "a_log")):
        c.dma("sp", pr[0:4, i * L:(i + 1) * L], d[nm].rearrange("l h -> h l"), writes=[pr], allow_slow_non_contiguous=True)
    self.TS("dve", pr[0:4, L:2 * L], pr[0:4, L:2 * L], -1.0, None, ALU.mult, reads=[pr], writes=[pr])
    self.ACT(pr[0:4, 3 * L:4 * L], pr[0:4, 3 * L:4 * L], AF.Exp, reads=[pr], writes=[pr])
    self.TS("dve", pr[0:4, 3 * L:4 * L], pr[0:4, 3 * L:4 * L], -1.0, None, ALU.mult, reads=[pr], writes=[pr])
    if not self.dry():
        self.ws.emit_conversions()
    for l in range(L):
        for h in range(4):
            for grp in (self.SA, self.SC):
                f, b = grp[l][h]
                self.MEMSET("dve", f[:], 0.0, writes=[f])
                self.MEMSET("dve", b[:], 0.0, writes=[b])
            f, b, n = self.SB[l][h]
            self.MEMSET("dve", f[:], 0.0, writes=[f])
            self.MEMSET("dve", b[:], 0.0, writes=[b])
            self.MEMSET("dve", n[:], 0.0, writes=[n])
        self.MEMSET("dve", self.carryB[l][:], 0.0, writes=[self.carryB[l]])
        self.MEMSET("dve", self.hist[l][:], 0.0, writes=[self.hist[l]])


def _load_x(self, g):
    c, d = self.c, self.d
    src = d["xp"] if g.kind == "p" else d["xs"]
    for j in range(g.nt):
        c.dma("sp", self.h_tok[:, j, :], src[g.tok0 + j * 128:g.tok0 + (j + 1) * 128, :], writes=[self.h_tok])
    self.make_hT(g)


def _make_hT(self, g):
    ident = self.cs("ident")
    for j in range(g.nt):
        for half in range(2):
            bank = self.dense_bank()
            for q in range(4):
                k = half * 4 + q
                self.TR(bank[:, q * 128:(q + 1) * 128], self.h_tok[:, j, k * 128:(k + 1) * 128], ident,
                        reads=[self.h_tok, self.cst], writes=[bank])
            self.CP("act", self.hT[:, half * 4:(half + 1) * 4, j * 128:(j + 1) * 128],
                    bank[:].rearrange("p (a b) -> p a b", a=4), reads=[bank], writes=[self.hT])


def _store_y(self, g):
    c = self.c
    dst = self.o["yp"] if g.kind == "p" else self.o["ys"]
    for j in range(g.nt):
        c.dma("sp", dst[g.tok0 + j * 128:g.tok0 + (j + 1) * 128, :], self.h_tok[:, j, :], reads=[self.h_tok])


def _proj_fm(self, wv, ws, c0, ncol, T, bank=None):
    bank = bank or self.dense_bank()
    for k in range(8):
        self.MM(bank[0:ncol, 0:T], wv[:, k, c0:c0 + ncol], self.hT[:, k, 0:T], start=(k == 0), stop=(k == 7),
                reads=[ws, self.hT], writes=[bank])
    return bank


def _win(self, l, c0, n):
    src = self.d["w_in"][l].rearrange("(k p) n -> p k n", p=128)[:, :, c0:c0 + n]
    return self.ws.get(("w_in", l, c0, n), src, 8, n)


def _proj_tm(self, g, dst, dres, l, c0, ncols):
    for b0 in range(0, ncols, 256):
        wv, ws = self.win(l, c0 + b0, 256)
        for j in range(g.nt):
            bank = self.dense_bank()
            for k in range(8):
                self.MM(bank[:, 0:256], self.hT[:, k, j * 128:(j + 1) * 128], wv[:, k, :], start=(k == 0), stop=(k == 7),
                        reads=[ws, self.hT], writes=[bank])
            self.CP("act", dst[:, j, b0:b0 + 256], bank[:, 0:256], reads=[bank], writes=[dres])


Builder.emit = _emit
Builder.setup = _setup
Builder.load_x = _load_x
Builder.make_hT = _make_hT
Builder.store_y = _store_y
Builder.proj_fm = _proj_fm
Builder.win = _win
Builder.proj_tm = _proj_tm


def _rms_gate(self, g, l, br, h, obank, gs, gtile, first, last):
    T = g.T
    osq = self.T16(f"osq{h % 2}")
    self.ACT(osq[:, 0:T], obank[:, 0:T], AF.Square, reads=[obank], writes=[osq])
    rows = self.ps_rows[1]
    self.MM(rows[0:4, 0:T], self.cs("onecol", bf=True, c0=4 * h, n=4), osq[:, 0:T], start=first, stop=last,
            reads=[osq, self.cstb], writes=[rows])
    t1 = self.T32(f"t1_{h}")
    self.STT(t1[:, 0:T], obank[:, 0:T], gtile[:, l, h:h + 1], gs[:, 0:T], ALU.mult, ALU.mult,
             reads=[obank, gtile, gs], writes=[t1])


def _rms_finish(self, g, l, n, scale_row=None):
    T = g.T
    rows = self.ps_rows[1]
    r = self.R32("rms_r")
    if scale_row is None:
        self.ACT(r[0:4, 0:T], rows[0:4, 0:T], AF.Ln, reads=[rows], writes=[r], scale=1.0 / 128, bias=NORM_EPS)
        self.ACT(r[0:4, 0:T], r[0:4, 0:T], AF.Exp, reads=[r], writes=[r], scale=-0.5)
    else:
        t = self.R32("rms_t")
        self.TT("dve", t[0:4, 0:T], rows[0:4, 0:T], scale_row[0:4, 0:T], ALU.mult, reads=[rows, scale_row], writes=[t])
        self.TT("dve", t[0:4, 0:T], t[0:4, 0:T], scale_row[0:4, 0:T], ALU.mult, reads=[t, scale_row], writes=[t])
        self.ACT(r[0:4, 0:T], t[0:4, 0:T], AF.Ln, reads=[t], writes=[r], scale=1.0 / 128, bias=NORM_EPS)
        self.ACT(r[0:4, 0:T], r[0:4, 0:T], AF.Exp, reads=[r], writes=[r], scale=-0.5)
        self.TT("dve", r[0:4, 0:T], r[0:4, 0:T], scale_row[0:4, 0:T], ALU.mult, reads=[r, scale_row], writes=[r])
    for h in range(4):
        bank = self.dense_bank()
        self.MM(bank[:, 0:T], self.cs("sel", rows=4, c0=h * 128, n=128), r[0:4, 0:T], reads=[self.cst, r], writes=[bank])
        t1 = self.T32(f"t1_{h}")
        self.TT("dve", self.yT[n][:, h, 0:T], t1[:, 0:T], bank[:, 0:T], ALU.mult, reads=[t1, bank], writes=[self.yT[n]])
        self.dump(f"y{n}_{h}_l{l}_g{g.idx}", self.yT[n], self.yT[n][:, h, 0:T], 128, T, q="pool")


def _branch_A(self, g, l):
    c, cfg, d = self.c, self.cfg, self.d
    T, nt = g.T, g.nt
    samp = g.kind == "s"
    seglen = 8 if samp else 64
    spt = 128 // seglen
    nseg = T // seglen
    maskA = self.cs("mask8" if samp else "maskA64")
    rst = self.cs("rst8" if samp else "rst64", n=T)
    ident = self.cs("ident")
    rm = self.cs("rm16" if samp else "rm2")
    vA = self.T16("vtok", 512 * cfg.tg)
    vA3 = vA[:, :].rearrange("p (j n) -> p j n", n=512)
    self.proj_tm(g, vA3, vA, l, 1024, 512)
    for p in range(2):
        wq, wqs = self.win(l, 0 + p * 256, 256)
        wg, wgs = self.win(l, 1536 + p * 256, 256)
        wf, wfs = self.win(l, 512 + p * 256, 256)
        def head(hh):
            h = 2 * p + hh
            c0 = hh * 128
            hs = [0]

            def hsmall():
                pool = (self.ps_small[hh], self.ps_dense[hh])
                b_ = pool[hs[0] % 2]
                hs[0] += 1
                return b_[:, 0:128], b_
            if samp:
                c.dma("sp", self.SsF[hh][:, :, 0:128], d["st_a"][l, :, h].rearrange("s k v -> k s v"), writes=[self.SsF[hh]])
                self.CP("act", self.SsB[hh][:, :, :], self.SsF[hh][:, :, 0:128], reads=[self.SsF[hh]], writes=[self.SsB[hh]])
                yield
            psq = self.proj_fm(wq, wqs, c0, 128, T, bank=self.ps_dense[hh])
            qs = self.W(0, hh)
            self.ACT(qs[:, 0:T], psq[:, 0:T], AF.Silu, reads=[psq], writes=[qs])
            yield
            psg = self.proj_fm(wg, wgs, c0, 128, T, bank=self.ps_dense[hh])
            gs = self.W(7, hh)
            self.ACT(gs[:, 0:T], psg[:, 0:T], AF.Silu, reads=[psg], writes=[gs])
            yield
            psf = self.proj_fm(wf, wfs, c0, 128, T, bank=self.ps_dense[hh])
            sig = self.W(1, hh)
            self.ACT(sig[:, 0:T], psf[:, 0:T], AF.Sigmoid, reads=[psf], writes=[sig])
            yield
            f = self.W(2, hh)
            self.TS("dve", f[:, 0:T], sig[:, 0:T], self.omlt[:, l, h:h + 1], self.lbt[:, l, h:h + 1], ALU.mult, ALU.add,
                    reads=[sig, self.omlt, self.lbt], writes=[f])
            kk = self.W(3, hh)
            self.TS("dve", kk[:, 0:T], sig[:, 0:T], self.nomlt[:, l, h:h + 1], self.omlt[:, l, h:h + 1], ALU.mult, ALU.add,
                    reads=[sig, self.omlt, self.nomlt], writes=[kk])
            self.ACT(f[:, 0:T], f[:, 0:T], AF.Ln, reads=[f], writes=[f])
            yield
            b = self.W(4, hh)
            self.SCAN(b[:, 0:T], rst, f[:, 0:T], 0.0, ALU.mult, ALU.add, reads=[self.cst, f], writes=[b])
            yield
            E1 = sig
            self.ACT(E1[:, 0:T], b[:, 0:T], AF.Exp, reads=[b], writes=[E1])
            yield
            En = f
            self.ACT(En[:, 0:T], b[:, 0:T], AF.Exp, reads=[b], writes=[En], scale=-1.0)
            yield
            q2 = self.V(0, hh)
            self.TT("dve", q2[:, 0:T], qs[:, 0:T], E1[:, 0:T], ALU.mult, reads=[qs, E1], writes=[q2])
            yield
            kt = self.V(1, hh)
            self.TT("dve", kt[:, 0:T], kk[:, 0:T], En[:, 0:T], ALU.mult, reads=[kk, En], writes=[kt])
            yield
            b3 = b[:, 0:T].rearrange("p (s c) -> p s c", c=seglen)
            bl = b[:, 0:T].rearrange("p (s c) -> p s c", c=seglen)[:, :, seglen - 1:seglen]
            dd = self.W(5, hh)
            dd3 = dd[:, 0:T].rearrange("p (s c) -> p s c", c=seglen)
            self.TT("dve", dd3, b3, bl.to_broadcast([128, nseg, seglen]), ALU.subtract, reads=[b], writes=[dd])
            yield
            self.ACT(dd[:, 0:T], dd[:, 0:T], AF.Exp, reads=[dd], writes=[dd], scale=-1.0)
            yield
            khT = self.W(6, hh)
            self.TT("dve", khT[:, 0:T], kk[:, 0:T], dd[:, 0:T], ALU.mult, reads=[kk, dd], writes=[khT])
            yield
            ob = self.ps_o[hh]
            for j in range(nt):
                jc = slice(j * 128, (j + 1) * 128)
                pt, ptr = hsmall()
                self.TR(pt, khT[:, jc], ident, reads=[khT, self.cst], writes=[ptr])
                khs = self.T16(f"khsb_{hh}", 128)
                self.CP("dve", khs[:, :], pt, reads=[ptr], writes=[khs])
                yield
                pa, par = hsmall()
                self.MM(pa, kt[:, jc], q2[:, jc], reads=[kt, q2], writes=[par])
                attm = self.T16(f"c_attm_{hh}", 128)
                self.TT("dve", attm[:, :], pa, maskA, ALU.mult, reads=[par, self.cst], writes=[attm])
                yield
                vh = vA3[:, j, h * 128:(h + 1) * 128]
                self.MM(ob[:, jc], vh, attm[:, :], start=True, stop=False, reads=[vA, attm], writes=[ob])
                for i in range(spt):
                    seg = j * spt + i
                    sc = slice(seg * seglen, (seg + 1) * seglen)
                    if samp:
                        Sf, Sb, Sfr, Sbr = self.SsF[hh][:, seg, 0:128], self.SsB[hh][:, seg, :], self.SsF[hh], self.SsB[hh]
                    else:
                        Sft, Sbt = self.SA[l][h]
                        Sf, Sb, Sfr, Sbr = Sft[:, :], Sbt[:, :], Sft, Sbt
                    self.MM(ob[:, sc], Sb, q2[:, sc], start=False, stop=(i == spt - 1), reads=[Sbr, q2], writes=[ob])
                    khm = self.T16(f"khseg{i % 2}_{hh}", 128)
                    self.TS("dve", khm[:, :], khs[:, :], rm[:, i:i + 1], None, ALU.mult, reads=[khs, self.cst], writes=[khm])
                    pS, pSr = hsmall()
                    self.MM(pS, khm[:, :], vh, reads=[khm, vA], writes=[pSr])
                    self.STT(Sf, Sf, E1[:, (seg + 1) * seglen - 1:(seg + 1) * seglen], pS, ALU.mult, ALU.add,
                             reads=[Sfr, E1, pSr], writes=[Sfr])
                    if not samp:
                        self.CP("act", Sb, Sf, reads=[Sfr], writes=[Sbr])
                        yield
            self.rms_gate(g, l, 0, h, ob, gs, self.gA, first=(h == 0), last=(h == 3))
            if samp:
                c.dma("sp", self.o["oa_s"][l, :, h].rearrange("s k v -> k s v"), self.SsF[hh][:, :, 0:128], reads=[self.SsF[hh]])
            elif g.last:
                c.dma("sp", self.o["oa_p"][l, h], self.SA[l][h][0][:, :], reads=[self.SA[l][h][0]])
        self.interleave([head(0), head(1)])


Builder.rms_gate = _rms_gate
Builder.rms_finish = _rms_finish
Builder.branch_A = _branch_A


def _branch_B(self, g, l):
    c, cfg, d = self.c, self.cfg, self.d
    L = cfg.depth
    T, nt = g.T, g.nt
    samp = g.kind == "s"
    seglen = 8 if samp else 128
    spt = 128 // seglen
    nseg = T // seglen
    ident = self.cs("ident")
    pos = self.cs("posS_incl" if samp else "posP_incl")
    rm = self.cs("rm16")
    i4 = self.cs("i4", rows=4)
    pr = self.prow
    w4, w4s = self.win(l, 3584, 8)
    bank = self.dense_bank()
    for k in range(8):
        self.MM(bank[0:4, 0:T], w4[:, k, 0:4], self.hT[:, k, 0:T], start=(k == 0), stop=(k == 7), reads=[w4s, self.hT], writes=[bank])
    bi = self.R32("b_bi")
    self.ACT(bi[0:4, 0:T], bank[0:4, 0:T], AF.Identity, reads=[bank, pr], writes=[bi], bias=pr[0:4, l:l + 1])
    bank = self.dense_bank()
    for k in range(8):
        self.MM(bank[0:4, 0:T], w4[:, k, 4:8], self.hT[:, k, 0:T], start=(k == 0), stop=(k == 7), reads=[w4s, self.hT], writes=[bank])
    sp = self.R32("b_sp")
    self.ACT(sp[0:4, 0:T], bank[0:4, 0:T], AF.Exp, reads=[bank, pr], writes=[sp], scale=-1.0, bias=pr[0:4, L + l:L + l + 1])
    self.ACT(sp[0:4, 0:T], sp[0:4, 0:T], AF.Ln, reads=[sp], writes=[sp], bias=1.0)
    Bn = self.R32("b_Bn")
    cb = self.carryB[l]
    if samp:
        self.SCAN(Bn[0:4, 0:T], self.cs("rst8", rows=4, n=T), sp[0:4, 0:T], 0.0, ALU.mult, ALU.add, reads=[self.cst, sp], writes=[Bn])
    else:
        self.SCAN(Bn[0:4, 0:T], self.cs("ones", rows=4, n=T), sp[0:4, 0:T], cb[0:4, 0:1], ALU.mult, ALU.add,
                  reads=[self.cst, sp, cb], writes=[Bn])
    u = self.R32("b_u")
    self.TT("dve", u[0:4, 0:T], bi[0:4, 0:T], Bn[0:4, 0:T], ALU.add, reads=[bi, Bn], writes=[u])
    mu = self.R32("b_mu")
    si = self.R32("b_si", 16)
    if samp:
        c.dma("sp", self.m0row[0:4, 0:16], d["st_m"][l].rearrange("s h -> h s"), writes=[self.m0row], allow_slow_non_contiguous=True)
        for s in range(16):
            sl = slice(s * 8, (s + 1) * 8)
            self.SCAN(mu[0:4, sl], u[0:4, sl], u[0:4, sl], self.m0row[0:4, s:s + 1], ALU.max, ALU.max, reads=[u, self.m0row], writes=[mu])
        self.CP("dve", si[0:4, 0:16], self.m0row[0:4, 0:16], reads=[self.m0row], writes=[si])
    else:
        self.SCAN(mu[0:4, 0:T], u[0:4, 0:T], u[0:4, 0:T], cb[0:4, 1:2], ALU.max, ALU.max, reads=[u, cb], writes=[mu])
        self.CP("dve", si[0:4, 0:1], cb[0:4, 1:2], reads=[cb], writes=[si])
        for j in range(1, nt):
            self.CP("dve", si[0:4, j:j + 1], mu[0:4, j * 128 - 1:j * 128], reads=[mu], writes=[si])
    mu3 = mu[0:4, 0:T].rearrange("p (s c) -> p s c", c=seglen)
    u3 = u[0:4, 0:T].rearrange("p (s c) -> p s c", c=seglen)
    wint = self.R32("b_wint")
    wint3 = wint[0:4, 0:T].rearrange("p (s c) -> p s c", c=seglen)
    self.TT("dve", wint3, mu3, si[0:4, 0:nseg].unsqueeze(2).to_broadcast([4, nseg, seglen]), ALU.subtract, reads=[mu, si], writes=[wint])
    self.ACT(wint[0:4, 0:T], wint[0:4, 0:T], AF.Exp, reads=[wint], writes=[wint], scale=-1.0)
    mt = self.R32("b_mt")
    self.TT("dve", mt[0:4, 0:T], mu[0:4, 0:T], Bn[0:4, 0:T], ALU.subtract, reads=[mu, Bn], writes=[mt])
    emt = self.R32("b_emt")
    self.ACT(emt[0:4, 0:T], mt[0:4, 0:T], AF.Exp, reads=[mt], writes=[emt], scale=-1.0)
    muend = mu3[:, :, seglen - 1:seglen]
    wk = self.R32("b_wk")
    wk3 = wk[0:4, 0:T].rearrange("p (s c) -> p s c", c=seglen)
    self.TT("dve", wk3, u3, muend.to_broadcast([4, nseg, seglen]), ALU.subtract, reads=[u, mu], writes=[wk])
    self.ACT(wk[0:4, 0:T], wk[0:4, 0:T], AF.Exp, reads=[wk], writes=[wk])
    wold = self.R32("b_wold", 16)
    self.TT("dve", wold[0:4, 0:nseg].unsqueeze(2), si[0:4, 0:nseg].unsqueeze(2), muend, ALU.subtract, reads=[si, mu], writes=[wold])
    self.ACT(wold[0:4, 0:nseg], wold[0:4, 0:nseg], AF.Exp, reads=[wold], writes=[wold])
    if samp:
        c.dma("sp", self.o["om_s"][l].rearrange("s h -> h s"), mt[0:4, 0:T].rearrange("p (s c) -> p s c", c=8)[:, :, 7],
              reads=[mt], allow_slow_non_contiguous=True)
    else:
        if g.last:
            c.dma("sp", self.o["om_p"][l].rearrange("(h o) -> h o", o=1), mt[0:4, T - 1:T], reads=[mt])
        self.CP("dve", cb[0:4, 0:1], Bn[0:4, T - 1:T], reads=[Bn], writes=[cb])
        self.CP("dve", cb[0:4, 1:2], mu[0:4, T - 1:T], reads=[mu], writes=[cb])
    vB = self.T16("vtok", 512 * cfg.tg)
    vB3 = vB[:, :].rearrange("p (j n) -> p j n", n=512)
    self.proj_tm(g, vB3, vB, l, 2560, 512)
    kB = self.T16("b_ktok", 256 * cfg.tg)
    kB3 = kB[:, :].rearrange("p (j n) -> p j n", n=256)
    self.proj_tm(g, kB3, kB, l, 2304, 256)
    cols = self.T32("b_cols", 8 * cfg.tg)
    for j in range(nt):
        jc = slice(j * 128, (j + 1) * 128)
        pc, pcr = self.small()
        self.MM(pc[:, 0:4], u[0:4, jc], i4, reads=[u, self.cst], writes=[pcr])
        self.MM(pc[:, 4:8], wk[0:4, jc], i4, reads=[wk, self.cst], writes=[pcr])
        self.CP("dve", cols[:, j * 8:(j + 1) * 8], pc[:, 0:8], reads=[pcr], writes=[cols])
    wq, wqs = self.win(l, 2048, 256)
    wk_, wks = self.win(l, 2304, 256)
    rowsA = self.ps_rows[0]
    yield "pre"
    for p in range(2):
        wo, wos = self.win(l, 3072 + p * 256, 256)

        def head(hh):
            h = 2 * p + hh
            hs = [0]

            def hsmall():
                pool = (self.ps_small[hh], self.ps_dense[hh])
                b_ = pool[hs[0] % 2]
                hs[0] += 1
                return b_[:, 0:128], b_
            if samp:
                for s4 in range(4):
                    cld = self.T32(f"b_cld_{hh}", 4 * 64)
                    cld3 = cld[:, :].rearrange("p (s k) -> p s k", k=64)
                    c.dma("sp", cld3, d["st_c"][l, s4 * 4:(s4 + 1) * 4, h].rearrange("s v k -> v s k"), writes=[cld])
                    bk_ = self.ps_dense[hh]
                    for q in range(4):
                        self.TR(bk_[0:64, q * 128:(q + 1) * 128], cld3[:, q, :], ident, reads=[cld, self.cst], writes=[bk_])
                    self.CP("act", self.SsF[hh][0:64, s4 * 4:(s4 + 1) * 4, 0:128], bk_[0:64, :].rearrange("p (a b) -> p a b", a=4), reads=[bk_], writes=[self.SsF[hh]])
                    yield
                nld = self.T32(f"b_nld_{hh}", 64, p=16)
                c.dma("sp", nld[0:16, 0:64], d["st_n"][l, :, h, :], writes=[nld])
                bk_ = self.ps_dense[hh]
                self.TR(bk_[0:64, 0:16], nld[0:16, 0:64], ident[0:16, 0:16], reads=[nld, self.cst], writes=[bk_])
                self.CP("dve", self.SsF[hh][0:64, :, 128], bk_[0:64, 0:16], reads=[bk_], writes=[self.SsF[hh]])
                yield
                self.CP("act", self.SsB[hh][0:64, :, :], self.SsF[hh][0:64, :, 0:128], reads=[self.SsF[hh]], writes=[self.SsB[hh]])
                yield
                self.MEMSET("dve", self.SsN[hh][:], 0.0, writes=[self.SsN[hh]])
                self.CP("dve", self.SsN[hh][0:64, :, h], self.SsF[hh][0:64, :, 128], reads=[self.SsF[hh]], writes=[self.SsN[hh]])
                yield
            psq = self.proj_fm(wq, wqs, h * 64, 64, T, bank=self.ps_dense[hh])
            qT = self.T16(f"b_qT_{hh}", p=64)
            self.ACT(qT[0:64, 0:T], psq[0:64, 0:T], AF.Copy, reads=[psq], writes=[qT], scale=0.125)
            yield
            psk = self.proj_fm(wk_, wks, h * 64, 64, T, bank=self.ps_dense[hh])
            kT = self.T16(f"b_kT_{hh}", p=64)
            self.CP("act", kT[0:64, 0:T], psk[0:64, 0:T], reads=[psk], writes=[kT])
            yield
            bcb = self.ps_dense[hh]
            self.MM(bcb[:, 0:T], self.cs("sel", rows=4, c0=h * 128, n=128), wint[0:4, 0:T], reads=[self.cst, wint], writes=[bcb])
            qp = self.T16(f"b_qp_{hh}", p=64)
            self.TT("dve", qp[0:64, 0:T], qT[0:64, 0:T], bcb[0:64, 0:T], ALU.mult, reads=[qT, bcb], writes=[qp])
            yield
            bcw = self.ps_dense[hh]
            self.MM(bcw[:, 0:nseg], self.cs("sel", rows=4, c0=h * 128, n=128), wold[0:4, 0:nseg], reads=[self.cst, wold], writes=[bcw])
            woldbc = self.T32(f"b_woldbc_{hh}", 16)
            self.CP("dve", woldbc[:, 0:nseg], bcw[:, 0:nseg], reads=[bcw], writes=[woldbc])
            yield
            pso = self.proj_fm(wo, wos, (h % 2) * 128, 128, T, bank=self.ps_dense[hh])
            gs = self.W(7, hh)
            self.ACT(gs[:, 0:T], pso[:, 0:T], AF.Sigmoid, reads=[pso], writes=[gs])
            yield
            ob = self.ps_o[hh]
            for j in range(nt):
                jc = slice(j * 128, (j + 1) * 128)
                X, Xr = (self.ps_small[hh][:, 0:128], self.ps_small[hh])
                self.MM(X, self.cs("sel", rows=4, c0=h * 128, n=128), mu[0:4, jc], start=True, stop=False, reads=[self.cst, mu], writes=[Xr])
                self.MM(X, self.cs("ident", bf=True), self.cs("posS_incl" if samp else "posP_incl", bf=True), start=False, stop=True, reads=[self.cstb], writes=[Xr])
                E = self.T32(f"b_E_{hh}", 128)
                self.ACT(E[:, :], X, AF.Exp, reads=[Xr, cols], writes=[E], scale=-1.0, bias=cols[:, j * 8 + h:j * 8 + h + 1])
                yield
                KQ, KQr = (self.ps_small[hh][:, 0:128], self.ps_small[hh])
                self.MM(KQ, kT[0:64, jc], qT[0:64, jc], reads=[kT, qT], writes=[KQr])
                scm = self.T16(f"b_scm_{hh}", 128)
                self.TT("dve", scm[:, :], KQ, E[:, :], ALU.mult, reads=[KQr, E], writes=[scm])
                yield
                vh = vB3[:, j, h * 128:(h + 1) * 128]
                self.MM(ob[:, jc], vh, scm[:, :], start=True, stop=False, reads=[vB, scm], writes=[ob])
                self.MM(rowsA[0:4, jc], self.cs("onecol", bf=True, c0=4 * h, n=4), scm[:, :], start=(h == 0 and j == 0), stop=False,
                        reads=[self.cstb, scm], writes=[rowsA])
                wkv = self.T16(f"b_wkv_{hh}", 132)
                self.TS("dve", wkv[:, 0:128], vh, cols[:, j * 8 + 4 + h:j * 8 + 5 + h], None, ALU.mult, reads=[vB, cols], writes=[wkv])
                yield
                self.CP("dve", wkv[:, 128:129], cols[:, j * 8 + 4 + h:j * 8 + 5 + h], reads=[cols], writes=[wkv])
                yield
                for i in range(spt):
                    seg = j * spt + i
                    sc = slice(seg * seglen, (seg + 1) * seglen)
                    if samp:
                        Cf, Cb, Cn = self.SsF[hh][0:64, seg, :], self.SsB[hh][0:64, seg, :], self.SsN[hh][0:64, seg, :]
                        Cfr, Cbr, Cnr = self.SsF[hh], self.SsB[hh], self.SsN[hh]
                    else:
                        Cft, Cbt, Cnt = self.SB[l][h]
                        Cf, Cb, Cn, Cfr, Cbr, Cnr = Cft[:, :], Cbt[:, :], Cnt[:, :], Cft, Cbt, Cnt
                    last = (i == spt - 1)
                    self.MM(ob[:, sc], Cb, qp[0:64, sc], start=False, stop=last, reads=[Cbr, qp], writes=[ob])
                    self.MM(rowsA[0:4, sc], Cn, qp[0:64, sc], start=False, stop=(last and h == 3 and j == nt - 1), reads=[Cnr, qp], writes=[rowsA])
                    if spt > 1:
                        kkm = self.T16(f"b_kkm_{hh}", 64)
                        self.TS("dve", kkm[:, :], kB3[:, j, h * 64:(h + 1) * 64], rm[:, i:i + 1], None, ALU.mult, reads=[kB, self.cst], writes=[kkm])
                        yield
                        kk_ap, kk_r = kkm[:, :], kkm
                    else:
                        kk_ap, kk_r = kB3[:, j, h * 64:(h + 1) * 64], kB
                    pS = self.ps_dense[hh]
                    self.MM(pS[0:64, 0:129], kk_ap, wkv[:, 0:129], reads=[kk_r, wkv], writes=[pS])
                    self.STT(Cf, Cf, woldbc[0:64, seg:seg + 1], pS[0:64, 0:129], ALU.mult, ALU.add, reads=[Cfr, woldbc, pS], writes=[Cfr])
                    yield
                    if not samp:
                        self.CP("act", Cb, Cf[:, 0:128], reads=[Cfr], writes=[Cbr])
                        yield
                        self.CP("act", Cn[:, h:h + 1], Cf[:, 128:129], reads=[Cfr], writes=[Cnr])
                        yield
            self.rms_gate(g, l, 1, h, ob, gs, self.gB, first=(h == 0), last=(h == 3))
            if samp:
                for s4 in range(4):
                    cld = self.T32(f"b_cld_{hh}", 4 * 64)
                    cld3 = cld[:, :].rearrange("p (s k) -> p s k", k=64)
                    bk_ = self.ps_dense[hh]
                    for q in range(4):
                        self.TR(bk_[:, q * 64:(q + 1) * 64], self.SsF[hh][0:64, s4 * 4 + q, 0:128], ident[0:64, 0:64], reads=[self.SsF[hh], self.cst], writes=[bk_])
                    self.CP("act", cld3, bk_[:, 0:256].rearrange("p (a b) -> p a b", a=4), reads=[bk_], writes=[cld])
                    yield
                    c.dma("sp", self.o["oc_s"][l, s4 * 4:(s4 + 1) * 4, h].rearrange("s v k -> v s k"), cld3, reads=[cld])
                nst_ = self.T32(f"b_nst_{hh}", 16, p=64)
                self.CP("dve", nst_[0:64, 0:16], self.SsF[hh][0:64, :, 128], reads=[self.SsF[hh]], writes=[nst_])
                yield
                bk_ = self.ps_dense[hh]
                self.TR(bk_[0:16, 0:64], nst_[0:64, 0:16], ident[0:64, 0:64], reads=[nst_, self.cst], writes=[bk_])
                nld = self.T32(f"b_nld_{hh}", 64, p=16)
                self.CP("dve", nld[0:16, 0:64], bk_[0:16, 0:64], reads=[bk_], writes=[nld])
                yield
                c.dma("sp", self.o["on_s"][l, :, h, :], nld[0:16, 0:64], reads=[nld])
            elif g.last:
                Cft = self.SB[l][h][0]
                bk_ = self.ps_dense[hh]
                self.TR(bk_[:, 0:64], Cft[0:64, 0:128], ident[0:64, 0:64], reads=[Cft, self.cst], writes=[bk_])
                co = self.T32(f"b_co_{hh}", 64)
                self.CP("act", co[:, 0:64], bk_[:, 0:64], reads=[bk_], writes=[co])
                yield
                c.dma("sp", self.o["oc_p"][l, h], co[:, 0:64], reads=[co])
                c.dma("sp", self.o["on_p"][l, h].rearrange("(k o) -> k o", o=1), Cft[0:64, 128:129], reads=[Cft])
        self.interleave([head(0), head(1)])
    aden = self.R32("b_aden")
    self.ACT(aden[0:4, 0:T], rowsA[0:4, 0:T], AF.Abs, reads=[rowsA], writes=[aden])
    self.TT("dve", aden[0:4, 0:T], aden[0:4, 0:T], emt[0:4, 0:T], ALU.max, reads=[aden, emt], writes=[aden])
    self.RECIP(aden[0:4, 0:T], aden[0:4, 0:T], reads=[aden], writes=[aden])
    self._b_aden = aden


def _ones_row(self, T):
    return self.cs("ones", rows=4, n=128) if T <= 128 else self.onesrow[0:4, 0:T]


Builder.branch_B = _branch_B
Builder.ones_row = _ones_row


def _branch_C(self, g, l):
    c, cfg, d = self.c, self.cfg, self.d
    L = cfg.depth
    T, nt = g.T, g.nt
    samp = g.kind == "s"
    seglen = 8 if samp else 128
    spt = 128 // seglen
    nseg = T // seglen
    nsteps = 2 if samp else 6
    ident = self.cs("ident")
    nident = self.cs("nident")
    pos_strict = self.cs("posS_strict_ts" if samp else "posP_strict_ts")
    pos_incl = self.cs("posS_incl" if samp else "posP_incl")
    rm = self.cs("rm16")
    i4 = self.cs("i4", rows=4)
    ones_bf = self.cs("ones", bf=True, n=128)
    identb = self.cs("ident", bf=True)
    nidentb = self.cs("nident", bf=True)
    pos_strictb = self.cs("posS_strict_ts" if samp else "posP_strict_ts", bf=True)
    pos_inclb = self.cs("posS_incl" if samp else "posP_incl", bf=True)
    pr = self.prow
    cseg, clen = (16, 8) if samp else (1, T)
    XW = cseg * (clen + 3)
    w5, w5s = self.win(l, 5640, 8)
    bank = self.dense_bank()
    for k in range(8):
        self.MM(bank[0:4, 0:T], w5[:, k, 0:4], self.hT[:, k, 0:T], start=(k == 0), stop=(k == 7), reads=[w5s, self.hT], writes=[bank])
    beta = self.R32("b_bi")
    self.ACT(beta[0:4, 0:T], bank[0:4, 0:T], AF.Sigmoid, reads=[bank], writes=[beta])
    bank = self.dense_bank()
    for k in range(8):
        self.MM(bank[0:4, 0:T], w5[:, k, 4:8], self.hT[:, k, 0:T], start=(k == 0), stop=(k == 7), reads=[w5s, self.hT], writes=[bank])
    gg = self.R32("b_sp")
    self.ACT(gg[0:4, 0:T], bank[0:4, 0:T], AF.Exp, reads=[bank, pr], writes=[gg], bias=pr[0:4, 2 * L + l:2 * L + l + 1])
    self.ACT(gg[0:4, 0:T], gg[0:4, 0:T], AF.Ln, reads=[gg], writes=[gg], bias=1.0)
    self.TS("dve", gg[0:4, 0:T], gg[0:4, 0:T], pr[0:4, 3 * L + l:3 * L + l + 1], None, ALU.mult, reads=[gg, pr], writes=[gg])
    b = self.R32("b_Bn")
    self.SCAN(b[0:4, 0:T], self.cs("rst8" if samp else "rst128", rows=4, n=T), gg[0:4, 0:T], 0.0, ALU.mult, ALU.add,
              reads=[self.cst, gg], writes=[b])
    nb = self.R32("b_u")
    self.TS("dve", nb[0:4, 0:T], b[0:4, 0:T], -1.0, None, ALU.mult, reads=[b], writes=[nb])
    nbeta = self.R32("b_mu")
    self.TS("dve", nbeta[0:4, 0:T], beta[0:4, 0:T], -1.0, None, ALU.mult, reads=[beta], writes=[nbeta])
    eb = self.R32("b_wint")
    self.ACT(eb[0:4, 0:T], b[0:4, 0:T], AF.Exp, reads=[b], writes=[eb])
    bebe = self.R32("b_mt")
    self.TT("dve", bebe[0:4, 0:T], beta[0:4, 0:T], eb[0:4, 0:T], ALU.mult, reads=[beta, eb], writes=[bebe])
    ebl = self.R32("b_emt")
    b3 = b[0:4, 0:T].rearrange("p (s c) -> p s c", c=seglen)
    self.TT("dve", ebl[0:4, 0:T].rearrange("p (s c) -> p s c", c=seglen), b3, b3[:, :, seglen - 1:seglen].to_broadcast([4, nseg, seglen]),
            ALU.subtract, reads=[b], writes=[ebl])
    self.ACT(ebl[0:4, 0:T], ebl[0:4, 0:T], AF.Exp, reads=[ebl], writes=[ebl], scale=-1.0)
    eblast = self.R32("b_wold", 16)
    self.CP("dve", eblast[0:4, 0:nseg].unsqueeze(2), eb[0:4, 0:T].rearrange("p (s c) -> p s c", c=seglen)[:, :, seglen - 1:seglen],
            reads=[eb], writes=[eblast])
    cols = self.T32("c_cols", 24 * cfg.tg)
    rowlist = (b, nb, nbeta, beta, bebe, ebl)
    for j in range(nt):
        jc = slice(j * 128, (j + 1) * 128)
        pc, pcr = self.small()
        for qi, rw in enumerate(rowlist):
            self.MM(pc[:, qi * 4:(qi + 1) * 4], rw[0:4, jc], i4, reads=[rw, self.cst], writes=[pcr])
        self.CP("dve", cols[:, j * 24:(j + 1) * 24], pc[:, 0:24], reads=[pcr], writes=[cols])

    def col(j, qi, h):
        o = j * 24 + qi * 4 + h
        return cols[:, o:o + 1]
    yield "pre"
    for p in range(2):
        wcq, wcqs = self.win(l, 3592 + p * 256, 256)
        wck, wcks = self.win(l, 4104 + p * 256, 256)
        wcv, wcvs = self.win(l, 4616 + p * 256, 256)
        wcg, wcgs = self.win(l, 5128 + p * 256, 256)
        def head(hh):
            h = 2 * p + hh
            c0 = hh * 128
            hs = [0]

            def hsmall():
                pool = (self.ps_small[hh], self.ps_dense[hh])
                b_ = pool[hs[0] % 2]
                hs[0] += 1
                return b_[:, 0:128], b_
            if samp:
                c.dma("sp", self.SsF[hh][:, :, 0:128], d["st_g"][l, :, h].rearrange("s k v -> k s v"), writes=[self.SsF[hh]])
                self.CP("act", self.SsB[hh][:, :, :], self.SsF[hh][:, :, 0:128], reads=[self.SsF[hh]], writes=[self.SsB[hh]])
                yield
            outs = []
            for ci, (wv_, ws_) in enumerate(((wcq, wcqs), (wck, wcks), (wcv, wcvs))):
                chunk = ci * 4 + h
                ps = self.proj_fm(wv_, ws_, c0, 128, T, bank=self.ps_dense[hh])
                xe = self.W(0, hh)
                xe3 = xe[:, 0:XW].rearrange("p (s c) -> p s c", c=clen + 3)
                self.CP("act", xe3[:, :, 3:3 + clen], ps[:, 0:T].rearrange("p (s c) -> p s c", c=clen), reads=[ps], writes=[xe])
                yield
                if samp:
                    cvs = self.T32(f"c_cvs_{hh}", 128, p=48)
                    c.dma("sp", cvs[0:48, :], d["st_v"][l].rearrange("s j c -> (s j) c")[:, chunk * 128:(chunk + 1) * 128], writes=[cvs])
                    bk_ = self.ps_dense[hh]
                    self.TR(bk_[:, 0:48], cvs[0:48, 0:128], ident[0:48, 0:48], reads=[cvs, self.cst], writes=[bk_])
                    self.CP("dve", xe3[:, :, 0:3], bk_[:, 0:48].rearrange("p (s c) -> p s c", c=3), reads=[bk_], writes=[xe])
                    yield
                else:
                    self.CP("dve", xe3[:, :, 0:3], self.hist[l][:, chunk, :].unsqueeze(1), reads=[self.hist[l]], writes=[xe])
                    yield
                    self.CP("dve", self.hist[l][:, chunk, :].unsqueeze(1), xe3[:, :, clen:clen + 3], reads=[xe], writes=[self.hist[l]])
                    yield
                if samp or g.last:
                    nrow = 48 if samp else 3
                    xl = self.T32(f"c_xl_{hh}", 48)
                    self.CP("dve", xl[:, 0:nrow].rearrange("p (s c) -> p s c", c=3), xe3[:, :, clen:clen + 3], reads=[xe], writes=[xl])
                    yield
                    bk_ = self.ps_dense[hh]
                    self.TR(bk_[0:nrow, 0:128], xl[:, 0:nrow], ident, reads=[xl, self.cst], writes=[bk_])
                    cvo = self.T32(f"c_cvo_{hh}", 128, p=48)
                    self.CP("act", cvo[0:nrow, 0:128], bk_[0:nrow, 0:128], reads=[bk_], writes=[cvo])
                    yield
                    if samp:
                        c.dma("sp", self.o["ov_s"][l].rearrange("s j c -> (s j) c")[:, chunk * 128:(chunk + 1) * 128], cvo[0:48, 0:128], reads=[cvo])
                    else:
                        c.dma("sp", self.o["ov_p"][l][:, chunk * 128:(chunk + 1) * 128], cvo[0:3, 0:128], reads=[cvo])
                acc = self.W(1 + ci, hh)
                acc3 = acc[:, 0:T].rearrange("p (s c) -> p s c", c=clen)
                cw = self.cw
                self.TS("dve", acc3, xe3[:, :, 3:3 + clen], cw[:, l, chunk, 3:4], None, ALU.mult, reads=[xe, cw], writes=[acc])
                for tap in (2, 1, 0):
                    self.STT(acc3, xe3[:, :, tap:tap + clen], cw[:, l, chunk, tap:tap + 1], acc3, ALU.mult, ALU.add, reads=[xe, cw, acc], writes=[acc])
                    yield
                self.ACT(acc[:, 0:T], acc[:, 0:T], AF.Silu, reads=[acc], writes=[acc])
                yield
                outs.append(acc)
            cq, ck, cv = outs
            psg = self.proj_fm(wcg, wcgs, c0, 128, T, bank=self.ps_dense[hh])
            gs = self.W(7, hh)
            self.ACT(gs[:, 0:T], psg[:, 0:T], AF.Silu, reads=[psg], writes=[gs])
            yield
            sq = self.V(0, hh)
            self.ACT(sq[:, 0:T], cq[:, 0:T], AF.Square, reads=[cq], writes=[sq])
            yield
            bk_ = self.ps_dense[hh]
            self.MM(bk_[:, 0:T], ones_bf, sq[:, 0:T], reads=[self.cstb, sq], writes=[bk_])
            rq = self.W(4, hh)
            self.ACT(rq[:, 0:T], bk_[:, 0:T], AF.Ln, reads=[bk_], writes=[rq], bias=NORM_EPS)
            yield
            self.ACT(rq[:, 0:T], rq[:, 0:T], AF.Exp, reads=[rq], writes=[rq], scale=-0.5)
            yield
            q1 = self.V(2, hh)
            self.STT(q1[:, 0:T], cq[:, 0:T], 128.0 ** -0.5, rq[:, 0:T], ALU.mult, ALU.mult, reads=[cq, rq], writes=[q1])
            yield
            bk_ = self.ps_dense[hh]
            self.MM(bk_[:, 0:T], self.cs("sel", rows=4, c0=h * 128, n=128), eb[0:4, 0:T], reads=[self.cst, eb], writes=[bk_])
            q2 = self.V(3, hh)
            self.TT("dve", q2[:, 0:T], q1[:, 0:T], bk_[:, 0:T], ALU.mult, reads=[q1, bk_], writes=[q2])
            yield
            sq2 = self.V(1, hh)
            self.ACT(sq2[:, 0:T], ck[:, 0:T], AF.Square, reads=[ck], writes=[sq2])
            yield
            bk_ = self.ps_dense[hh]
            self.MM(bk_[:, 0:T], ones_bf, sq2[:, 0:T], reads=[self.cstb, sq2], writes=[bk_])
            rk = self.W(4, hh)
            self.ACT(rk[:, 0:T], bk_[:, 0:T], AF.Ln, reads=[bk_], writes=[rk], bias=NORM_EPS)
            yield
            self.ACT(rk[:, 0:T], rk[:, 0:T], AF.Exp, reads=[rk], writes=[rk], scale=-0.5)
            yield
            kn = ck
            self.TT("dve", kn[:, 0:T], ck[:, 0:T], rk[:, 0:T], ALU.mult, reads=[ck, rk], writes=[kn])
            yield
            knb = self.V(4, hh)
            self.CP("act", knb[:, 0:T], kn[:, 0:T], reads=[kn], writes=[knb])
            yield
            bk_ = self.ps_dense[hh]
            self.MM(bk_[:, 0:nseg], self.cs("sel", rows=4, c0=h * 128, n=128), eblast[0:4, 0:nseg], reads=[self.cst, eblast], writes=[bk_])
            decbc = self.T32(f"c_decbc_{hh}", 16)
            self.CP("dve", decbc[:, 0:nseg], bk_[:, 0:nseg], reads=[bk_], writes=[decbc])
            yield
            ob = self.ps_o[hh]
            selh = self.cs("sel", rows=4, c0=h * 128, n=128)
            bA, bB = self.ps_small[hh], self.ps_dense[hh]
            WN = nt * 128
            Qa = [self.T32(f"c_Qb{i}_{hh}", cfg.TMAX) for i in range(2)]
            QTa = [self.T32(f"c_QTb{i}_{hh}", cfg.TMAX) for i in range(2)]
            PTa = [self.T32(f"c_PTb{i}_{hh}", cfg.TMAX) for i in range(2)]
            Dmb = self.T32(f"c_Dmb_{hh}", cfg.TMAX)
            DmTb = self.T32(f"c_DmTb_{hh}", cfg.TMAX)
            tc_ = lambda j: slice(j * 128, (j + 1) * 128)
            for j in range(nt):
                self.MM(bA[:, tc_(j)], selh, b[0:4, tc_(j)], start=True, stop=False, reads=[self.cst, b], writes=[bA])
                self.MM(bA[:, tc_(j)], identb, pos_strictb, start=False, stop=True, reads=[self.cstb], writes=[bA])
            for j in range(nt):
                self.ACT(Dmb[:, tc_(j)], bA[:, tc_(j)], AF.Exp, reads=[bA, cols], writes=[Dmb], scale=-1.0, bias=col(j, 0, h))
            yield
            for j in range(nt):
                self.MM(bB[:, tc_(j)], selh, b[0:4, tc_(j)], start=True, stop=False, reads=[self.cst, b], writes=[bB])
                self.MM(bB[:, tc_(j)], nidentb, pos_inclb, start=False, stop=True, reads=[self.cstb], writes=[bB])
            for j in range(nt):
                self.ACT(DmTb[:, tc_(j)], bB[:, tc_(j)], AF.Exp, reads=[bB, cols], writes=[DmTb], bias=col(j, 1, h))
            yield
            for j in range(nt):
                self.MM(bA[:, tc_(j)], knb[:, tc_(j)], knb[:, tc_(j)], reads=[knb], writes=[bA])
            for j in range(nt):
                self.STT(Qa[0][:, tc_(j)], bA[:, tc_(j)], col(j, 2, h), Dmb[:, tc_(j)], ALU.mult, ALU.mult, reads=[bA, cols, Dmb], writes=[Qa[0]])
            yield
            for j in range(nt):
                self.TR(bB[:, tc_(j)], Qa[0][:, tc_(j)], ident, reads=[Qa[0], self.cst], writes=[bB])
            for j in range(nt):
                self.TT("dve", PTa[0][:, tc_(j)], bB[:, tc_(j)], ident, ALU.add, reads=[bB, self.cst], writes=[PTa[0]])
            self.CP("dve", QTa[0][:, 0:WN], bB[:, 0:WN], reads=[bB], writes=[QTa[0]])
            yield
            for stp in range(nsteps):
                cur, nxt = stp % 2, (stp + 1) % 2
                lastst = stp == nsteps - 1
                for j in range(nt):
                    self.MM(bA[:, tc_(j)], QTa[cur][:, tc_(j)], Qa[cur][:, tc_(j)], reads=[QTa[cur], Qa[cur]], writes=[bA])
                self.CP("act", Qa[nxt][:, 0:WN], bA[:, 0:WN], reads=[bA], writes=[Qa[nxt]])
                yield
                if not lastst:
                    for j in range(nt):
                        self.TR(bB[:, tc_(j)], Qa[nxt][:, tc_(j)], ident, reads=[Qa[nxt], self.cst], writes=[bB])
                    self.CP("dve", QTa[nxt][:, 0:WN], bB[:, 0:WN], reads=[bB], writes=[QTa[nxt]])
                    yield
                for j in range(nt):
                    self.MM(bA[:, tc_(j)], Qa[nxt][:, tc_(j)], PTa[cur][:, tc_(j)], reads=[Qa[nxt], PTa[cur]], writes=[bA])
                self.TT("dve", PTa[nxt][:, 0:WN], PTa[cur][:, 0:WN], bA[:, 0:WN], ALU.add, reads=[PTa[cur], bA], writes=[PTa[nxt]])
                yield
            PTf = PTa[nsteps % 2]
            for j in range(nt):
                jc = slice(j * 128, (j + 1) * 128)
                PT = self._View(PTf[:, jc], "ptv")
                PT.res = PTf.res
                DmT = self._View(DmTb[:, jc], "dmtv")
                DmT.res = DmTb.res
                aT, aTr = hsmall()
                self.MM(aT, knb[:, jc], q1[:, jc], reads=[knb, q1], writes=[aTr])
                attm = self.T16(f"c_attm_{hh}", 128)
                self.TT("dve", attm[:, :], aT, DmT[:, :], ALU.mult, reads=[aTr, DmT], writes=[attm])
                yield
                kt_, ktr = hsmall()
                self.TR(kt_, kn[:, jc], ident, reads=[kn, self.cst], writes=[ktr])
                kbe = self.T32(f"c_kbe_{hh}", 128)
                self.ACT(kbe[:, :], kt_, AF.Copy, reads=[ktr, cols], writes=[kbe], scale=col(j, 4, h))
                yield
                khat = self.T32(f"c_khat_{hh}", 128)
                self.ACT(khat[:, :], kt_, AF.Copy, reads=[ktr, cols], writes=[khat], scale=col(j, 5, h))
                yield
                vt_, vtr = hsmall()
                self.TR(vt_, cv[:, jc], ident, reads=[cv, self.cst], writes=[vtr])
                vb = self.T32(f"c_vb_{hh}", 128)
                self.ACT(vb[:, :], vt_, AF.Copy, reads=[vtr, cols], writes=[vb], scale=col(j, 3, h))
                yield
                WT, WTr = hsmall()
                self.MM(WT, kbe[:, :], PT[:, :], reads=[kbe, PT], writes=[WTr])
                nWT = self.T32(f"c_nWT_{hh}", 128)
                self.ACT(nWT[:, :], WT, AF.Copy, reads=[WTr], writes=[nWT], scale=-1.0)
                yield
                vnT, vnTr = hsmall()
                self.MM(vnT, vb[:, :], PT[:, :], start=True, stop=False, reads=[vb, PT], writes=[vnTr])
                for i in range(spt):
                    seg = j * spt + i
                    lc = slice(i * seglen, (i + 1) * seglen)
                    if samp:
                        Sf, Sfr = self.SsF[hh][:, seg, 0:128], self.SsF[hh]
                    else:
                        Sft = self.SC[l][h][0]
                        Sf, Sfr = Sft[:, :], Sft
                    self.MM(vnT[:, lc], Sf, nWT[:, lc], start=False, stop=(i == spt - 1), reads=[Sfr, nWT], writes=[vnTr])
                vnTs = self.T32(f"c_vnTs_{hh}", 128)
                self.CP("act", vnTs[:, :], vnT, reads=[vnTr], writes=[vnTs])
                yield
                vn_, vnr = hsmall()
                self.TR(vn_, vnTs[:, :], ident, reads=[vnTs, self.cst], writes=[vnr])
                vnb = self.T16(f"c_vnb_{hh}", 128)
                self.CP("act", vnb[:, :], vn_, reads=[vnr], writes=[vnb])
                yield
                self.MM(ob[:, jc], vnb[:, :], attm[:, :], start=True, stop=False, reads=[vnb, attm], writes=[ob])
                for i in range(spt):
                    seg = j * spt + i
                    sc = slice(seg * seglen, (seg + 1) * seglen)
                    if samp:
                        Sf, Sb, Sfr, Sbr = self.SsF[hh][:, seg, 0:128], self.SsB[hh][:, seg, :], self.SsF[hh], self.SsB[hh]
                    else:
                        Sft, Sbt = self.SC[l][h]
                        Sf, Sb, Sfr, Sbr = Sft[:, :], Sbt[:, :], Sft, Sbt
                    self.MM(ob[:, sc], Sb, q2[:, sc], start=False, stop=(i == spt - 1), reads=[Sbr, q2], writes=[ob])
                    if spt > 1:
                        khm = self.T16(f"khseg{i % 2}_{hh}", 128)
                        self.TS("dve", khm[:, :], khat[:, :], rm[:, i:i + 1], None, ALU.mult, reads=[khat, self.cst], writes=[khm])
                    else:
                        khm = self.T16(f"khseg0_{hh}", 128)
                        self.CP("dve", khm[:, :], khat[:, :], reads=[khat], writes=[khm])
                    pS, pSr = hsmall()
                    self.MM(pS, khm[:, :], vnb[:, :], reads=[khm, vnb], writes=[pSr])
                    self.STT(Sf, Sf, decbc[:, seg:seg + 1], pS, ALU.mult, ALU.add, reads=[Sfr, decbc, pSr], writes=[Sfr])
                    yield
                    if not samp:
                        self.CP("act", Sb, Sf, reads=[Sfr], writes=[Sbr])
                        yield
            self.rms_gate(g, l, 2, h, ob, gs, self.gC, first=(h == 0), last=(h == 3))
            if samp:
                c.dma("sp", self.o["og_s"][l, :, h].rearrange("s k v -> k s v"), self.SsF[hh][:, :, 0:128], reads=[self.SsF[hh]])
            elif g.last:
                c.dma("sp", self.o["og_p"][l, h], self.SC[l][h][0][:, :], reads=[self.SC[l][h][0]])

        self.interleave([head(0), head(1)])


Builder.branch_C = _branch_C


def _ln_prefetch(self, l, which, half):
    c, d = self.c, self.d
    gname, bname = ("ln1_g", "ln1_b") if which == 1 else ("ln2_g", "ln2_b")
    hs = slice(half * 512, (half + 1) * 512)
    c.dma("sp", self.lnp[0][:], d[gname][l][hs].partition_broadcast(128), writes=[self.lnp[0]])
    c.dma("sp", self.lnp[1][:], d[bname][l][hs].partition_broadcast(128), writes=[self.lnp[1]])


def _layer_norm(self, g, l, which):
    c, d = self.c, self.d
    st = self.T32("ln_st", 8 * self.cfg.tg)
    junk = self.big[:, 0:D // 2].bitcast(BF16)
    junkr = self.big
    tiles = range(g.nt)
    X = lambda j: self.h_tok[:, j, :]
    S = lambda j: st[:, j * 8:(j + 1) * 8]
    for j in tiles:
        self.ACT(junk, X(j), AF.Identity, reads=[self.h_tok], writes=[junkr, st], accum=S(j)[:, 0:1])
    for j in tiles:
        self.TS("dve", S(j)[:, 1:2], S(j)[:, 0:1], -1.0 / D, None, ALU.mult, reads=[st], writes=[st])
    for j in tiles:
        self.ACT(junk, X(j), AF.Square, reads=[self.h_tok, st], writes=[junkr, st], bias=S(j)[:, 1:2], accum=S(j)[:, 2:3])
    for j in tiles:
        self.ACT(S(j)[:, 3:4], S(j)[:, 2:3], AF.Ln, reads=[st], writes=[st], scale=1.0 / D, bias=LN_EPS)
    for j in tiles:
        self.ACT(S(j)[:, 3:4], S(j)[:, 3:4], AF.Exp, reads=[st], writes=[st], scale=-0.5)
    for j in tiles:
        self.TS("dve", X(j), X(j), S(j)[:, 1:2], S(j)[:, 3:4], ALU.add, ALU.mult, reads=[self.h_tok, st], writes=[self.h_tok])
    for half in range(2):
        hs = slice(half * 512, (half + 1) * 512)
        if half == 1:
            self.ln_prefetch(l, which, 1)
        for j in tiles:
            x = self.h_tok[:, j, hs]
            self.TT("dve", x, x, self.lnp[0][:], ALU.mult, reads=[self.h_tok, self.lnp[0]], writes=[self.h_tok])
        for j in tiles:
            x = self.h_tok[:, j, hs]
            self.TT("dve", x, x, self.lnp[1][:], ALU.add, reads=[self.h_tok, self.lnp[1]], writes=[self.h_tok])


def _merge(self, g, l):
    c, cfg, d = self.c, self.cfg, self.d
    self.dense_pool = self.ps_dense + self.ps_small + self.ps_rows + self.ps_o
    T, nt, TM = g.T, g.nt, cfg.TMAX
    self.ln_prefetch(l, 1, 0)
    big = self.big
    macc = big[:, 0:8 * TM].rearrange("p (k t) -> p k t", t=TM)
    mbf = big[:, 8 * TM:12 * TM].bitcast(BF16).rearrange("p (k t) -> p k t", t=TM)
    for n in range(3):
        wbr = d["w_branch"][l, n].rearrange("(k p) n -> p k n", p=128)
        for dh in range(2):
            wbv, wbs = self.ws.get(("w_branch", l, n, dh), wbr[:, :, dh * 512:(dh + 1) * 512], 4, 512)
            for mgb in range(2):
                wmg, wmgs = self.win(l, 5648 + n * 1024 + dh * 512 + mgb * 256, 256)
                for q in range(2):
                    j = dh * 4 + mgb * 2 + q
                    psm = self.proj_fm(wmg, wmgs, q * 128, 128, T)
                    sg = self.W(0, 0)
                    self.ACT(sg[:, 0:T], psm[:, 0:T], AF.Sigmoid, reads=[psm], writes=[sg])
                    psz = self.dense_bank()
                    for k in range(4):
                        self.MM(psz[:, 0:T], wbv[:, k, (mgb * 2 + q) * 128:(mgb * 2 + q + 1) * 128], self.yT[n][:, k, 0:T],
                                start=(k == 0), stop=(k == 3), reads=[wbs, self.yT[n]], writes=[psz])
                    if n == 0:
                        self.TT("dve", macc[:, j, 0:T], sg[:, 0:T], psz[:, 0:T], ALU.mult, reads=[sg, psz], writes=[big])
                    else:
                        prod = self.W(1, 0)
                        self.TT("dve", prod[:, 0:T], sg[:, 0:T], psz[:, 0:T], ALU.mult, reads=[sg, psz], writes=[prod])
                        if n == 1:
                            self.TT("dve", macc[:, j, 0:T], macc[:, j, 0:T], prod[:, 0:T], ALU.add, reads=[big, prod], writes=[big])
                        else:
                            self.TT("dve", mbf[:, j, 0:T], macc[:, j, 0:T], prod[:, 0:T], ALU.add, reads=[big, prod], writes=[big])
    wor = d["w_out"][l].rearrange("(k p) n -> p k n", p=128)
    for half in range(2):
        hs = slice(half * 512, (half + 1) * 512)
        w0, w0s = self.ws.get(("w_out", l, 0, half), wor[:, 0:4, hs], 4, 512)
        w1, w1s = self.ws.get(("w_out", l, 1, half), wor[:, 4:8, hs], 4, 512)
        for j in range(nt):
            bank = self.dense_bank()
            for k in range(8):
                wv, wsx = (w0, w0s) if k < 4 else (w1, w1s)
                self.MM(bank[:, 0:512], mbf[:, k, j * 128:(j + 1) * 128], wv[:, k % 4, :], start=(k == 0), stop=(k == 7),
                        reads=[big, wsx], writes=[bank])
            self.STT(self.h_tok[:, j, hs], self.h_tok[:, j, hs], ALPHA, bank[:, 0:512], ALU.mult, ALU.add,
                     reads=[self.h_tok, bank], writes=[self.h_tok])
    self.dense_pool = self.ps_dense
    self.layer_norm(g, l, 1)
    for j in range(nt):
        self.dump(f"h1_{j}_l{l}_g{g.idx}", self.h_tok, self.h_tok[:, j, :], 128, D)
    self.make_hT(g)


def _ffn(self, g, l):
    c, cfg, d = self.c, self.cfg, self.d
    self.dense_pool = self.ps_dense + self.ps_small + self.ps_rows
    T, nt, TM = g.T, g.nt, cfg.TMAX
    self.ln_prefetch(l, 2, 0)
    big = self.big
    aT = big[:, 0:11 * TM].bitcast(BF16).rearrange("p (k t) -> p k t", t=TM)
    wfi = d["w_ffn_in"][l].rearrange("(k p) n -> p k n", p=128)
    for part in range(2):
        for blk in range(11):
            wv, wsx = self.ws.get(("w_ffn_in", l, part, blk), wfi[:, :, part * FH + blk * 256:part * FH + (blk + 1) * 256], 8, 256)
            for q in range(2):
                j = blk * 2 + q
                ps = self.proj_fm(wv, wsx, q * 128, 128, T)
                if part == 0:
                    self.ACT(aT[:, j, 0:T], ps[:, 0:T], AF.Silu, reads=[ps], writes=[big])
                else:
                    self.TT("dve", aT[:, j, 0:T], aT[:, j, 0:T], ps[:, 0:T], ALU.mult, reads=[big, ps], writes=[big])
    wfo = d["w_ffn_out"][l].rearrange("(k p) n -> p k n", p=128)
    for half in range(2):
        hs = slice(half * 512, (half + 1) * 512)
        accs = [self.ps_o[j % 2] for j in range(nt)]
        assert nt <= 2
        for kb in range(6):
            k0 = kb * 4
            nk = min(4, 22 - k0)
            wv, wsx = self.ws.get(("w_ffn_out", l, kb, half), wfo[:, k0:k0 + nk, hs], nk, 512)
            for j in range(nt):
                for kk in range(nk):
                    k = k0 + kk
                    self.MM(accs[j][:, 0:512], aT[:, k, j * 128:(j + 1) * 128], wv[:, kk, :], start=(k == 0), stop=(k == 21),
                            reads=[big, wsx], writes=[accs[j]])
        for j in range(nt):
            self.STT(self.h_tok[:, j, hs], self.h_tok[:, j, hs], ALPHA, accs[j][:, 0:512], ALU.mult, ALU.add,
                     reads=[self.h_tok, accs[j]], writes=[self.h_tok])
    self.dense_pool = self.ps_dense
    self.layer_norm(g, l, 2)
    if l < cfg.depth - 1:
        self.make_hT(g)


Builder.layer_norm = _layer_norm
Builder.ln_prefetch = _ln_prefetch
Builder.merge = _merge
Builder.ffn = _ffn


_W_NAMES = ["w_in", "lb_logits", "a_norm_g", "b_mi", "b_mf", "b_norm_g", "conv_w", "a_log", "dt_bias",
            "c_norm_g", "w_branch", "w_out", "ln1_g", "ln1_b", "w_ffn_in", "w_ffn_out", "ln2_g", "ln2_b"]


def run_cfg(cfg, inputs, n_cores=8, trace=False):
    b = Builder(cfg)
    nc = b.build()
    consts = _make_consts()
    f = lambda a: np.ascontiguousarray(a, dtype=np.float32)
    in_maps = []
    for i in range(n_cores):
        s0, s1 = i * 16, (i + 1) * 16
        m = dict(
            xp=f(inputs["x_prompt"][i][:cfg.seq]),
            xs=f(inputs["x_sample"][s0:s1].reshape(128, D)),
            st_a=f(inputs["state_hgrn"][:cfg.depth, s0:s1]), st_c=f(inputs["state_mlstm_c"][:cfg.depth, s0:s1]),
            st_n=f(inputs["state_mlstm_n"][:cfg.depth, s0:s1]), st_m=f(inputs["state_mlstm_m"][:cfg.depth, s0:s1]),
            st_g=f(inputs["state_gdn"][:cfg.depth, s0:s1]), st_v=f(inputs["state_gdn_conv"][:cfg.depth, s0:s1]),
            consts=consts,
        )
        for nm in _W_NAMES:
            m[nm] = f(inputs[nm][:cfg.depth])
        in_maps.append(m)
    res = run_bass_kernel_spmd(nc, in_maps, core_ids=list(range(n_cores)), **({"trace": True} if trace else {}))
    return res, b


def kernel(**inputs):
    cfg = Cfg(depth=DEPTH, n_pg=8, tg=2, sample=True)
    res, b = run_cfg(cfg, inputs, 8)
    R = res.results
    cat = lambda k, ax: np.concatenate([r[k] for r in R], axis=ax)
    stk = lambda k: np.stack([r[k] for r in R], axis=1)
    y_p = np.stack([r["yp"] for r in R], axis=0)
    y_s = np.concatenate([r["ys"].reshape(16, 8, D) for r in R], axis=0)
    outs = (y_p, y_s,
            stk("oa_p"), cat("oa_s", 1), stk("oc_p"), cat("oc_s", 1), stk("on_p"), cat("on_s", 1),
            stk("om_p"), cat("om_s", 1), stk("og_p"), cat("og_s", 1), stk("ov_p"), cat("ov_s", 1))
    return tuple(np.ascontiguousarray(o, dtype=np.float32) for o in outs)
```

```python
import numpy as np
from contextlib import ExitStack
import concourse.bass as bass
import concourse.mybir as mybir
from concourse.bass_utils import run_bass_kernel_spmd

F32 = mybir.dt.float32
BF16 = mybir.dt.bfloat16
ALU = mybir.AluOpType
AF = mybir.ActivationFunctionType
AX = mybir.AxisListType

D = 1024
NIN = 8720
FH = 2816
DEPTH = 4
ALPHA = (2 * DEPTH) ** 0.25
LN_EPS = 1e-5
NORM_EPS = 1e-6
BIG = 30000.0

ENGS = ("pe", "act", "dve", "pool", "sp")
N_DMA_SEMS = 8
SAME_ENG_SYNC = True


class Res:
    def __init__(self, name=""):
        self.name = name
        self.last_write = None
        self.reads = {}
        self.excl = False


class Tl:
    def __init__(self, t, name):
        self.t = t
        self.res = Res(name)
        self.name = name

    def __getitem__(self, idx):
        return self.t[idx]


class Ctx:
    def __init__(self, nc, stack):
        self.nc = nc
        self.stack = stack
        self.ops = {e: [] for e in ENGS}
        self.seq = {e: 0 for e in ENGS}
        self.esem = {e: stack.enter_context(nc.semaphore("s_" + e)) for e in ENGS}
        self.dsem, self.dcount, self.dnext = {}, {}, {}
        for q in ("sp", "act", "pool"):
            self.dsem[q] = [stack.enter_context(nc.semaphore(f"d_{q}{i}")) for i in range(N_DMA_SEMS)]
            self.dcount[q] = [0] * N_DMA_SEMS
            self.dnext[q] = 0
        self.waited = {e: {} for e in ENGS}
        self.semobj = {("e", e): self.esem[e] for e in ENGS}
        for q in self.dsem:
            for i, s in enumerate(self.dsem[q]):
                self.semobj[("d", q, i)] = s
        self.n_inst = 0
        self.out_tokens = []

    def sb(self, name, shape, dtype=F32):
        t = self.stack.enter_context(self.nc.sbuf_tensor(name, list(shape), dtype))
        return Tl(t, name)

    def ps(self, name, shape, dtype=F32):
        t = self.stack.enter_context(self.nc.psum_tensor(name, list(shape), dtype))
        tl = Tl(t, name)
        tl.res.excl = True
        return tl

    def _collect(self, eng, reads, writes):
        deps = {}

        def need(tok):
            if tok is None:
                return
            key, val = tok
            if key == ("e", eng) and (eng == "pe" or not SAME_ENG_SYNC):
                return
            if val > deps.get(key, 0):
                deps[key] = val
        for r in reads:
            r = getattr(r, "res", r)
            need(r.last_write)
            if r.excl:
                for k, v in r.reads.items():
                    need((k, v))
        for w in writes:
            w = getattr(w, "res", w)
            need(w.last_write)
            for k, v in w.reads.items():
                need((k, v))
        out = []
        wd = self.waited[eng]
        for k, v in deps.items():
            if wd.get(k, 0) >= v:
                continue
            wd[k] = v
            out.append((k, v))
        return out

    def _commit(self, reads, writes, token):
        key, val = token
        for r in reads:
            r = getattr(r, "res", r)
            if r.reads.get(key, 0) < val:
                r.reads[key] = val
        for w in writes:
            w = getattr(w, "res", w)
            w.last_write = token
            w.reads = {}

    def op(self, eng, fn, reads=(), writes=()):
        waits = self._collect(eng, reads, writes)
        self.seq[eng] += 1
        token = (("e", eng), self.seq[eng])
        self.ops[eng].append(("op", fn, waits))
        self._commit(reads, writes, token)
        self.n_inst += 1

    def dma(self, q, out, in_, reads=(), writes=(), is_output=False, **kw):
        i = self.dnext[q]
        self.dnext[q] = (i + 1) % N_DMA_SEMS
        waits = self._collect(q, reads, writes)
        key = ("d", q, i)
        prev = self.dcount[q][i] * 16
        if prev > 0 and self.waited[q].get(key, 0) < prev:
            self.waited[q][key] = prev
            waits.append((key, prev))
        self.dcount[q][i] += 1
        token = (key, self.dcount[q][i] * 16)
        self.ops[q].append(("dma", (out, in_, kw, self.dsem[q][i]), waits))
        self._commit(reads, writes, token)
        self.n_inst += 1
        return token

    def finish(self):
        toks = []
        for q in self.dsem:
            for i in range(N_DMA_SEMS):
                if self.dcount[q][i] > 0:
                    toks.append((("d", q, i), self.dcount[q][i] * 16))
        for e in ENGS:
            if self.seq[e] > 0:
                toks.append((("e", e), self.seq[e]))
        self.ops["sp"].append(("wait", None, toks))
        nc = self.nc
        block = self.stack.enter_context(nc.Block())
        ctx = self

        def run(eng_name, engobj):
            esem = ctx.esem[eng_name]
            for kind, payload, waits in ctx.ops[eng_name]:
                for key, val in waits:
                    engobj.wait_ge(ctx.semobj[key], val)
                if kind == "op":
                    payload(engobj).then_inc(esem, 1)
                elif kind == "dma":
                    out, in_, kw, sem = payload
                    engobj.dma_start(out=out, in_=in_, **kw).then_inc(sem, 16)

        block.sync(lambda e: run("sp", e))
        block.scalar(lambda e: run("act", e))
        block.vector(lambda e: run("dve", e))
        block.gpsimd(lambda e: run("pool", e))
        block.tensor(lambda e: run("pe", e))


def _const_layout():
    cols = {}
    off = 0

    def add(name, n):
        nonlocal off
        cols[name] = (off, n)
        off += n
    add("ident", 128)
    add("maskA64", 128)
    add("mask8", 128)
    add("mask128", 128)
    add("posP_incl", 128)
    add("posS_incl", 128)
    add("posP_strict_ts", 128)
    add("posS_strict_ts", 128)
    add("sel", 512)
    add("onecol", 16)
    add("i4", 4)
    add("rm2", 2)
    add("rm16", 16)
    add("rst64", 256)
    add("rst8", 256)
    add("rst128", 256)
    add("ones", 256)
    add("nident", 128)
    return cols, off


CL, NCONST = _const_layout()


def _make_consts():
    c = np.zeros((128, NCONST), np.float32)

    def put(name, arr):
        o, n = CL[name]
        c[:arr.shape[0], o:o + n] = arr
    s = np.arange(128)[:, None]
    t = np.arange(128)[None, :]
    put("ident", (s == t).astype(np.float32))
    put("maskA64", ((s // 64 == t // 64) & (s <= t)).astype(np.float32))
    put("mask8", ((s // 8 == t // 8) & (s <= t)).astype(np.float32))
    put("mask128", (s <= t).astype(np.float32))
    put("posP_incl", np.where(s <= t, 0.0, BIG).astype(np.float32))
    put("posS_incl", np.where((s // 8 == t // 8) & (s <= t), 0.0, BIG).astype(np.float32))
    tt = np.arange(128)[:, None]
    ss = np.arange(128)[None, :]
    put("posP_strict_ts", np.where(ss < tt, 0.0, BIG).astype(np.float32))
    put("posS_strict_ts", np.where((ss // 8 == tt // 8) & (ss < tt), 0.0, BIG).astype(np.float32))
    sel = np.zeros((4, 512), np.float32)
    for h in range(4):
        sel[h, h * 128:(h + 1) * 128] = 1.0
    put("sel", sel)
    oc = np.zeros((128, 16), np.float32)
    for h in range(4):
        oc[:, 4 * h + h] = 1.0
    put("onecol", oc)
    put("i4", np.eye(4, dtype=np.float32))
    put("rm2", (s // 64 == np.arange(2)[None, :]).astype(np.float32))
    put("rm16", (s // 8 == np.arange(16)[None, :]).astype(np.float32))
    tr = np.arange(256)[None, :]
    put("rst64", np.broadcast_to((tr % 64 != 0).astype(np.float32), (128, 256)))
    put("rst8", np.broadcast_to((tr % 8 != 0).astype(np.float32), (128, 256)))
    put("rst128", np.broadcast_to((tr % 128 != 0).astype(np.float32), (128, 256)))
    put("ones", np.ones((128, 256), np.float32))
    put("nident", -(s == t).astype(np.float32))
    return c


def _extra_consts(c):
    return c


class WStream:
    SLOT = 2048

    def __init__(self, c, nslot, pf):
        self.c = c
        self.nslot = nslot
        self.pf = pf
        self.slots = [c.sb(f"wslot{i}", [128, self.SLOT], BF16) for i in range(nslot)]
        self.sched = []
        self.blocks = {}
        self.rec = True
        self.pos = 0
        self.issued = 0
        self.scratch = None
        self.bres = []

    def start_real(self, nc):
        self.rec = False
        self.pos = 0
        self.issued = 0
        nb = max(1, len(self.blocks))
        self.scratch = nc.dram_tensor("w_scratch", [nb, 128, self.SLOT], BF16, kind="Internal").ap()
        self.bres = [Res(f"wblk{i}") for i in range(nb)]

    def emit_conversions(self):
        for key, (idx, src, k, n) in self.blocks.items():
            dst = self.scratch[idx][:, 0:k * n].rearrange("p (k n) -> p k n", k=k)
            self.c.dma("pool", dst, src, writes=[self.bres[idx]])

    def _view(self, i, k, n):
        s = self.slots[i % self.nslot]
        return s[:, 0:k * n].rearrange("p (k n) -> p k n", k=k), s

    def get(self, key, src, k, n):
        assert k * n <= self.SLOT
        if self.rec:
            if key not in self.blocks:
                self.blocks[key] = (len(self.blocks), src, k, n)
            self.sched.append((key, k, n))
            return self._view(len(self.sched) - 1, k, n)
        i = self.pos
        self.pos += 1
        lim = min(len(self.sched), i + 1 + self.pf)
        while self.issued < lim:
            j = self.issued
            skey, sk, sn = self.sched[j]
            idx = self.blocks[skey][0]
            s = self.slots[j % self.nslot]
            self.c.dma("sp", s[:, 0:sk * sn], self.scratch[idx][:, 0:sk * sn], reads=[self.bres[idx]], writes=[s])
            self.issued += 1
        return self._view(i, k, n)


class Cfg:
    def __init__(self, depth=DEPTH, n_pg=8, tg=2, sample=True, dbg=False, parts="ABCMF"):
        self.parts = parts
        self.depth = depth
        self.n_pg = n_pg
        self.tg = tg
        self.sample = sample
        self.dbg = dbg
        self.TMAX = tg * 128
        self.seq = n_pg * tg * 128


IN_NAMES = ["xp", "xs", "st_a", "st_c", "st_n", "st_m", "st_g", "st_v",
            "w_in", "lb_logits", "a_norm_g", "b_mi", "b_mf", "b_norm_g", "conv_w", "a_log", "dt_bias",
            "c_norm_g", "w_branch", "w_out", "ln1_g", "ln1_b", "w_ffn_in", "w_ffn_out", "ln2_g", "ln2_b", "consts"]


class Builder:
    def __init__(self, cfg):
        self.cfg = cfg
        L = cfg.depth
        nc = bass.Bass("TRN2", target_bir_lowering=False)
        self.nc = nc

        def din(name, shape):
            return nc.dram_tensor(name, list(shape), F32, kind="ExternalInput").ap()

        def dout(name, shape):
            return nc.dram_tensor(name, list(shape), F32, kind="ExternalOutput").ap()
        S = cfg.seq
        self.d = dict(
            xp=din("xp", [S, D]), xs=din("xs", [128, D]),
            st_a=din("st_a", [L, 16, 4, 128, 128]), st_c=din("st_c", [L, 16, 4, 128, 64]),
            st_n=din("st_n", [L, 16, 4, 64]), st_m=din("st_m", [L, 16, 4]),
            st_g=din("st_g", [L, 16, 4, 128, 128]), st_v=din("st_v", [L, 16, 3, 1536]),
            w_in=din("w_in", [L, D, NIN]), lb_logits=din("lb_logits", [L, 512]),
            a_norm_g=din("a_norm_g", [L, 512]), b_mi=din("b_mi", [L, 4]), b_mf=din("b_mf", [L, 4]),
            b_norm_g=din("b_norm_g", [L, 512]), conv_w=din("conv_w", [L, 4, 1536]),
            a_log=din("a_log", [L, 4]), dt_bias=din("dt_bias", [L, 4]), c_norm_g=din("c_norm_g", [L, 512]),
            w_branch=din("w_branch", [L, 3, 512, D]), w_out=din("w_out", [L, D, D]),
            ln1_g=din("ln1_g", [L, D]), ln1_b=din("ln1_b", [L, D]),
            w_ffn_in=din("w_ffn_in", [L, D, 2 * FH]), w_ffn_out=din("w_ffn_out", [L, FH, D]),
            ln2_g=din("ln2_g", [L, D]), ln2_b=din("ln2_b", [L, D]),
            consts=din("consts", [128, NCONST]),
        )
        self.o = dict(
            yp=dout("yp", [S, D]), ys=dout("ys", [128, D]),
            oa_p=dout("oa_p", [L, 4, 128, 128]), oa_s=dout("oa_s", [L, 16, 4, 128, 128]),
            oc_p=dout("oc_p", [L, 4, 128, 64]), oc_s=dout("oc_s", [L, 16, 4, 128, 64]),
            on_p=dout("on_p", [L, 4, 64]), on_s=dout("on_s", [L, 16, 4, 64]),
            om_p=dout("om_p", [L, 4]), om_s=dout("om_s", [L, 16, 4]),
            og_p=dout("og_p", [L, 4, 128, 128]), og_s=dout("og_s", [L, 16, 4, 128, 128]),
            ov_p=dout("ov_p", [L, 3, 1536]), ov_s=dout("ov_s", [L, 16, 3, 1536]),
        )
        self.dbg_aps = {}
        self.dbg_off = 0
        if cfg.dbg:
            self.dbgt = dout("dbg", [128, 65536])

    def build(self):
        with ExitStack() as st:
            self.st = st
            self.c = Ctx(self.nc, st)
            self.alloc()
            self.c_real = self.c
            self.c = DryCtx()
            self.ws.c = self.c
            self.emit()
            self.c = self.c_real
            self.ws.c = self.c
            self.ws.start_real(self.nc)
            self.emit()
            self.c.finish()
        return self.nc

    def dry(self):
        return isinstance(self.c, DryCtx)

    def alloc(self):
        c, cfg = self.c, self.cfg
        L, TM = cfg.depth, cfg.TMAX
        self.tiles = {}
        self.cst = c.sb("cst", [128, NCONST])
        self.cstb = c.sb("cstb", [128, 144 + 6 * 128], BF16)
        self.ws = WStream(c, 8, 4)
        self.ps_dense = [c.ps(f"psd{i}", [128, 512]) for i in range(2)]
        self.ps_o = [c.ps(f"pso{i}", [128, 512]) for i in range(2)]
        self.ps_small = [c.ps(f"pss{i}", [128, 512]) for i in range(2)]
        self.ps_rows = [c.ps(f"psr{i}", [128, 512]) for i in range(2)]
        self.small_res = [[Res(f"pss{i}q{q}") for q in range(4)] for i in range(2)]
        self.n_dense = 0
        self.n_small = 0
        self.n_o = 0
        self.small_pool = self.ps_small
        self.dense_pool = self.ps_dense
        self.h_tok = c.sb("h_tok", [128, cfg.tg, D])
        self.hT = c.sb("hT", [128, 8, TM], BF16)
        self.yT = [c.sb(f"yT{n}", [128, 4, TM], BF16) for n in range(3)]
        self.big = c.sb("big", [128, 12 * TM])
        self.lbt = c.sb("lbt", [128, L, 4])
        self.omlt = c.sb("omlt", [128, L, 4])
        self.nomlt = c.sb("nomlt", [128, L, 4])
        self.gA = c.sb("gA", [128, L, 4])
        self.gB = c.sb("gB", [128, L, 4])
        self.gC = c.sb("gC", [128, L, 4])
        self.cw = c.sb("cw", [128, L, 12, 4])
        self.prow = c.sb("prow", [4, 8 * L])
        nst = 2 * L * 4
        SL = 16 * 129 + 16 * 64
        self.arena = c.sb("arena", [128, max(nst * 128, 2 * SL)])

        class _View:
            def __init__(s_, ap, name):
                s_.ap = ap
                s_.res = Res(name)

            def __getitem__(s_, idx):
                return s_.ap[idx]
        self._View = _View
        self.SA = [[(_View(self.arena[:, (l * 4 + h) * 128:(l * 4 + h + 1) * 128], f"SAf{l}_{h}"), c.sb(f"SAb{l}_{h}", [128, 128], BF16))
                    for h in range(4)] for l in range(L)]
        self.SC = [[(_View(self.arena[:, (L * 4 + l * 4 + h) * 128:(L * 4 + l * 4 + h + 1) * 128], f"SCf{l}_{h}"), c.sb(f"SCb{l}_{h}", [128, 128], BF16))
                    for h in range(4)] for l in range(L)]
        self.SB = [[(c.sb(f"SBf{l}_{h}", [64, 129]), c.sb(f"SBb{l}_{h}", [64, 128], BF16), c.sb(f"SBn{l}_{h}", [64, 4], BF16))
                    for h in range(4)] for l in range(L)]
        self.carryB = [c.sb(f"carryB{l}", [4, 4]) for l in range(L)]
        self.hist = [c.sb(f"hist{l}", [128, 12, 3]) for l in range(L)]
        if cfg.sample:
            self.SsF = [_View(self.arena[:, i * SL:i * SL + 16 * 129].rearrange("p (s n) -> p s n", n=129), f"SsF{i}") for i in range(2)]
            self.SsB = [_View(self.arena[:, i * SL + 16 * 129:(i + 1) * SL].bitcast(BF16).rearrange("p (s n) -> p s n", n=128), f"SsB{i}") for i in range(2)]
            self.SsN = [c.sb(f"SsN{i}", [64, 16, 4], BF16) for i in range(2)]
            self.m0row = c.sb("m0row", [4, 16])
        self.lnp = [c.sb(f"lnp{i}", [128, D // 2]) for i in range(2)]

    def T32(self, role, n=None, p=128):
        key = ("f", role)
        if key not in self.tiles:
            self.tiles[key] = self.c_real.sb("t_" + role, [p, n or self.cfg.TMAX], F32)
        return self.tiles[key]

    def T16(self, role, n=None, p=128):
        key = ("b", role)
        if key not in self.tiles:
            self.tiles[key] = self.c_real.sb("b_" + role, [p, n or self.cfg.TMAX], BF16)
        return self.tiles[key]

    def W(self, i, par):
        return self.T32(f"w{i}_{par}", self.cfg.TMAX + (64 if i == 0 else 0))

    def V(self, i, par):
        return self.T16(f"v{i}_{par}")

    @staticmethod
    def interleave(gens):
        gens = list(gens)
        while gens:
            for gq in list(gens):
                try:
                    next(gq)
                except StopIteration:
                    gens.remove(gq)

    def R32(self, role, n=None):
        return self.T32("r_" + role, n, p=4)

    def dense_bank(self):
        pool = self.dense_pool
        b = pool[self.n_dense % len(pool)]
        self.n_dense += 1
        return b

    def o_bank(self):
        b = self.ps_o[self.n_o % 2]
        self.n_o += 1
        return b

    def small(self):
        pool = self.small_pool
        bank = pool[self.n_small % len(pool)]
        self.n_small += 1
        return bank[:, 0:128], bank

    def cs(self, name, rows=128, bf=False, c0=0, n=None):
        o, w = CL[name]
        if bf:
            o = {"onecol": 0, "ones": 16, "ident": 144, "nident": 272, "posP_incl": 400, "posS_incl": 528,
                 "posP_strict_ts": 656, "posS_strict_ts": 784}[name]
        t = self.cstb if bf else self.cst
        n = w - c0 if n is None else n
        return t[0:rows, o + c0:o + c0 + n]

    def MM(self, out, lhsT, rhs, start=True, stop=True, reads=(), writes=()):
        self.c.op("pe", lambda e: e.matmul(out, lhsT, rhs, start=start, stop=stop), reads, writes)

    def TR(self, out, in_, ident, reads=(), writes=()):
        self.c.op("pe", lambda e: e.transpose(out, in_, ident), reads, writes)

    def ACT(self, out, in_, func, reads=(), writes=(), bias=None, scale=None, accum=None):
        kw = {}
        if bias is not None:
            kw["bias"] = bias
        if scale is not None:
            kw["scale"] = scale
        if accum is not None:
            kw["accum_out"] = accum
        self.c.op("act", lambda e: e.activation(out, in_, func, **kw), reads, writes)

    def TT(self, eng, out, in0, in1, op, reads=(), writes=()):
        self.c.op(eng, lambda e: e.tensor_tensor(out, in0, in1, op), reads, writes)

    def TS(self, eng, out, in0, s1, s2, op0, op1=None, reads=(), writes=()):
        if op1 is None:
            self.c.op(eng, lambda e: e.tensor_scalar(out, in0, s1, None, op0), reads, writes)
        else:
            self.c.op(eng, lambda e: e.tensor_scalar(out, in0, s1, s2, op0, op1), reads, writes)

    def STT(self, out, in0, scalar, in1, op0, op1, reads=(), writes=()):
        self.c.op("dve", lambda e: e.scalar_tensor_tensor(out, in0, scalar, in1, op0, op1), reads, writes)

    def CP(self, eng, out, in_, reads=(), writes=()):
        if eng == "act":
            self.c.op("act", lambda e: e.activation(out, in_, AF.Copy), reads, writes)
        else:
            self.c.op(eng, lambda e: e.tensor_copy(out, in_), reads, writes)

    def SCAN(self, out, d0, d1, init, op0, op1, reads=(), writes=()):
        self.c.op("dve", lambda e: e.tensor_tensor_scan(out, d0, d1, init, op0, op1), reads, writes)

    def RECIP(self, out, in_, reads=(), writes=()):
        self.c.op("dve", lambda e: e.reciprocal(out, in_), reads, writes)

    def MEMSET(self, eng, ap, val, writes=()):
        self.c.op(eng, lambda e: e.memset(ap, val), (), writes)

    def dump(self, name, tile, ap, p, n, q="sp"):
        if not self.cfg.dbg or self.dry():
            return
        off = self.dbg_off
        self.dbg_aps[name] = (off, p, n)
        self.dbg_off += n
        self.c.dma(q, self.dbgt[0:p, off:off + n], ap, reads=[tile])


class DryCtx:
    def op(self, *a, **k):
        pass

    def dma(self, *a, **k):
        return None


class Grp:
    def __init__(self, kind, T, tok0, first, last, idx):
        self.kind, self.T, self.tok0, self.first, self.last, self.idx = kind, T, tok0, first, last, idx
        self.nt = T // 128


def _emit(self):
    cfg = self.cfg
    self.n_dense = self.n_small = self.n_o = 0
    self.setup()
    groups = []
    for i in range(cfg.n_pg):
        groups.append(Grp("p", cfg.TMAX, i * cfg.TMAX, i == 0, i == cfg.n_pg - 1, i))
    if cfg.sample:
        groups.append(Grp("s", 128, 0, True, True, cfg.n_pg))
    for g in groups:
        if g.kind == "s":
            allst = [t for row in self.SA + self.SC for (t, _) in row]
            self.MEMSET("dve", self.arena[:, 0:1], 0.0, writes=allst + self.SsF + self.SsB)
        self.load_x(g)
        for l in range(cfg.depth):
            hasA, hasB, hasC = ("A" in cfg.parts), ("B" in cfg.parts), ("C" in cfg.parts)
            if hasA:
                self.branch_A(g, l)
            gB = self.branch_B(g, l) if hasB else None
            if gB is not None:
                next(gB)
            if hasA:
                self.rms_finish(g, l, 0)
            if gB is not None:
                for _ in gB:
                    pass
            gC = self.branch_C(g, l) if hasC else None
            if gC is not None:
                next(gC)
            if hasB:
                self.rms_finish(g, l, 1, scale_row=self._b_aden)
            if gC is not None:
                for _ in gC:
                    pass
            gM = self.merge(g, l) if "M" in cfg.parts else None
            if gM is not None:
                next(gM)
            if gC is not None:
                keep = self.dense_pool
                self.dense_pool = self.ps_dense
                self.rms_finish(g, l, 2)
                self.dense_pool = keep
            if gM is not None:
                for _ in gM:
                    pass
            if "F" in cfg.parts:
                self.ffn(g, l)
        self.store_y(g)


def _setup(self):
    c, cfg, d = self.c, self.cfg, self.d
    L = cfg.depth
    cst, cstb = self.cst, self.cstb
    c.dma("sp", cst[:], d["consts"], writes=[cst])
    o1, _ = CL["onecol"]
    o2, _ = CL["ones"]
    self.CP("dve", cstb[:, 0:16], cst[:, o1:o1 + 16], reads=[cst], writes=[cstb])
    self.CP("dve", cstb[:, 16:144], cst[:, o2:o2 + 128], reads=[cst], writes=[cstb])
    for i, nm in enumerate(("ident", "nident", "posP_incl", "posS_incl", "posP_strict_ts", "posS_strict_ts")):
        o3, _ = CL[nm]
        self.CP("dve", cstb[:, 144 + i * 128:144 + (i + 1) * 128], cst[:, o3:o3 + 128], reads=[cst], writes=[cstb])
    e = self.T32("su_e", 4 * L)
    ev = e[:, 0:4 * L].rearrange("p (l h) -> p l h", l=L)
    c.dma("sp", ev, d["lb_logits"].rearrange("l (h k) -> k l h", k=128), writes=[e], allow_slow_non_contiguous=True)
    self.ACT(e[:, 0:4 * L], e[:, 0:4 * L], AF.Exp, reads=[e], writes=[e])
    s = self.T32("su_s", 4)
    self.CP("dve", s[:, 0:4], ev[:, 0, :], reads=[e], writes=[s])
    for l in range(1, L):
        self.TT("dve", s[:, 0:4], s[:, 0:4], ev[:, l, :], ALU.add, reads=[s, e], writes=[s])
    self.RECIP(s[:, 0:4], s[:, 0:4], reads=[s], writes=[s])
    lbt, omlt, nomlt = self.lbt, self.omlt, self.nomlt
    self.MEMSET("dve", lbt[:, 0, :], 0.0, writes=[lbt])
    for l in range(1, L):
        w = self.T32("su_w", 4)
        self.TT("dve", w[:, 0:4], ev[:, l, :], s[:, 0:4], ALU.mult, reads=[e, s], writes=[w])
        self.TT("dve", lbt[:, l, :], lbt[:, l - 1, :], w[:, 0:4], ALU.add, reads=[lbt, w], writes=[lbt])
    self.TS("dve", omlt[:].rearrange("p l h -> p (l h)"), lbt[:].rearrange("p l h -> p (l h)"), -1.0, 1.0, ALU.mult, ALU.add, reads=[lbt], writes=[omlt])
    self.TS("dve", nomlt[:].rearrange("p l h -> p (l h)"), omlt[:].rearrange("p l h -> p (l h)"), -1.0, None, ALU.mult, reads=[omlt], writes=[nomlt])
    for t, nm in ((self.gA, "a_norm_g"), (self.gB, "b_norm_g"), (self.gC, "c_norm_g")):
        c.dma("sp", t[:], d[nm].rearrange("l (h k) -> k l h", k=128), writes=[t], allow_slow_non_contiguous=True)
    for l in range(L):
        for j in range(4):
            c.dma("sp", self.cw[:, l, :, j], d["conv_w"][l, j].rearrange("(c p) -> p c", p=128), writes=[self.cw], allow_slow_non_contiguous=True)
    pr = self.prow
    for i, nm in enumerate(("b_mi", "b_mf", "dt_bias", "a_log")):
        c.dma("sp", pr[0:4, i * L:(i + 1) * L], d[nm].rearrange("l h -> h l"), writes=[pr], allow_slow_non_contiguous=True)
    self.TS("dve", pr[0:4, L:2 * L], pr[0:4, L:2 * L], -1.0, None, ALU.mult, reads=[pr], writes=[pr])
    self.ACT(pr[0:4, 3 * L:4 * L], pr[0:4, 3 * L:4 * L], AF.Exp, reads=[pr], writes=[pr])
    self.TS("dve", pr[0:4, 3 * L:4 * L], pr[0:4, 3 * L:4 * L], -1.0, None, ALU.mult, reads=[pr], writes=[pr])
    if not self.dry():
        self.ws.emit_conversions()
    for l in range(L):
        for h in range(4):
            for grp in (self.SA, self.SC):
                f, b = grp[l][h]
                self.MEMSET("dve", f[:], 0.0, writes=[f])
                self.MEMSET("dve", b[:], 0.0, writes=[b])
            f, b, n = self.SB[l][h]
            self.MEMSET("dve", f[:], 0.0, writes=[f])
            self.MEMSET("dve", b[:], 0.0, writes=[b])
            self.MEMSET("dve", n[:], 0.0, writes=[n])
        self.MEMSET("dve", self.carryB[l][:], 0.0, writes=[self.carryB[l]])
        self.MEMSET("dve", self.hist[l][:], 0.0, writes=[self.hist[l]])


def _load_x(self, g):
    c, d = self.c, self.d
    src = d["xp"] if g.kind == "p" else d["xs"]
    for j in range(g.nt):
        c.dma("sp", self.h_tok[:, j, :], src[g.tok0 + j * 128:g.tok0 + (j + 1) * 128, :], writes=[self.h_tok])
    self.make_hT(g)


def _make_hT(self, g):
    ident = self.cs("ident")
    for j in range(g.nt):
        for half in range(2):
            bank = self.dense_bank()
            for q in range(4):
                k = half * 4 + q
                self.TR(bank[:, q * 128:(q + 1) * 128], self.h_tok[:, j, k * 128:(k + 1) * 128], ident,
                        reads=[self.h_tok, self.cst], writes=[bank])
            self.CP("act", self.hT[:, half * 4:(half + 1) * 4, j * 128:(j + 1) * 128],
                    bank[:].rearrange("p (a b) -> p a b", a=4), reads=[bank], writes=[self.hT])


def _store_y(self, g):
    c = self.c
    dst = self.o["yp"] if g.kind == "p" else self.o["ys"]
    for j in range(g.nt):
        c.dma("sp", dst[g.tok0 + j * 128:g.tok0 + (j + 1) * 128, :], self.h_tok[:, j, :], reads=[self.h_tok])


def _proj_fm(self, wv, ws, c0, ncol, T, bank=None):
    bank = bank or self.dense_bank()
    for k in range(8):
        self.MM(bank[0:ncol, 0:T], wv[:, k, c0:c0 + ncol], self.hT[:, k, 0:T], start=(k == 0), stop=(k == 7),
                reads=[ws, self.hT], writes=[bank])
    return bank


def _win(self, l, c0, n):
    src = self.d["w_in"][l].rearrange("(k p) n -> p k n", p=128)[:, :, c0:c0 + n]
    return self.ws.get(("w_in", l, c0, n), src, 8, n)


def _proj_tm(self, g, dst, dres, l, c0, ncols):
    for b0 in range(0, ncols, 256):
        wv, ws = self.win(l, c0 + b0, 256)
        for j in range(g.nt):
            bank = self.dense_bank()
            for k in range(8):
                self.MM(bank[:, 0:256], self.hT[:, k, j * 128:(j + 1) * 128], wv[:, k, :], start=(k == 0), stop=(k == 7),
                        reads=[ws, self.hT], writes=[bank])
            self.CP("act", dst[:, j, b0:b0 + 256], bank[:, 0:256], reads=[bank], writes=[dres])


Builder.emit = _emit
Builder.setup = _setup
Builder.load_x = _load_x
Builder.make_hT = _make_hT
Builder.store_y = _store_y
Builder.proj_fm = _proj_fm
Builder.win = _win
Builder.proj_tm = _proj_tm


def _rms_gate(self, g, l, br, h, obank, gs, gtile, first, last):
    T = g.T
    osq = self.T16(f"osq{h % 2}")
    self.ACT(osq[:, 0:T], obank[:, 0:T], AF.Square, reads=[obank], writes=[osq])
    rows = self.ps_rows[1]
    self.MM(rows[0:4, 0:T], self.cs("onecol", bf=True, c0=4 * h, n=4), osq[:, 0:T], start=first, stop=last,
            reads=[osq, self.cstb], writes=[rows])
    t1 = self.T32(f"t1_{h}")
    self.STT(t1[:, 0:T], obank[:, 0:T], gtile[:, l, h:h + 1], gs[:, 0:T], ALU.mult, ALU.mult,
             reads=[obank, gtile, gs], writes=[t1])


def _rms_finish(self, g, l, n, scale_row=None):
    T = g.T
    rows = self.ps_rows[1]
    r = self.R32("rms_r")
    if scale_row is None:
        self.ACT(r[0:4, 0:T], rows[0:4, 0:T], AF.Ln, reads=[rows], writes=[r], scale=1.0 / 128, bias=NORM_EPS)
        self.ACT(r[0:4, 0:T], r[0:4, 0:T], AF.Exp, reads=[r], writes=[r], scale=-0.5)
    else:
        t = self.R32("rms_t")
        self.TT("dve", t[0:4, 0:T], rows[0:4, 0:T], scale_row[0:4, 0:T], ALU.mult, reads=[rows, scale_row], writes=[t])
        self.TT("dve", t[0:4, 0:T], t[0:4, 0:T], scale_row[0:4, 0:T], ALU.mult, reads=[t, scale_row], writes=[t])
        self.ACT(r[0:4, 0:T], t[0:4, 0:T], AF.Ln, reads=[t], writes=[r], scale=1.0 / 128, bias=NORM_EPS)
        self.ACT(r[0:4, 0:T], r[0:4, 0:T], AF.Exp, reads=[r], writes=[r], scale=-0.5)
        self.TT("dve", r[0:4, 0:T], r[0:4, 0:T], scale_row[0:4, 0:T], ALU.mult, reads=[r, scale_row], writes=[r])
    for h in range(4):
        bank = self.dense_bank()
        self.MM(bank[:, 0:T], self.cs("sel", rows=4, c0=h * 128, n=128), r[0:4, 0:T], reads=[self.cst, r], writes=[bank])
        t1 = self.T32(f"t1_{h}")
        self.TT("dve", self.yT[n][:, h, 0:T], t1[:, 0:T], bank[:, 0:T], ALU.mult, reads=[t1, bank], writes=[self.yT[n]])
        self.dump(f"y{n}_{h}_l{l}_g{g.idx}", self.yT[n], self.yT[n][:, h, 0:T], 128, T, q="pool")


def _branch_A(self, g, l):
    c, cfg, d = self.c, self.cfg, self.d
    T, nt = g.T, g.nt
    samp = g.kind == "s"
    seglen = 8 if samp else 64
    spt = 128 // seglen
    nseg = T // seglen
    maskA = self.cs("mask8" if samp else "maskA64")
    rst = self.cs("rst8" if samp else "rst64", n=T)
    ident = self.cs("ident")
    rm = self.cs("rm16" if samp else "rm2")
    vA = self.T16("vtok", 512 * cfg.tg)
    vA3 = vA[:, :].rearrange("p (j n) -> p j n", n=512)
    self.proj_tm(g, vA3, vA, l, 1024, 512)
    for p in range(2):
        wq, wqs = self.win(l, 0 + p * 256, 256)
        wg, wgs = self.win(l, 1536 + p * 256, 256)
        wf, wfs = self.win(l, 512 + p * 256, 256)
        def head(hh):
            h = 2 * p + hh
            c0 = hh * 128
            hs = [0]

            def hsmall():
                pool = (self.ps_small[hh], self.ps_dense[hh])
                b_ = pool[hs[0] % 2]
                hs[0] += 1
                return b_[:, 0:128], b_
            if samp:
                c.dma("sp", self.SsF[hh][:, :, 0:128], d["st_a"][l, :, h].rearrange("s k v -> k s v"), writes=[self.SsF[hh]])
                self.CP("act", self.SsB[hh][:, :, :], self.SsF[hh][:, :, 0:128], reads=[self.SsF[hh]], writes=[self.SsB[hh]])
                yield
            psq = self.proj_fm(wq, wqs, c0, 128, T, bank=self.ps_dense[hh])
            qs = self.W(0, hh)
            self.ACT(qs[:, 0:T], psq[:, 0:T], AF.Silu, reads=[psq], writes=[qs])
            yield
            psg = self.proj_fm(wg, wgs, c0, 128, T, bank=self.ps_dense[hh])
            gs = self.W(7, hh)
            self.ACT(gs[:, 0:T], psg[:, 0:T], AF.Silu, reads=[psg], writes=[gs])
            yield
            psf = self.proj_fm(wf, wfs, c0, 128, T, bank=self.ps_dense[hh])
            sig = self.W(1, hh)
            self.ACT(sig[:, 0:T], psf[:, 0:T], AF.Sigmoid, reads=[psf], writes=[sig])
            yield
            f = self.W(2, hh)
            self.TS("dve", f[:, 0:T], sig[:, 0:T], self.omlt[:, l, h:h + 1], self.lbt[:, l, h:h + 1], ALU.mult, ALU.add,
                    reads=[sig, self.omlt, self.lbt], writes=[f])
            kk = self.W(3, hh)
            self.TS("dve", kk[:, 0:T], sig[:, 0:T], self.nomlt[:, l, h:h + 1], self.omlt[:, l, h:h + 1], ALU.mult, ALU.add,
                    reads=[sig, self.omlt, self.nomlt], writes=[kk])
            self.ACT(f[:, 0:T], f[:, 0:T], AF.Ln, reads=[f], writes=[f])
            yield
            b = self.W(4, hh)
            self.SCAN(b[:, 0:T], rst, f[:, 0:T], 0.0, ALU.mult, ALU.add, reads=[self.cst, f], writes=[b])
            yield
            E1 = sig
            self.ACT(E1[:, 0:T], b[:, 0:T], AF.Exp, reads=[b], writes=[E1])
            yield
            En = f
            self.ACT(En[:, 0:T], b[:, 0:T], AF.Exp, reads=[b], writes=[En], scale=-1.0)
            yield
            q2 = self.V(0, hh)
            self.TT("dve", q2[:, 0:T], qs[:, 0:T], E1[:, 0:T], ALU.mult, reads=[qs, E1], writes=[q2])
            yield
            kt = self.V(1, hh)
            self.TT("dve", kt[:, 0:T], kk[:, 0:T], En[:, 0:T], ALU.mult, reads=[kk, En], writes=[kt])
            yield
            b3 = b[:, 0:T].rearrange("p (s c) -> p s c", c=seglen)
            bl = b[:, 0:T].rearrange("p (s c) -> p s c", c=seglen)[:, :, seglen - 1:seglen]
            dd = self.W(5, hh)
            dd3 = dd[:, 0:T].rearrange("p (s c) -> p s c", c=seglen)
            self.TT("dve", dd3, b3, bl.to_broadcast([128, nseg, seglen]), ALU.subtract, reads=[b], writes=[dd])
            yield
            self.ACT(dd[:, 0:T], dd[:, 0:T], AF.Exp, reads=[dd], writes=[dd], scale=-1.0)
            yield
            khT = self.W(6, hh)
            self.TT("dve", khT[:, 0:T], kk[:, 0:T], dd[:, 0:T], ALU.mult, reads=[kk, dd], writes=[khT])
            yield
            ob = self.ps_o[hh]
            for j in range(nt):
                jc = slice(j * 128, (j + 1) * 128)
                pt, ptr = hsmall()
                self.TR(pt, khT[:, jc], ident, reads=[khT, self.cst], writes=[ptr])
                khs = self.T16(f"khsb_{hh}", 128)
                self.CP("dve", khs[:, :], pt, reads=[ptr], writes=[khs])
                yield
                pa, par = hsmall()
                self.MM(pa, kt[:, jc], q2[:, jc], reads=[kt, q2], writes=[par])
                attm = self.T16(f"c_attm_{hh}", 128)
                self.TT("dve", attm[:, :], pa, maskA, ALU.mult, reads=[par, self.cst], writes=[attm])
                yield
                vh = vA3[:, j, h * 128:(h + 1) * 128]
                self.MM(ob[:, jc], vh, attm[:, :], start=True, stop=False, reads=[vA, attm], writes=[ob])
                for i in range(spt):
                    seg = j * spt + i
                    sc = slice(seg * seglen, (seg + 1) * seglen)
                    if samp:
                        Sf, Sb, Sfr, Sbr = self.SsF[hh][:, seg, 0:128], self.SsB[hh][:, seg, :], self.SsF[hh], self.SsB[hh]
                    else:
                        Sft, Sbt = self.SA[l][h]
                        Sf, Sb, Sfr, Sbr = Sft[:, :], Sbt[:, :], Sft, Sbt
                    self.MM(ob[:, sc], Sb, q2[:, sc], start=False, stop=(i == spt - 1), reads=[Sbr, q2], writes=[ob])
                    khm = self.T16(f"khseg{i % 2}_{hh}", 128)
                    self.TS("dve", khm[:, :], khs[:, :], rm[:, i:i + 1], None, ALU.mult, reads=[khs, self.cst], writes=[khm])
                    pS, pSr = hsmall()
                    self.MM(pS, khm[:, :], vh, reads=[khm, vA], writes=[pSr])
                    self.STT(Sf, Sf, E1[:, (seg + 1) * seglen - 1:(seg + 1) * seglen], pS, ALU.mult, ALU.add,
                             reads=[Sfr, E1, pSr], writes=[Sfr])
                    if not samp:
                        self.CP("act", Sb, Sf, reads=[Sfr], writes=[Sbr])
                        yield
            self.rms_gate(g, l, 0, h, ob, gs, self.gA, first=(h == 0), last=(h == 3))
            if samp:
                c.dma("sp", self.o["oa_s"][l, :, h].rearrange("s k v -> k s v"), self.SsF[hh][:, :, 0:128], reads=[self.SsF[hh]])
            elif g.last:
                c.dma("sp", self.o["oa_p"][l, h], self.SA[l][h][0][:, :], reads=[self.SA[l][h][0]])
        self.interleave([head(0), head(1)])


Builder.rms_gate = _rms_gate
Builder.rms_finish = _rms_finish
Builder.branch_A = _branch_A


def _branch_B(self, g, l):
    c, cfg, d = self.c, self.cfg, self.d
    L = cfg.depth
    T, nt = g.T, g.nt
    samp = g.kind == "s"
    seglen = 8 if samp else 128
    spt = 128 // seglen
    nseg = T // seglen
    ident = self.cs("ident")
    pos = self.cs("posS_incl" if samp else "posP_incl")
    rm = self.cs("rm16")
    i4 = self.cs("i4", rows=4)
    pr = self.prow
    w4, w4s = self.win(l, 3584, 8)
    bank = self.dense_bank()
    for k in range(8):
        self.MM(bank[0:4, 0:T], w4[:, k, 0:4], self.hT[:, k, 0:T], start=(k == 0), stop=(k == 7), reads=[w4s, self.hT], writes=[bank])
    bi = self.R32("b_bi")
    self.ACT(bi[0:4, 0:T], bank[0:4, 0:T], AF.Identity, reads=[bank, pr], writes=[bi], bias=pr[0:4, l:l + 1])
    bank = self.dense_bank()
    for k in range(8):
        self.MM(bank[0:4, 0:T], w4[:, k, 4:8], self.hT[:, k, 0:T], start=(k == 0), stop=(k == 7), reads=[w4s, self.hT], writes=[bank])
    sp = self.R32("b_sp")
    self.ACT(sp[0:4, 0:T], bank[0:4, 0:T], AF.Exp, reads=[bank, pr], writes=[sp], scale=-1.0, bias=pr[0:4, L + l:L + l + 1])
    self.ACT(sp[0:4, 0:T], sp[0:4, 0:T], AF.Ln, reads=[sp], writes=[sp], bias=1.0)
    Bn = self.R32("b_Bn")
    cb = self.carryB[l]
    if samp:
        self.SCAN(Bn[0:4, 0:T], self.cs("rst8", rows=4, n=T), sp[0:4, 0:T], 0.0, ALU.mult, ALU.add, reads=[self.cst, sp], writes=[Bn])
    else:
        self.SCAN(Bn[0:4, 0:T], self.cs("ones", rows=4, n=T), sp[0:4, 0:T], cb[0:4, 0:1], ALU.mult, ALU.add,
                  reads=[self.cst, sp, cb], writes=[Bn])
    u = self.R32("b_u")
    self.TT("dve", u[0:4, 0:T], bi[0:4, 0:T], Bn[0:4, 0:T], ALU.add, reads=[bi, Bn], writes=[u])
    mu = self.R32("b_mu")
    si = self.R32("b_si", 16)
    if samp:
        c.dma("sp", self.m0row[0:4, 0:16], d["st_m"][l].rearrange("s h -> h s"), writes=[self.m0row], allow_slow_non_contiguous=True)
        for s in range(16):
            sl = slice(s * 8, (s + 1) * 8)
            self.SCAN(mu[0:4, sl], u[0:4, sl], u[0:4, sl], self.m0row[0:4, s:s + 1], ALU.max, ALU.max, reads=[u, self.m0row], writes=[mu])
        self.CP("dve", si[0:4, 0:16], self.m0row[0:4, 0:16], reads=[self.m0row], writes=[si])
    else:
        self.SCAN(mu[0:4, 0:T], u[0:4, 0:T], u[0:4, 0:T], cb[0:4, 1:2], ALU.max, ALU.max, reads=[u, cb], writes=[mu])
        self.CP("dve", si[0:4, 0:1], cb[0:4, 1:2], reads=[cb], writes=[si])
        for j in range(1, nt):
            self.CP("dve", si[0:4, j:j + 1], mu[0:4, j * 128 - 1:j * 128], reads=[mu], writes=[si])
    mu3 = mu[0:4, 0:T].rearrange("p (s c) -> p s c", c=seglen)
    u3 = u[0:4, 0:T].rearrange("p (s c) -> p s c", c=seglen)
    wint = self.R32("b_wint")
    wint3 = wint[0:4, 0:T].rearrange("p (s c) -> p s c", c=seglen)
    self.TT("dve", wint3, mu3, si[0:4, 0:nseg].unsqueeze(2).to_broadcast([4, nseg, seglen]), ALU.subtract, reads=[mu, si], writes=[wint])
    self.ACT(wint[0:4, 0:T], wint[0:4, 0:T], AF.Exp, reads=[wint], writes=[wint], scale=-1.0)
    mt = self.R32("b_mt")
    self.TT("dve", mt[0:4, 0:T], mu[0:4, 0:T], Bn[0:4, 0:T], ALU.subtract, reads=[mu, Bn], writes=[mt])
    emt = self.R32("b_emt")
    self.ACT(emt[0:4, 0:T], mt[0:4, 0:T], AF.Exp, reads=[mt], writes=[emt], scale=-1.0)
    muend = mu3[:, :, seglen - 1:seglen]
    wk = self.R32("b_wk")
    wk3 = wk[0:4, 0:T].rearrange("p (s c) -> p s c", c=seglen)
    self.TT("dve", wk3, u3, muend.to_broadcast([4, nseg, seglen]), ALU.subtract, reads=[u, mu], writes=[wk])
    self.ACT(wk[0:4, 0:T], wk[0:4, 0:T], AF.Exp, reads=[wk], writes=[wk])
    wold = self.R32("b_wold", 16)
    self.TT("dve", wold[0:4, 0:nseg].unsqueeze(2), si[0:4, 0:nseg].unsqueeze(2), muend, ALU.subtract, reads=[si, mu], writes=[wold])
    self.ACT(wold[0:4, 0:nseg], wold[0:4, 0:nseg], AF.Exp, reads=[wold], writes=[wold])
    if samp:
        c.dma("sp", self.o["om_s"][l].rearrange("s h -> h s"), mt[0:4, 0:T].rearrange("p (s c) -> p s c", c=8)[:, :, 7],
              reads=[mt], allow_slow_non_contiguous=True)
    else:
        if g.last:
            c.dma("sp", self.o["om_p"][l].rearrange("(h o) -> h o", o=1), mt[0:4, T - 1:T], reads=[mt])
        self.CP("dve", cb[0:4, 0:1], Bn[0:4, T - 1:T], reads=[Bn], writes=[cb])
        self.CP("dve", cb[0:4, 1:2], mu[0:4, T - 1:T], reads=[mu], writes=[cb])
    vB = self.T16("vtok", 512 * cfg.tg)
    vB3 = vB[:, :].rearrange("p (j n) -> p j n", n=512)
    self.proj_tm(g, vB3, vB, l, 2560, 512)
    kB = self.T16("b_ktok", 256 * cfg.tg)
    kB3 = kB[:, :].rearrange("p (j n) -> p j n", n=256)
    self.proj_tm(g, kB3, kB, l, 2304, 256)
    cols = self.T32("b_cols", 8 * cfg.tg)
    for j in range(nt):
        jc = slice(j * 128, (j + 1) * 128)
        pc, pcr = self.small()
        self.MM(pc[:, 0:4], u[0:4, jc], i4, reads=[u, self.cst], writes=[pcr])
        self.MM(pc[:, 4:8], wk[0:4, jc], i4, reads=[wk, self.cst], writes=[pcr])
        self.CP("dve", cols[:, j * 8:(j + 1) * 8], pc[:, 0:8], reads=[pcr], writes=[cols])
    wq, wqs = self.win(l, 2048, 256)
    wk_, wks = self.win(l, 2304, 256)
    rowsA = self.ps_rows[0]
    yield "pre"
    for p in range(2):
        wo, wos = self.win(l, 3072 + p * 256, 256)

        def head(hh):
            h = 2 * p + hh
            hs = [0]

            def hsmall():
                pool = (self.ps_small[hh], self.ps_dense[hh])
                b_ = pool[hs[0] % 2]
                hs[0] += 1
                return b_[:, 0:128], b_
            if samp:
                for s4 in range(4):
                    cld = self.T32(f"b_cld_{hh}", 4 * 64)
                    cld3 = cld[:, :].rearrange("p (s k) -> p s k", k=64)
                    c.dma("sp", cld3, d["st_c"][l, s4 * 4:(s4 + 1) * 4, h].rearrange("s v k -> v s k"), writes=[cld])
                    bk_ = self.ps_dense[hh]
                    for q in range(4):
                        self.TR(bk_[0:64, q * 128:(q + 1) * 128], cld3[:, q, :], ident, reads=[cld, self.cst], writes=[bk_])
                    self.CP("act", self.SsF[hh][0:64, s4 * 4:(s4 + 1) * 4, 0:128], bk_[0:64, :].rearrange("p (a b) -> p a b", a=4), reads=[bk_], writes=[self.SsF[hh]])
                    yield
                nld = self.T32(f"b_nld_{hh}", 64, p=16)
                c.dma("sp", nld[0:16, 0:64], d["st_n"][l, :, h, :], writes=[nld])
                bk_ = self.ps_dense[hh]
                self.TR(bk_[0:64, 0:16], nld[0:16, 0:64], ident[0:16, 0:16], reads=[nld, self.cst], writes=[bk_])
                self.CP("dve", self.SsF[hh][0:64, :, 128], bk_[0:64, 0:16], reads=[bk_], writes=[self.SsF[hh]])
                yield
                self.CP("act", self.SsB[hh][0:64, :, :], self.SsF[hh][0:64, :, 0:128], reads=[self.SsF[hh]], writes=[self.SsB[hh]])
                yield
                self.MEMSET("dve", self.SsN[hh][:], 0.0, writes=[self.SsN[hh]])
                self.CP("dve", self.SsN[hh][0:64, :, h], self.SsF[hh][0:64, :, 128], reads=[self.SsF[hh]], writes=[self.SsN[hh]])
                yield
            psq = self.proj_fm(wq, wqs, h * 64, 64, T, bank=self.ps_dense[hh])
            qT = self.T16(f"b_qT_{hh}", p=64)
            self.ACT(qT[0:64, 0:T], psq[0:64, 0:T], AF.Copy, reads=[psq], writes=[qT], scale=0.125)
            yield
            psk = self.proj_fm(wk_, wks, h * 64, 64, T, bank=self.ps_dense[hh])
            kT = self.T16(f"b_kT_{hh}", p=64)
            self.CP("act", kT[0:64, 0:T], psk[0:64, 0:T], reads=[psk], writes=[kT])
            yield
            bcb = self.ps_dense[hh]
            self.MM(bcb[:, 0:T], self.cs("sel", rows=4, c0=h * 128, n=128), wint[0:4, 0:T], reads=[self.cst, wint], writes=[bcb])
            qp = self.T16(f"b_qp_{hh}", p=64)
            self.TT("dve", qp[0:64, 0:T], qT[0:64, 0:T], bcb[0:64, 0:T], ALU.mult, reads=[qT, bcb], writes=[qp])
            yield
            bcw = self.ps_dense[hh]
            self.MM(bcw[:, 0:nseg], self.cs("sel", rows=4, c0=h * 128, n=128), wold[0:4, 0:nseg], reads=[self.cst, wold], writes=[bcw])
            woldbc = self.T32(f"b_woldbc_{hh}", 16)
            self.CP("dve", woldbc[:, 0:nseg], bcw[:, 0:nseg], reads=[bcw], writes=[woldbc])
            yield
            pso = self.proj_fm(wo, wos, (h % 2) * 128, 128, T, bank=self.ps_dense[hh])
            gs = self.W(7, hh)
            self.ACT(gs[:, 0:T], pso[:, 0:T], AF.Sigmoid, reads=[pso], writes=[gs])
            yield
            ob = self.ps_o[hh]
            for j in range(nt):
                jc = slice(j * 128, (j + 1) * 128)
                X, Xr = (self.ps_small[hh][:, 0:128], self.ps_small[hh])
                self.MM(X, self.cs("sel", rows=4, c0=h * 128, n=128), mu[0:4, jc], start=True, stop=False, reads=[self.cst, mu], writes=[Xr])
                self.MM(X, self.cs("ident", bf=True), self.cs("posS_incl" if samp else "posP_incl", bf=True), start=False, stop=True, reads=[self.cstb], writes=[Xr])
                E = self.T32(f"b_E_{hh}", 128)
                self.ACT(E[:, :], X, AF.Exp, reads=[Xr, cols], writes=[E], scale=-1.0, bias=cols[:, j * 8 + h:j * 8 + h + 1])
                yield
                KQ, KQr = (self.ps_small[hh][:, 0:128], self.ps_small[hh])
                self.MM(KQ, kT[0:64, jc], qT[0:64, jc], reads=[kT, qT], writes=[KQr])
                scm = self.T16(f"b_scm_{hh}", 128)
                self.TT("dve", scm[:, :], KQ, E[:, :], ALU.mult, reads=[KQr, E], writes=[scm])
                yield
                vh = vB3[:, j, h * 128:(h + 1) * 128]
                self.MM(ob[:, jc], vh, scm[:, :], start=True, stop=False, reads=[vB, scm], writes=[ob])
                self.MM(rowsA[0:4, jc], self.cs("onecol", bf=True, c0=4 * h, n=4), scm[:, :], start=(h == 0 and j == 0), stop=False,
                        reads=[self.cstb, scm], writes=[rowsA])
                wkv = self.T16(f"b_wkv_{hh}", 132)
                self.TS("dve", wkv[:, 0:128], vh, cols[:, j * 8 + 4 + h:j * 8 + 5 + h], None, ALU.mult, reads=[vB, cols], writes=[wkv])
                yield
                self.CP("dve", wkv[:, 128:129], cols[:, j * 8 + 4 + h:j * 8 + 5 + h], reads=[cols], writes=[wkv])
                yield
                for i in range(spt):
                    seg = j * spt + i
                    sc = slice(seg * seglen, (seg + 1) * seglen)
                    if samp:
                        Cf, Cb, Cn = self.SsF[hh][0:64, seg, :], self.SsB[hh][0:64, seg, :], self.SsN[hh][0:64, seg, :]
                        Cfr, Cbr, Cnr = self.SsF[hh], self.SsB[hh], self.SsN[hh]
                    else:
                        Cft, Cbt, Cnt = self.SB[l][h]
                        Cf, Cb, Cn, Cfr, Cbr, Cnr = Cft[:, :], Cbt[:, :], Cnt[:, :], Cft, Cbt, Cnt
                    last = (i == spt - 1)
                    self.MM(ob[:, sc], Cb, qp[0:64, sc], start=False, stop=last, reads=[Cbr, qp], writes=[ob])
                    self.MM(rowsA[0:4, sc], Cn, qp[0:64, sc], start=False, stop=(last and h == 3 and j == nt - 1), reads=[Cnr, qp], writes=[rowsA])
                    if spt > 1:
                        kkm = self.T16(f"b_kkm_{hh}", 64)
                        self.TS("dve", kkm[:, :], kB3[:, j, h * 64:(h + 1) * 64], rm[:, i:i + 1], None, ALU.mult, reads=[kB, self.cst], writes=[kkm])
                        yield
                        kk_ap, kk_r = kkm[:, :], kkm
                    else:
                        kk_ap, kk_r = kB3[:, j, h * 64:(h + 1) * 64], kB
                    pS = self.ps_dense[hh]
                    self.MM(pS[0:64, 0:129], kk_ap, wkv[:, 0:129], reads=[kk_r, wkv], writes=[pS])
                    self.STT(Cf, Cf, woldbc[0:64, seg:seg + 1], pS[0:64, 0:129], ALU.mult, ALU.add, reads=[Cfr, woldbc, pS], writes=[Cfr])
                    yield
                    if not samp:
                        self.CP("act", Cb, Cf[:, 0:128], reads=[Cfr], writes=[Cbr])
                        yield
                        self.CP("act", Cn[:, h:h + 1], Cf[:, 128:129], reads=[Cfr], writes=[Cnr])
                        yield
            self.rms_gate(g, l, 1, h, ob, gs, self.gB, first=(h == 0), last=(h == 3))
            if samp:
                for s4 in range(4):
                    cld = self.T32(f"b_cld_{hh}", 4 * 64)
                    cld3 = cld[:, :].rearrange("p (s k) -> p s k", k=64)
                    bk_ = self.ps_dense[hh]
                    for q in range(4):
                        self.TR(bk_[:, q * 64:(q + 1) * 64], self.SsF[hh][0:64, s4 * 4 + q, 0:128], ident[0:64, 0:64], reads=[self.SsF[hh], self.cst], writes=[bk_])
                    self.CP("act", cld3, bk_[:, 0:256].rearrange("p (a b) -> p a b", a=4), reads=[bk_], writes=[cld])
                    yield
                    c.dma("sp", self.o["oc_s"][l, s4 * 4:(s4 + 1) * 4, h].rearrange("s v k -> v s k"), cld3, reads=[cld])
                nst_ = self.T32(f"b_nst_{hh}", 16, p=64)
                self.CP("dve", nst_[0:64, 0:16], self.SsF[hh][0:64, :, 128], reads=[self.SsF[hh]], writes=[nst_])
                yield
                bk_ = self.ps_dense[hh]
                self.TR(bk_[0:16, 0:64], nst_[0:64, 0:16], ident[0:64, 0:64], reads=[nst_, self.cst], writes=[bk_])
                nld = self.T32(f"b_nld_{hh}", 64, p=16)
                self.CP("dve", nld[0:16, 0:64], bk_[0:16, 0:64], reads=[bk_], writes=[nld])
                yield
                c.dma("sp", self.o["on_s"][l, :, h, :], nld[0:16, 0:64], reads=[nld])
            elif g.last:
                Cft = self.SB[l][h][0]
                bk_ = self.ps_dense[hh]
                self.TR(bk_[:, 0:64], Cft[0:64, 0:128], ident[0:64, 0:64], reads=[Cft, self.cst], writes=[bk_])
                co = self.T32(f"b_co_{hh}", 64)
                self.CP("act", co[:, 0:64], bk_[:, 0:64], reads=[bk_], writes=[co])
                yield
                c.dma("sp", self.o["oc_p"][l, h], co[:, 0:64], reads=[co])
                c.dma("sp", self.o["on_p"][l, h].rearrange("(k o) -> k o", o=1), Cft[0:64, 128:129], reads=[Cft])
        self.interleave([head(0), head(1)])
    aden = self.R32("b_aden")
    self.ACT(aden[0:4, 0:T], rowsA[0:4, 0:T], AF.Abs, reads=[rowsA], writes=[aden])
    self.TT("dve", aden[0:4, 0:T], aden[0:4, 0:T], emt[0:4, 0:T], ALU.max, reads=[aden, emt], writes=[aden])
    self.RECIP(aden[0:4, 0:T], aden[0:4, 0:T], reads=[aden], writes=[aden])
    self._b_aden = aden


def _ones_row(self, T):
    return self.cs("ones", rows=4, n=128) if T <= 128 else self.onesrow[0:4, 0:T]


Builder.branch_B = _branch_B
Builder.ones_row = _ones_row


def _branch_C(self, g, l):
    c, cfg, d = self.c, self.cfg, self.d
    L = cfg.depth
    T, nt = g.T, g.nt
    samp = g.kind == "s"
    seglen = 8 if samp else 128
    spt = 128 // seglen
    nseg = T // seglen
    nsteps = 2 if samp else 6
    ident = self.cs("ident")
    nident = self.cs("nident")
    pos_strict = self.cs("posS_strict_ts" if samp else "posP_strict_ts")
    pos_incl = self.cs("posS_incl" if samp else "posP_incl")
    rm = self.cs("rm16")
    i4 = self.cs("i4", rows=4)
    ones_bf = self.cs("ones", bf=True, n=128)
    identb = self.cs("ident", bf=True)
    nidentb = self.cs("nident", bf=True)
    pos_strictb = self.cs("posS_strict_ts" if samp else "posP_strict_ts", bf=True)
    pos_inclb = self.cs("posS_incl" if samp else "posP_incl", bf=True)
    pr = self.prow
    cseg, clen = (16, 8) if samp else (1, T)
    XW = cseg * (clen + 3)
    w5, w5s = self.win(l, 5640, 8)
    bank = self.dense_bank()
    for k in range(8):
        self.MM(bank[0:4, 0:T], w5[:, k, 0:4], self.hT[:, k, 0:T], start=(k == 0), stop=(k == 7), reads=[w5s, self.hT], writes=[bank])
    beta = self.R32("b_bi")
    self.ACT(beta[0:4, 0:T], bank[0:4, 0:T], AF.Sigmoid, reads=[bank], writes=[beta])
    bank = self.dense_bank()
    for k in range(8):
        self.MM(bank[0:4, 0:T], w5[:, k, 4:8], self.hT[:, k, 0:T], start=(k == 0), stop=(k == 7), reads=[w5s, self.hT], writes=[bank])
    gg = self.R32("b_sp")
    self.ACT(gg[0:4, 0:T], bank[0:4, 0:T], AF.Exp, reads=[bank, pr], writes=[gg], bias=pr[0:4, 2 * L + l:2 * L + l + 1])
    self.ACT(gg[0:4, 0:T], gg[0:4, 0:T], AF.Ln, reads=[gg], writes=[gg], bias=1.0)
    self.TS("dve", gg[0:4, 0:T], gg[0:4, 0:T], pr[0:4, 3 * L + l:3 * L + l + 1], None, ALU.mult, reads=[gg, pr], writes=[gg])
    b = self.R32("b_Bn")
    self.SCAN(b[0:4, 0:T], self.cs("rst8" if samp else "rst128", rows=4, n=T), gg[0:4, 0:T], 0.0, ALU.mult, ALU.add,
              reads=[self.cst, gg], writes=[b])
    nb = self.R32("b_u")
    self.TS("dve", nb[0:4, 0:T], b[0:4, 0:T], -1.0, None, ALU.mult, reads=[b], writes=[nb])
    nbeta = self.R32("b_mu")
    self.TS("dve", nbeta[0:4, 0:T], beta[0:4, 0:T], -1.0, None, ALU.mult, reads=[beta], writes=[nbeta])
    eb = self.R32("b_wint")
    self.ACT(eb[0:4, 0:T], b[0:4, 0:T], AF.Exp, reads=[b], writes=[eb])
    bebe = self.R32("b_mt")
    self.TT("dve", bebe[0:4, 0:T], beta[0:4, 0:T], eb[0:4, 0:T], ALU.mult, reads=[beta, eb], writes=[bebe])
    ebl = self.R32("b_emt")
    b3 = b[0:4, 0:T].rearrange("p (s c) -> p s c", c=seglen)
    self.TT("dve", ebl[0:4, 0:T].rearrange("p (s c) -> p s c", c=seglen), b3, b3[:, :, seglen - 1:seglen].to_broadcast([4, nseg, seglen]),
            ALU.subtract, reads=[b], writes=[ebl])
    self.ACT(ebl[0:4, 0:T], ebl[0:4, 0:T], AF.Exp, reads=[ebl], writes=[ebl], scale=-1.0)
    eblast = self.R32("b_wold", 16)
    self.CP("dve", eblast[0:4, 0:nseg].unsqueeze(2), eb[0:4, 0:T].rearrange("p (s c) -> p s c", c=seglen)[:, :, seglen - 1:seglen],
            reads=[eb], writes=[eblast])
    cols = self.T32("c_cols", 24 * cfg.tg)
    rowlist = (b, nb, nbeta, beta, bebe, ebl)
    for j in range(nt):
        jc = slice(j * 128, (j + 1) * 128)
        pc, pcr = self.small()
        for qi, rw in enumerate(rowlist):
            self.MM(pc[:, qi * 4:(qi + 1) * 4], rw[0:4, jc], i4, reads=[rw, self.cst], writes=[pcr])
        self.CP("dve", cols[:, j * 24:(j + 1) * 24], pc[:, 0:24], reads=[pcr], writes=[cols])

    def col(j, qi, h):
        o = j * 24 + qi * 4 + h
        return cols[:, o:o + 1]
    yield "pre"
    for p in range(2):
        wcq, wcqs = self.win(l, 3592 + p * 256, 256)
        wck, wcks = self.win(l, 4104 + p * 256, 256)
        wcv, wcvs = self.win(l, 4616 + p * 256, 256)
        wcg, wcgs = self.win(l, 5128 + p * 256, 256)
        def head(hh):
            h = 2 * p + hh
            c0 = hh * 128
            hs = [0]

            def hsmall():
                pool = (self.ps_small[hh], self.ps_dense[hh])
                b_ = pool[hs[0] % 2]
                hs[0] += 1
                return b_[:, 0:128], b_
            if samp:
                c.dma("sp", self.SsF[hh][:, :, 0:128], d["st_g"][l, :, h].rearrange("s k v -> k s v"), writes=[self.SsF[hh]])
                self.CP("act", self.SsB[hh][:, :, :], self.SsF[hh][:, :, 0:128], reads=[self.SsF[hh]], writes=[self.SsB[hh]])
                yield
            outs = []
            for ci, (wv_, ws_) in enumerate(((wcq, wcqs), (wck, wcks), (wcv, wcvs))):
                chunk = ci * 4 + h
                ps = self.proj_fm(wv_, ws_, c0, 128, T, bank=self.ps_dense[hh])
                xe = self.W(0, hh)
                xe3 = xe[:, 0:XW].rearrange("p (s c) -> p s c", c=clen + 3)
                self.CP("act", xe3[:, :, 3:3 + clen], ps[:, 0:T].rearrange("p (s c) -> p s c", c=clen), reads=[ps], writes=[xe])
                yield
                if samp:
                    cvs = self.T32(f"c_cvs_{hh}", 128, p=48)
                    c.dma("sp", cvs[0:48, :], d["st_v"][l].rearrange("s j c -> (s j) c")[:, chunk * 128:(chunk + 1) * 128], writes=[cvs])
                    bk_ = self.ps_dense[hh]
                    self.TR(bk_[:, 0:48], cvs[0:48, 0:128], ident[0:48, 0:48], reads=[cvs, self.cst], writes=[bk_])
                    self.CP("dve", xe3[:, :, 0:3], bk_[:, 0:48].rearrange("p (s c) -> p s c", c=3), reads=[bk_], writes=[xe])
                    yield
                else:
                    self.CP("dve", xe3[:, :, 0:3], self.hist[l][:, chunk, :].unsqueeze(1), reads=[self.hist[l]], writes=[xe])
                    yield
                    self.CP("dve", self.hist[l][:, chunk, :].unsqueeze(1), xe3[:, :, clen:clen + 3], reads=[xe], writes=[self.hist[l]])
                    yield
                if samp or g.last:
                    nrow = 48 if samp else 3
                    xl = self.T32(f"c_xl_{hh}", 48)
                    self.CP("dve", xl[:, 0:nrow].rearrange("p (s c) -> p s c", c=3), xe3[:, :, clen:clen + 3], reads=[xe], writes=[xl])
                    yield
                    bk_ = self.ps_dense[hh]
                    self.TR(bk_[0:nrow, 0:128], xl[:, 0:nrow], ident, reads=[xl, self.cst], writes=[bk_])
                    cvo = self.T32(f"c_cvo_{hh}", 128, p=48)
                    self.CP("act", cvo[0:nrow, 0:128], bk_[0:nrow, 0:128], reads=[bk_], writes=[cvo])
                    yield
                    if samp:
                        c.dma("sp", self.o["ov_s"][l].rearrange("s j c -> (s j) c")[:, chunk * 128:(chunk + 1) * 128], cvo[0:48, 0:128], reads=[cvo])
                    else:
                        c.dma("sp", self.o["ov_p"][l][:, chunk * 128:(chunk + 1) * 128], cvo[0:3, 0:128], reads=[cvo])
                acc = self.W(1 + ci, hh)
                acc3 = acc[:, 0:T].rearrange("p (s c) -> p s c", c=clen)
                cw = self.cw
                self.TS("dve", acc3, xe3[:, :, 3:3 + clen], cw[:, l, chunk, 3:4], None, ALU.mult, reads=[xe, cw], writes=[acc])
                for tap in (2, 1, 0):
                    self.STT(acc3, xe3[:, :, tap:tap + clen], cw[:, l, chunk, tap:tap + 1], acc3, ALU.mult, ALU.add, reads=[xe, cw, acc], writes=[acc])
                    yield
                self.ACT(acc[:, 0:T], acc[:, 0:T], AF.Silu, reads=[acc], writes=[acc])
                yield
                outs.append(acc)
            cq, ck, cv = outs
            psg = self.proj_fm(wcg, wcgs, c0, 128, T, bank=self.ps_dense[hh])
            gs = self.W(7, hh)
            self.ACT(gs[:, 0:T], psg[:, 0:T], AF.Silu, reads=[psg], writes=[gs])
            yield
            sq = self.V(0, hh)
            self.ACT(sq[:, 0:T], cq[:, 0:T], AF.Square, reads=[cq], writes=[sq])
            yield
            bk_ = self.ps_dense[hh]
            self.MM(bk_[:, 0:T], ones_bf, sq[:, 0:T], reads=[self.cstb, sq], writes=[bk_])
            rq = self.W(4, hh)
            self.ACT(rq[:, 0:T], bk_[:, 0:T], AF.Ln, reads=[bk_], writes=[rq], bias=NORM_EPS)
            yield
            self.ACT(rq[:, 0:T], rq[:, 0:T], AF.Exp, reads=[rq], writes=[rq], scale=-0.5)
            yield
            q1 = self.V(2, hh)
            self.STT(q1[:, 0:T], cq[:, 0:T], 128.0 ** -0.5, rq[:, 0:T], ALU.mult, ALU.mult, reads=[cq, rq], writes=[q1])
            yield
            bk_ = self.ps_dense[hh]
            self.MM(bk_[:, 0:T], self.cs("sel", rows=4, c0=h * 128, n=128), eb[0:4, 0:T], reads=[self.cst, eb], writes=[bk_])
            q2 = self.V(3, hh)
            self.TT("dve", q2[:, 0:T], q1[:, 0:T], bk_[:, 0:T], ALU.mult, reads=[q1, bk_], writes=[q2])
            yield
            sq2 = self.V(1, hh)
            self.ACT(sq2[:, 0:T], ck[:, 0:T], AF.Square, reads=[ck], writes=[sq2])
            yield
            bk_ = self.ps_dense[hh]
            self.MM(bk_[:, 0:T], ones_bf, sq2[:, 0:T], reads=[self.cstb, sq2], writes=[bk_])
            rk = self.W(4, hh)
            self.ACT(rk[:, 0:T], bk_[:, 0:T], AF.Ln, reads=[bk_], writes=[rk], bias=NORM_EPS)
            yield
            self.ACT(rk[:, 0:T], rk[:, 0:T], AF.Exp, reads=[rk], writes=[rk], scale=-0.5)
            yield
            kn = ck
            self.TT("dve", kn[:, 0:T], ck[:, 0:T], rk[:, 0:T], ALU.mult, reads=[ck, rk], writes=[kn])
            yield
            knb = self.V(4, hh)
            self.CP("act", knb[:, 0:T], kn[:, 0:T], reads=[kn], writes=[knb])
            yield
            bk_ = self.ps_dense[hh]
            self.MM(bk_[:, 0:nseg], self.cs("sel", rows=4, c0=h * 128, n=128), eblast[0:4, 0:nseg], reads=[self.cst, eblast], writes=[bk_])
            decbc = self.T32(f"c_decbc_{hh}", 16)
            self.CP("dve", decbc[:, 0:nseg], bk_[:, 0:nseg], reads=[bk_], writes=[decbc])
            yield
            ob = self.ps_o[hh]
            selh = self.cs("sel", rows=4, c0=h * 128, n=128)
            bA, bB = self.ps_small[hh], self.ps_dense[hh]
            WN = nt * 128
            Qa = [self.T32(f"c_Qb{i}_{hh}", cfg.TMAX) for i in range(2)]
            QTa = [self.T32(f"c_QTb{i}_{hh}", cfg.TMAX) for i in range(2)]
            PTa = [self.T32(f"c_PTb{i}_{hh}", cfg.TMAX) for i in range(2)]
            Dmb = self.T32(f"c_Dmb_{hh}", cfg.TMAX)
            DmTb = self.T32(f"c_DmTb_{hh}", cfg.TMAX)
            tc_ = lambda j: slice(j * 128, (j + 1) * 128)
            for j in range(nt):
                self.MM(bA[:, tc_(j)], selh, b[0:4, tc_(j)], start=True, stop=False, reads=[self.cst, b], writes=[bA])
                self.MM(bA[:, tc_(j)], identb, pos_strictb, start=False, stop=True, reads=[self.cstb], writes=[bA])
            for j in range(nt):
                self.ACT(Dmb[:, tc_(j)], bA[:, tc_(j)], AF.Exp, reads=[bA, cols], writes=[Dmb], scale=-1.0, bias=col(j, 0, h))
            yield
            for j in range(nt):
                self.MM(bB[:, tc_(j)], selh, b[0:4, tc_(j)], start=True, stop=False, reads=[self.cst, b], writes=[bB])
                self.MM(bB[:, tc_(j)], nidentb, pos_inclb, start=False, stop=True, reads=[self.cstb], writes=[bB])
            for j in range(nt):
                self.ACT(DmTb[:, tc_(j)], bB[:, tc_(j)], AF.Exp, reads=[bB, cols], writes=[DmTb], bias=col(j, 1, h))
            yield
            for j in range(nt):
                self.MM(bA[:, tc_(j)], knb[:, tc_(j)], knb[:, tc_(j)], reads=[knb], writes=[bA])
            for j in range(nt):
                self.STT(Qa[0][:, tc_(j)], bA[:, tc_(j)], col(j, 2, h), Dmb[:, tc_(j)], ALU.mult, ALU.mult, reads=[bA, cols, Dmb], writes=[Qa[0]])
            yield
            for j in range(nt):
                self.TR(bB[:, tc_(j)], Qa[0][:, tc_(j)], ident, reads=[Qa[0], self.cst], writes=[bB])
            for j in range(nt):
                self.TT("dve", PTa[0][:, tc_(j)], bB[:, tc_(j)], ident, ALU.add, reads=[bB, self.cst], writes=[PTa[0]])
            self.CP("dve", QTa[0][:, 0:WN], bB[:, 0:WN], reads=[bB], writes=[QTa[0]])
            yield
            for stp in range(nsteps):
                cur, nxt = stp % 2, (stp + 1) % 2
                lastst = stp == nsteps - 1
                for j in range(nt):
                    self.MM(bA[:, tc_(j)], QTa[cur][:, tc_(j)], Qa[cur][:, tc_(j)], reads=[QTa[cur], Qa[cur]], writes=[bA])
                self.CP("act", Qa[nxt][:, 0:WN], bA[:, 0:WN], reads=[bA], writes=[Qa[nxt]])
                yield
                if not lastst:
                    for j in range(nt):
                        self.TR(bB[:, tc_(j)], Qa[nxt][:, tc_(j)], ident, reads=[Qa[nxt], self.cst], writes=[bB])
                    self.CP("dve", QTa[nxt][:, 0:WN], bB[:, 0:WN], reads=[bB], writes=[QTa[nxt]])
                    yield
                for j in range(nt):
                    self.MM(bA[:, tc_(j)], Qa[nxt][:, tc_(j)], PTa[cur][:, tc_(j)], reads=[Qa[nxt], PTa[cur]], writes=[bA])
                self.TT("dve", PTa[nxt][:, 0:WN], PTa[cur][:, 0:WN], bA[:, 0:WN], ALU.add, reads=[PTa[cur], bA], writes=[PTa[nxt]])
                yield
            PTf = PTa[nsteps % 2]
            for j in range(nt):
                jc = slice(j * 128, (j + 1) * 128)
                PT = self._View(PTf[:, jc], "ptv")
                PT.res = PTf.res
                DmT = self._View(DmTb[:, jc], "dmtv")
                DmT.res = DmTb.res
                aT, aTr = hsmall()
                self.MM(aT, knb[:, jc], q1[:, jc], reads=[knb, q1], writes=[aTr])
                attm = self.T16(f"c_attm_{hh}", 128)
                self.TT("dve", attm[:, :], aT, DmT[:, :], ALU.mult, reads=[aTr, DmT], writes=[attm])
                yield
                kt_, ktr = hsmall()
                self.TR(kt_, kn[:, jc], ident, reads=[kn, self.cst], writes=[ktr])
                kbe = self.T32(f"c_kbe_{hh}", 128)
                self.ACT(kbe[:, :], kt_, AF.Copy, reads=[ktr, cols], writes=[kbe], scale=col(j, 4, h))
                yield
                khat = self.T32(f"c_khat_{hh}", 128)
                self.ACT(khat[:, :], kt_, AF.Copy, reads=[ktr, cols], writes=[khat], scale=col(j, 5, h))
                yield
                vt_, vtr = hsmall()
                self.TR(vt_, cv[:, jc], ident, reads=[cv, self.cst], writes=[vtr])
                vb = self.T32(f"c_vb_{hh}", 128)
                self.ACT(vb[:, :], vt_, AF.Copy, reads=[vtr, cols], writes=[vb], scale=col(j, 3, h))
                yield
                WT, WTr = hsmall()
                self.MM(WT, kbe[:, :], PT[:, :], reads=[kbe, PT], writes=[WTr])
                nWT = self.T32(f"c_nWT_{hh}", 128)
                self.ACT(nWT[:, :], WT, AF.Copy, reads=[WTr], writes=[nWT], scale=-1.0)
                yield
                vnT, vnTr = hsmall()
                self.MM(vnT, vb[:, :], PT[:, :], start=True, stop=False, reads=[vb, PT], writes=[vnTr])
                for i in range(spt):
                    seg = j * spt + i
                    lc = slice(i * seglen, (i + 1) * seglen)
                    if samp:
                        Sf, Sfr = self.SsF[hh][:, seg, 0:128], self.SsF[hh]
                    else:
                        Sft = self.SC[l][h][0]
                        Sf, Sfr = Sft[:, :], Sft
                    self.MM(vnT[:, lc], Sf, nWT[:, lc], start=False, stop=(i == spt - 1), reads=[Sfr, nWT], writes=[vnTr])
                vnTs = self.T32(f"c_vnTs_{hh}", 128)
                self.CP("act", vnTs[:, :], vnT, reads=[vnTr], writes=[vnTs])
                yield
                vn_, vnr = hsmall()
                self.TR(vn_, vnTs[:, :], ident, reads=[vnTs, self.cst], writes=[vnr])
                vnb = self.T16(f"c_vnb_{hh}", 128)
                self.CP("act", vnb[:, :], vn_, reads=[vnr], writes=[vnb])
                yield
                self.MM(ob[:, jc], vnb[:, :], attm[:, :], start=True, stop=False, reads=[vnb, attm], writes=[ob])
                for i in range(spt):
                    seg = j * spt + i
                    sc = slice(seg * seglen, (seg + 1) * seglen)
                    if samp:
                        Sf, Sb, Sfr, Sbr = self.SsF[hh][:, seg, 0:128], self.SsB[hh][:, seg, :], self.SsF[hh], self.SsB[hh]
                    else:
                        Sft, Sbt = self.SC[l][h]
                        Sf, Sb, Sfr, Sbr = Sft[:, :], Sbt[:, :], Sft, Sbt
                    self.MM(ob[:, sc], Sb, q2[:, sc], start=False, stop=(i == spt - 1), reads=[Sbr, q2], writes=[ob])
                    if spt > 1:
                        khm = self.T16(f"khseg{i % 2}_{hh}", 128)
                        self.TS("dve", khm[:, :], khat[:, :], rm[:, i:i + 1], None, ALU.mult, reads=[khat, self.cst], writes=[khm])
                    else:
                        khm = self.T16(f"khseg0_{hh}", 128)
                        self.CP("dve", khm[:, :], khat[:, :], reads=[khat], writes=[khm])
                    pS, pSr = hsmall()
                    self.MM(pS, khm[:, :], vnb[:, :], reads=[khm, vnb], writes=[pSr])
                    self.STT(Sf, Sf, decbc[:, seg:seg + 1], pS, ALU.mult, ALU.add, reads=[Sfr, decbc, pSr], writes=[Sfr])
                    yield
                    if not samp:
                        self.CP("act", Sb, Sf, reads=[Sfr], writes=[Sbr])
                        yield
            self.rms_gate(g, l, 2, h, ob, gs, self.gC, first=(h == 0), last=(h == 3))
            if samp:
                c.dma("sp", self.o["og_s"][l, :, h].rearrange("s k v -> k s v"), self.SsF[hh][:, :, 0:128], reads=[self.SsF[hh]])
            elif g.last:
                c.dma("sp", self.o["og_p"][l, h], self.SC[l][h][0][:, :], reads=[self.SC[l][h][0]])

        self.interleave([head(0), head(1)])


Builder.branch_C = _branch_C


def _ln_prefetch(self, l, which, half):
    c, d = self.c, self.d
    gname, bname = ("ln1_g", "ln1_b") if which == 1 else ("ln2_g", "ln2_b")
    hs = slice(half * 512, (half + 1) * 512)
    c.dma("sp", self.lnp[0][:], d[gname][l][hs].partition_broadcast(128), writes=[self.lnp[0]])
    c.dma("sp", self.lnp[1][:], d[bname][l][hs].partition_broadcast(128), writes=[self.lnp[1]])


def _layer_norm(self, g, l, which):
    c, d = self.c, self.d
    st = self.T32("ln_st", 8 * self.cfg.tg)
    junk = self.big[:, 0:D // 2].bitcast(BF16)
    junkr = self.big
    tiles = range(g.nt)
    X = lambda j: self.h_tok[:, j, :]
    S = lambda j: st[:, j * 8:(j + 1) * 8]
    for j in tiles:
        self.ACT(junk, X(j), AF.Identity, reads=[self.h_tok], writes=[junkr, st], accum=S(j)[:, 0:1])
    for j in tiles:
        self.TS("dve", S(j)[:, 1:2], S(j)[:, 0:1], -1.0 / D, None, ALU.mult, reads=[st], writes=[st])
    for j in tiles:
        self.ACT(junk, X(j), AF.Square, reads=[self.h_tok, st], writes=[junkr, st], bias=S(j)[:, 1:2], accum=S(j)[:, 2:3])
    for j in tiles:
        self.ACT(S(j)[:, 3:4], S(j)[:, 2:3], AF.Ln, reads=[st], writes=[st], scale=1.0 / D, bias=LN_EPS)
    for j in tiles:
        self.ACT(S(j)[:, 3:4], S(j)[:, 3:4], AF.Exp, reads=[st], writes=[st], scale=-0.5)
    for j in tiles:
        self.TS("dve", X(j), X(j), S(j)[:, 1:2], S(j)[:, 3:4], ALU.add, ALU.mult, reads=[self.h_tok, st], writes=[self.h_tok])
    for half in range(2):
        hs = slice(half * 512, (half + 1) * 512)
        if half == 1:
            self.ln_prefetch(l, which, 1)
        for j in tiles:
            x = self.h_tok[:, j, hs]
            self.TT("dve", x, x, self.lnp[0][:], ALU.mult, reads=[self.h_tok, self.lnp[0]], writes=[self.h_tok])
        for j in tiles:
            x = self.h_tok[:, j, hs]
            self.TT("dve", x, x, self.lnp[1][:], ALU.add, reads=[self.h_tok, self.lnp[1]], writes=[self.h_tok])


def _merge(self, g, l):
    c, cfg, d = self.c, self.cfg, self.d
    self.dense_pool = self.ps_dense + self.ps_small + self.ps_o
    T, nt, TM = g.T, g.nt, cfg.TMAX
    self.ln_prefetch(l, 1, 0)
    big = self.big
    macc = big[:, 0:8 * TM].rearrange("p (k t) -> p k t", t=TM)
    mbf = big[:, 8 * TM:12 * TM].bitcast(BF16).rearrange("p (k t) -> p k t", t=TM)
    for n in range(3):
        if n == 2:
            yield "ab_done"
            self.dense_pool = self.ps_dense + self.ps_small + self.ps_rows + self.ps_o
        wbr = d["w_branch"][l, n].rearrange("(k p) n -> p k n", p=128)
        for dh in range(2):
            wbv, wbs = self.ws.get(("w_branch", l, n, dh), wbr[:, :, dh * 512:(dh + 1) * 512], 4, 512)
            for mgb in range(2):
                wmg, wmgs = self.win(l, 5648 + n * 1024 + dh * 512 + mgb * 256, 256)
                for q in range(2):
                    j = dh * 4 + mgb * 2 + q
                    psm = self.proj_fm(wmg, wmgs, q * 128, 128, T)
                    sg = self.W(0, 0)
                    self.ACT(sg[:, 0:T], psm[:, 0:T], AF.Sigmoid, reads=[psm], writes=[sg])
                    psz = self.dense_bank()
                    for k in range(4):
                        self.MM(psz[:, 0:T], wbv[:, k, (mgb * 2 + q) * 128:(mgb * 2 + q + 1) * 128], self.yT[n][:, k, 0:T],
                                start=(k == 0), stop=(k == 3), reads=[wbs, self.yT[n]], writes=[psz])
                    if n == 0:
                        self.TT("dve", macc[:, j, 0:T], sg[:, 0:T], psz[:, 0:T], ALU.mult, reads=[sg, psz], writes=[big])
                    else:
                        prod = self.W(1, 0)
                        self.TT("dve", prod[:, 0:T], sg[:, 0:T], psz[:, 0:T], ALU.mult, reads=[sg, psz], writes=[prod])
                        if n == 1:
                            self.TT("dve", macc[:, j, 0:T], macc[:, j, 0:T], prod[:, 0:T], ALU.add, reads=[big, prod], writes=[big])
                        else:
                            self.TT("dve", mbf[:, j, 0:T], macc[:, j, 0:T], prod[:, 0:T], ALU.add, reads=[big, prod], writes=[big])
    wor = d["w_out"][l].rearrange("(k p) n -> p k n", p=128)
    for half in range(2):
        hs = slice(half * 512, (half + 1) * 512)
        w0, w0s = self.ws.get(("w_out", l, 0, half), wor[:, 0:4, hs], 4, 512)
        w1, w1s = self.ws.get(("w_out", l, 1, half), wor[:, 4:8, hs], 4, 512)
        for j in range(nt):
            bank = self.dense_bank()
            for k in range(8):
                wv, wsx = (w0, w0s) if k < 4 else (w1, w1s)
                self.MM(bank[:, 0:512], mbf[:, k, j * 128:(j + 1) * 128], wv[:, k % 4, :], start=(k == 0), stop=(k == 7),
                        reads=[big, wsx], writes=[bank])
            self.STT(self.h_tok[:, j, hs], self.h_tok[:, j, hs], ALPHA, bank[:, 0:512], ALU.mult, ALU.add,
                     reads=[self.h_tok, bank], writes=[self.h_tok])
    self.dense_pool = self.ps_dense
    self.layer_norm(g, l, 1)
    for j in range(nt):
        self.dump(f"h1_{j}_l{l}_g{g.idx}", self.h_tok, self.h_tok[:, j, :], 128, D)
    self.make_hT(g)


def _ffn(self, g, l):
    c, cfg, d = self.c, self.cfg, self.d
    self.dense_pool = self.ps_dense + self.ps_small + self.ps_rows
    T, nt, TM = g.T, g.nt, cfg.TMAX
    self.ln_prefetch(l, 2, 0)
    big = self.big
    aT = big[:, 0:11 * TM].bitcast(BF16).rearrange("p (k t) -> p k t", t=TM)
    wfi = d["w_ffn_in"][l].rearrange("(k p) n -> p k n", p=128)
    for part in range(2):
        for blk in range(11):
            wv, wsx = self.ws.get(("w_ffn_in", l, part, blk), wfi[:, :, part * FH + blk * 256:part * FH + (blk + 1) * 256], 8, 256)
            for q in range(2):
                j = blk * 2 + q
                ps = self.proj_fm(wv, wsx, q * 128, 128, T)
                if part == 0:
                    self.ACT(aT[:, j, 0:T], ps[:, 0:T], AF.Silu, reads=[ps], writes=[big])
                else:
                    self.TT("dve", aT[:, j, 0:T], aT[:, j, 0:T], ps[:, 0:T], ALU.mult, reads=[big, ps], writes=[big])
    wfo = d["w_ffn_out"][l].rearrange("(k p) n -> p k n", p=128)
    for half in range(2):
        hs = slice(half * 512, (half + 1) * 512)
        accs = [self.ps_o[j % 2] for j in range(nt)]
        assert nt <= 2
        for kb in range(6):
            k0 = kb * 4
            nk = min(4, 22 - k0)
            wv, wsx = self.ws.get(("w_ffn_out", l, kb, half), wfo[:, k0:k0 + nk, hs], nk, 512)
            for j in range(nt):
                for kk in range(nk):
                    k = k0 + kk
                    self.MM(accs[j][:, 0:512], aT[:, k, j * 128:(j + 1) * 128], wv[:, kk, :], start=(k == 0), stop=(k == 21),
                            reads=[big, wsx], writes=[accs[j]])
        for j in range(nt):
            self.STT(self.h_tok[:, j, hs], self.h_tok[:, j, hs], ALPHA, accs[j][:, 0:512], ALU.mult, ALU.add,
                     reads=[self.h_tok, accs[j]], writes=[self.h_tok])
    self.dense_pool = self.ps_dense
    self.layer_norm(g, l, 2)
    if l < cfg.depth - 1:
        self.make_hT(g)


Builder.layer_norm = _layer_norm
Builder.ln_prefetch = _ln_prefetch
Builder.merge = _merge
Builder.ffn = _ffn


_W_NAMES = ["w_in", "lb_logits", "a_norm_g", "b_mi", "b_mf", "b_norm_g", "conv_w", "a_log", "dt_bias",
            "c_norm_g", "w_branch", "w_out", "ln1_g", "ln1_b", "w_ffn_in", "w_ffn_out", "ln2_g", "ln2_b"]


def run_cfg(cfg, inputs, n_cores=8, trace=False):
    b = Builder(cfg)
    nc = b.build()
    consts = _make_consts()
    f = lambda a: np.ascontiguousarray(a, dtype=np.float32)
    in_maps = []
    for i in range(n_cores):
        s0, s1 = i * 16, (i + 1) * 16
        m = dict(
            xp=f(inputs["x_prompt"][i][:cfg.seq]),
            xs=f(inputs["x_sample"][s0:s1].reshape(128, D)),
            st_a=f(inputs["state_hgrn"][:cfg.depth, s0:s1]), st_c=f(inputs["state_mlstm_c"][:cfg.depth, s0:s1]),
            st_n=f(inputs["state_mlstm_n"][:cfg.depth, s0:s1]), st_m=f(inputs["state_mlstm_m"][:cfg.depth, s0:s1]),
            st_g=f(inputs["state_gdn"][:cfg.depth, s0:s1]), st_v=f(inputs["state_gdn_conv"][:cfg.depth, s0:s1]),
            consts=consts,
        )
        for nm in _W_NAMES:
            m[nm] = f(inputs[nm][:cfg.depth])
        in_maps.append(m)
    res = run_bass_kernel_spmd(nc, in_maps, core_ids=list(range(n_cores)), **({"trace": True} if trace else {}))
    return res, b


def kernel(**inputs):
    cfg = Cfg(depth=DEPTH, n_pg=8, tg=2, sample=True)
    res, b = run_cfg(cfg, inputs, 8)
    R = res.results
    cat = lambda k, ax: np.concatenate([r[k] for r in R], axis=ax)
    stk = lambda k: np.stack([r[k] for r in R], axis=1)
    y_p = np.stack([r["yp"] for r in R], axis=0)
    y_s = np.concatenate([r["ys"].reshape(16, 8, D) for r in R], axis=0)
    outs = (y_p, y_s,
            stk("oa_p"), cat("oa_s", 1), stk("oc_p"), cat("oc_s", 1), stk("on_p"), cat("on_s", 1),
            stk("om_p"), cat("om_s", 1), stk("og_p"), cat("og_s", 1), stk("ov_p"), cat("ov_s", 1))
    return tuple(np.ascontiguousarray(o, dtype=np.float32) for o in outs)
```

```python
import numpy as np
from contextlib import ExitStack
import concourse.bass as bass
import concourse.mybir as mybir
from concourse.bass_utils import run_bass_kernel_spmd

F32 = mybir.dt.float32
BF16 = mybir.dt.bfloat16
ALU = mybir.AluOpType
AF = mybir.ActivationFunctionType
AX = mybir.AxisListType

D = 1024
NIN = 8720
FH = 2816
DEPTH = 4
ALPHA = (2 * DEPTH) ** 0.25
LN_EPS = 1e-5
NORM_EPS = 1e-6
BIG = 30000.0

ENGS = ("pe", "act", "dve", "pool", "sp")
N_DMA_SEMS = 8
SAME_ENG_SYNC = True


class Res:
    def __init__(self, name=""):
        self.name = name
        self.last_write = None
        self.reads = {}
        self.excl = False


class Tl:
    def __init__(self, t, name):
        self.t = t
        self.res = Res(name)
        self.name = name

    def __getitem__(self, idx):
        return self.t[idx]


class Ctx:
    def __init__(self, nc, stack):
        self.nc = nc
        self.stack = stack
        self.ops = {e: [] for e in ENGS}
        self.seq = {e: 0 for e in ENGS}
        self.esem = {e: stack.enter_context(nc.semaphore("s_" + e)) for e in ENGS}
        self.dsem, self.dcount, self.dnext = {}, {}, {}
        for q in ("sp", "act", "pool"):
            self.dsem[q] = [stack.enter_context(nc.semaphore(f"d_{q}{i}")) for i in range(N_DMA_SEMS)]
            self.dcount[q] = [0] * N_DMA_SEMS
            self.dnext[q] = 0
        self.waited = {e: {} for e in ENGS}
        self.semobj = {("e", e): self.esem[e] for e in ENGS}
        for q in self.dsem:
            for i, s in enumerate(self.dsem[q]):
                self.semobj[("d", q, i)] = s
        self.n_inst = 0
        self.out_tokens = []

    def sb(self, name, shape, dtype=F32):
        t = self.stack.enter_context(self.nc.sbuf_tensor(name, list(shape), dtype))
        return Tl(t, name)

    def ps(self, name, shape, dtype=F32):
        t = self.stack.enter_context(self.nc.psum_tensor(name, list(shape), dtype))
        tl = Tl(t, name)
        tl.res.excl = True
        return tl

    def _collect(self, eng, reads, writes):
        deps = {}

        def need(tok):
            if tok is None:
                return
            key, val = tok
            if key == ("e", eng) and (eng == "pe" or not SAME_ENG_SYNC):
                return
            if val > deps.get(key, 0):
                deps[key] = val
        for r in reads:
            r = getattr(r, "res", r)
            need(r.last_write)
            if r.excl:
                for k, v in r.reads.items():
                    need((k, v))
        for w in writes:
            w = getattr(w, "res", w)
            need(w.last_write)
            for k, v in w.reads.items():
                need((k, v))
        out = []
        wd = self.waited[eng]
        for k, v in deps.items():
            if wd.get(k, 0) >= v:
                continue
            wd[k] = v
            out.append((k, v))
        return out

    def _commit(self, reads, writes, token):
        key, val = token
        for r in reads:
            r = getattr(r, "res", r)
            if r.reads.get(key, 0) < val:
                r.reads[key] = val
        for w in writes:
            w = getattr(w, "res", w)
            w.last_write = token
            w.reads = {}

    def op(self, eng, fn, reads=(), writes=()):
        waits = self._collect(eng, reads, writes)
        self.seq[eng] += 1
        token = (("e", eng), self.seq[eng])
        self.ops[eng].append(("op", fn, waits))
        self._commit(reads, writes, token)
        self.n_inst += 1

    def dma(self, q, out, in_, reads=(), writes=(), is_output=False, **kw):
        i = self.dnext[q]
        self.dnext[q] = (i + 1) % N_DMA_SEMS
        waits = self._collect(q, reads, writes)
        key = ("d", q, i)
        prev = self.dcount[q][i] * 16
        if prev > 0 and self.waited[q].get(key, 0) < prev:
            self.waited[q][key] = prev
            waits.append((key, prev))
        self.dcount[q][i] += 1
        token = (key, self.dcount[q][i] * 16)
        self.ops[q].append(("dma", (out, in_, kw, self.dsem[q][i]), waits))
        self._commit(reads, writes, token)
        self.n_inst += 1
        return token

    def finish(self):
        toks = []
        for q in self.dsem:
            for i in range(N_DMA_SEMS):
                if self.dcount[q][i] > 0:
                    toks.append((("d", q, i), self.dcount[q][i] * 16))
        for e in ENGS:
            if self.seq[e] > 0:
                toks.append((("e", e), self.seq[e]))
        self.ops["sp"].append(("wait", None, toks))
        nc = self.nc
        block = self.stack.enter_context(nc.Block())
        ctx = self

        def run(eng_name, engobj):
            esem = ctx.esem[eng_name]
            for kind, payload, waits in ctx.ops[eng_name]:
                for key, val in waits:
                    engobj.wait_ge(ctx.semobj[key], val)
                if kind == "op":
                    payload(engobj).then_inc(esem, 1)
                elif kind == "dma":
                    out, in_, kw, sem = payload
                    engobj.dma_start(out=out, in_=in_, **kw).then_inc(sem, 16)

        block.sync(lambda e: run("sp", e))
        block.scalar(lambda e: run("act", e))
        block.vector(lambda e: run("dve", e))
        block.gpsimd(lambda e: run("pool", e))
        block.tensor(lambda e: run("pe", e))


def _const_layout():
    cols = {}
    off = 0

    def add(name, n):
        nonlocal off
        cols[name] = (off, n)
        off += n
    add("ident", 128)
    add("maskA64", 128)
    add("mask8", 128)
    add("mask128", 128)
    add("posP_incl", 128)
    add("posS_incl", 128)
    add("posP_strict_ts", 128)
    add("posS_strict_ts", 128)
    add("sel", 512)
    add("onecol", 16)
    add("i4", 4)
    add("rm2", 2)
    add("rm16", 16)
    add("rst64", 256)
    add("rst8", 256)
    add("rst128", 256)
    add("ones", 256)
    add("nident", 128)
    return cols, off


CL, NCONST = _const_layout()


def _make_consts():
    c = np.zeros((128, NCONST), np.float32)

    def put(name, arr):
        o, n = CL[name]
        c[:arr.shape[0], o:o + n] = arr
    s = np.arange(128)[:, None]
    t = np.arange(128)[None, :]
    put("ident", (s == t).astype(np.float32))
    put("maskA64", ((s // 64 == t // 64) & (s <= t)).astype(np.float32))
    put("mask8", ((s // 8 == t // 8) & (s <= t)).astype(np.float32))
    put("mask128", (s <= t).astype(np.float32))
    put("posP_incl", np.where(s <= t, 0.0, BIG).astype(np.float32))
    put("posS_incl", np.where((s // 8 == t // 8) & (s <= t), 0.0, BIG).astype(np.float32))
    tt = np.arange(128)[:, None]
    ss = np.arange(128)[None, :]
    put("posP_strict_ts", np.where(ss < tt, 0.0, BIG).astype(np.float32))
    put("posS_strict_ts", np.where((ss // 8 == tt // 8) & (ss < tt), 0.0, BIG).astype(np.float32))
    sel = np.zeros((4, 512), np.float32)
    for h in range(4):
        sel[h, h * 128:(h + 1) * 128] = 1.0
    put("sel", sel)
    oc = np.zeros((128, 16), np.float32)
    for h in range(4):
        oc[:, 4 * h + h] = 1.0
    put("onecol", oc)
    put("i4", np.eye(4, dtype=np.float32))
    put("rm2", (s // 64 == np.arange(2)[None, :]).astype(np.float32))
    put("rm16", (s // 8 == np.arange(16)[None, :]).astype(np.float32))
    tr = np.arange(256)[None, :]
    put("rst64", np.broadcast_to((tr % 64 != 0).astype(np.float32), (128, 256)))
    put("rst8", np.broadcast_to((tr % 8 != 0).astype(np.float32), (128, 256)))
    put("rst128", np.broadcast_to((tr % 128 != 0).astype(np.float32), (128, 256)))
    put("ones", np.ones((128, 256), np.float32))
    put("nident", -(s == t).astype(np.float32))
    return c


def _extra_consts(c):
    return c


class WStream:
    SLOT = 2048

    def __init__(self, c, nslot, pf):
        self.c = c
        self.nslot = nslot
        self.pf = pf
        self.slots = [c.sb(f"wslot{i}", [128, self.SLOT], BF16) for i in range(nslot)]
        self.sched = []
        self.blocks = {}
        self.rec = True
        self.pos = 0
        self.issued = 0
        self.scratch = None
        self.bres = []

    def start_real(self, nc):
        self.rec = False
        self.pos = 0
        self.issued = 0
        nb = max(1, len(self.blocks))
        self.scratch = nc.dram_tensor("w_scratch", [nb, 128, self.SLOT], BF16, kind="Internal").ap()
        self.bres = [Res(f"wblk{i}") for i in range(nb)]

    def emit_conversions(self):
        for key, (idx, src, k, n) in self.blocks.items():
            dst = self.scratch[idx][:, 0:k * n].rearrange("p (k n) -> p k n", k=k)
            self.c.dma("pool", dst, src, writes=[self.bres[idx]])

    def _view(self, i, k, n):
        s = self.slots[i % self.nslot]
        return s[:, 0:k * n].rearrange("p (k n) -> p k n", k=k), s

    def get(self, key, src, k, n):
        assert k * n <= self.SLOT
        if self.rec:
            if key not in self.blocks:
                self.blocks[key] = (len(self.blocks), src, k, n)
            self.sched.append((key, k, n))
            return self._view(len(self.sched) - 1, k, n)
        i = self.pos
        self.pos += 1
        lim = min(len(self.sched), i + 1 + self.pf)
        while self.issued < lim:
            j = self.issued
            skey, sk, sn = self.sched[j]
            idx = self.blocks[skey][0]
            s = self.slots[j % self.nslot]
            self.c.dma("sp", s[:, 0:sk * sn], self.scratch[idx][:, 0:sk * sn], reads=[self.bres[idx]], writes=[s])
            self.issued += 1
        return self._view(i, k, n)


class Cfg:
    def __init__(self, depth=DEPTH, n_pg=8, tg=2, sample=True, dbg=False, parts="ABCMF"):
        self.parts = parts
        self.depth = depth
        self.n_pg = n_pg
        self.tg = tg
        self.sample = sample
        self.dbg = dbg
        self.TMAX = tg * 128
        self.seq = n_pg * tg * 128


IN_NAMES = ["xp", "xs", "st_a", "st_c", "st_n", "st_m", "st_g", "st_v",
            "w_in", "lb_logits", "a_norm_g", "b_mi", "b_mf", "b_norm_g", "conv_w", "a_log", "dt_bias",
            "c_norm_g", "w_branch", "w_out", "ln1_g", "ln1_b", "w_ffn_in", "w_ffn_out", "ln2_g", "ln2_b", "consts"]


class Builder:
    def __init__(self, cfg):
        self.cfg = cfg
        L = cfg.depth
        nc = bass.Bass("TRN2", target_bir_lowering=False)
        self.nc = nc

        def din(name, shape):
            return nc.dram_tensor(name, list(shape), F32, kind="ExternalInput").ap()

        def dout(name, shape):
            return nc.dram_tensor(name, list(shape), F32, kind="ExternalOutput").ap()
        S = cfg.seq
        self.d = dict(
            xp=din("xp", [S, D]), xs=din("xs", [128, D]),
            st_a=din("st_a", [L, 16, 4, 128, 128]), st_c=din("st_c", [L, 16, 4, 128, 64]),
            st_n=din("st_n", [L, 16, 4, 64]), st_m=din("st_m", [L, 16, 4]),
            st_g=din("st_g", [L, 16, 4, 128, 128]), st_v=din("st_v", [L, 16, 3, 1536]),
            w_in=din("w_in", [L, D, NIN]), lb_logits=din("lb_logits", [L, 512]),
            a_norm_g=din("a_norm_g", [L, 512]), b_mi=din("b_mi", [L, 4]), b_mf=din("b_mf", [L, 4]),
            b_norm_g=din("b_norm_g", [L, 512]), conv_w=din("conv_w", [L, 4, 1536]),
            a_log=din("a_log", [L, 4]), dt_bias=din("dt_bias", [L, 4]), c_norm_g=din("c_norm_g", [L, 512]),
            w_branch=din("w_branch", [L, 3, 512, D]), w_out=din("w_out", [L, D, D]),
            ln1_g=din("ln1_g", [L, D]), ln1_b=din("ln1_b", [L, D]),
            w_ffn_in=din("w_ffn_in", [L, D, 2 * FH]), w_ffn_out=din("w_ffn_out", [L, FH, D]),
            ln2_g=din("ln2_g", [L, D]), ln2_b=din("ln2_b", [L, D]),
            consts=din("consts", [128, NCONST]),
        )
        self.o = dict(
            yp=dout("yp", [S, D]), ys=dout("ys", [128, D]),
            oa_p=dout("oa_p", [L, 4, 128, 128]), oa_s=dout("oa_s", [L, 16, 4, 128, 128]),
            oc_p=dout("oc_p", [L, 4, 128, 64]), oc_s=dout("oc_s", [L, 16, 4, 128, 64]),
            on_p=dout("on_p", [L, 4, 64]), on_s=dout("on_s", [L, 16, 4, 64]),
            om_p=dout("om_p", [L, 4]), om_s=dout("om_s", [L, 16, 4]),
            og_p=dout("og_p", [L, 4, 128, 128]), og_s=dout("og_s", [L, 16, 4, 128, 128]),
            ov_p=dout("ov_p", [L, 3, 1536]), ov_s=dout("ov_s", [L, 16, 3, 1536]),
        )
        self.dbg_aps = {}
        self.dbg_off = 0
        if cfg.dbg:
            self.dbgt = dout("dbg", [128, 65536])

    def build(self):
        with ExitStack() as st:
            self.st = st
            self.c = Ctx(self.nc, st)
            self.alloc()
            self.c_real = self.c
            self.c = DryCtx()
            self.ws.c = self.c
            self.emit()
            self.c = self.c_real
            self.ws.c = self.c
            self.ws.start_real(self.nc)
            self.emit()
            self.c.finish()
        return self.nc

    def dry(self):
        return isinstance(self.c, DryCtx)

    def alloc(self):
        c, cfg = self.c, self.cfg
        L, TM = cfg.depth, cfg.TMAX
        self.tiles = {}
        self.cst = c.sb("cst", [128, NCONST])
        self.cstb = c.sb("cstb", [128, 144 + 6 * 128], BF16)
        self.ws = WStream(c, 8, 4)
        self.ps_dense = [c.ps(f"psd{i}", [128, 512]) for i in range(2)]
        self.ps_o = [c.ps(f"pso{i}", [128, 512]) for i in range(2)]
        self.ps_small = [c.ps(f"pss{i}", [128, 512]) for i in range(2)]
        self.ps_rows = [c.ps(f"psr{i}", [128, 512]) for i in range(2)]
        self.small_res = [[Res(f"pss{i}q{q}") for q in range(4)] for i in range(2)]
        self.n_dense = 0
        self.n_small = 0
        self.n_o = 0
        self.small_pool = self.ps_small
        self.dense_pool = self.ps_dense
        self.h_tok = c.sb("h_tok", [128, cfg.tg, D])
        self.hT = c.sb("hT", [128, 8, TM], BF16)
        self.yT = [c.sb(f"yT{n}", [128, 4, TM], BF16) for n in range(3)]
        self.big = c.sb("big", [128, 12 * TM])
        self.lbt = c.sb("lbt", [128, L, 4])
        self.omlt = c.sb("omlt", [128, L, 4])
        self.nomlt = c.sb("nomlt", [128, L, 4])
        self.gA = c.sb("gA", [128, L, 4])
        self.gB = c.sb("gB", [128, L, 4])
        self.gC = c.sb("gC", [128, L, 4])
        self.cw = c.sb("cw", [128, L, 12, 4])
        self.prow = c.sb("prow", [4, 8 * L])
        nst = 2 * L * 4
        SL = 16 * 129 + 16 * 64
        self.arena = c.sb("arena", [128, max(nst * 128, 2 * SL)])

        class _View:
            def __init__(s_, ap, name):
                s_.ap = ap
                s_.res = Res(name)

            def __getitem__(s_, idx):
                return s_.ap[idx]
        self._View = _View
        self.SA = [[(_View(self.arena[:, (l * 4 + h) * 128:(l * 4 + h + 1) * 128], f"SAf{l}_{h}"), c.sb(f"SAb{l}_{h}", [128, 128], BF16))
                    for h in range(4)] for l in range(L)]
        self.SC = [[(_View(self.arena[:, (L * 4 + l * 4 + h) * 128:(L * 4 + l * 4 + h + 1) * 128], f"SCf{l}_{h}"), c.sb(f"SCb{l}_{h}", [128, 128], BF16))
                    for h in range(4)] for l in range(L)]
        self.SB = [[(c.sb(f"SBf{l}_{h}", [64, 129]), c.sb(f"SBb{l}_{h}", [64, 128], BF16), c.sb(f"SBn{l}_{h}", [64, 4], BF16))
                    for h in range(4)] for l in range(L)]
        self.carryB = [c.sb(f"carryB{l}", [4, 4]) for l in range(L)]
        self.hist = [c.sb(f"hist{l}", [128, 12, 3]) for l in range(L)]
        if cfg.sample:
            self.SsF = [_View(self.arena[:, i * SL:i * SL + 16 * 129].rearrange("p (s n) -> p s n", n=129), f"SsF{i}") for i in range(2)]
            self.SsB = [_View(self.arena[:, i * SL + 16 * 129:(i + 1) * SL].bitcast(BF16).rearrange("p (s n) -> p s n", n=128), f"SsB{i}") for i in range(2)]
            self.SsN = [c.sb(f"SsN{i}", [64, 16, 4], BF16) for i in range(2)]
            self.m0row = c.sb("m0row", [4, 16])
        self.lnp = [c.sb(f"lnp{i}", [128, D // 2]) for i in range(2)]

    def T32(self, role, n=None, p=128):
        key = ("f", role)
        if key not in self.tiles:
            self.tiles[key] = self.c_real.sb("t_" + role, [p, n or self.cfg.TMAX], F32)
        return self.tiles[key]

    def T16(self, role, n=None, p=128):
        key = ("b", role)
        if key not in self.tiles:
            self.tiles[key] = self.c_real.sb("b_" + role, [p, n or self.cfg.TMAX], BF16)
        return self.tiles[key]

    def W(self, i, par):
        return self.T32(f"w{i}_{par}", self.cfg.TMAX + (64 if i == 0 else 0))

    def V(self, i, par):
        return self.T16(f"v{i}_{par}")

    @staticmethod
    def interleave(gens):
        gens = list(gens)
        while gens:
            for gq in list(gens):
                try:
                    next(gq)
                except StopIteration:
                    gens.remove(gq)

    def R32(self, role, n=None):
        return self.T32("r_" + role, n, p=4)

    def dense_bank(self):
        pool = self.dense_pool
        b = pool[self.n_dense % len(pool)]
        self.n_dense += 1
        return b

    def o_bank(self):
        b = self.ps_o[self.n_o % 2]
        self.n_o += 1
        return b

    def small(self):
        pool = self.small_pool
        bank = pool[self.n_small % len(pool)]
        self.n_small += 1
        return bank[:, 0:128], bank

    def cs(self, name, rows=128, bf=False, c0=0, n=None):
        o, w = CL[name]
        if bf:
            o = {"onecol": 0, "ones": 16, "ident": 144, "nident": 272, "posP_incl": 400, "posS_incl": 528,
                 "posP_strict_ts": 656, "posS_strict_ts": 784}[name]
        t = self.cstb if bf else self.cst
        n = w - c0 if n is None else n
        return t[0:rows, o + c0:o + c0 + n]

    def MM(self, out, lhsT, rhs, start=True, stop=True, reads=(), writes=()):
        self.c.op("pe", lambda e: e.matmul(out, lhsT, rhs, start=start, stop=stop), reads, writes)

    def TR(self, out, in_, ident, reads=(), writes=()):
        self.c.op("pe", lambda e: e.transpose(out, in_, ident), reads, writes)

    def ACT(self, out, in_, func, reads=(), writes=(), bias=None, scale=None, accum=None):
        kw = {}
        if bias is not None:
            kw["bias"] = bias
        if scale is not None:
            kw["scale"] = scale
        if accum is not None:
            kw["accum_out"] = accum
        self.c.op("act", lambda e: e.activation(out, in_, func, **kw), reads, writes)

    def TT(self, eng, out, in0, in1, op, reads=(), writes=()):
        self.c.op(eng, lambda e: e.tensor_tensor(out, in0, in1, op), reads, writes)

    def TS(self, eng, out, in0, s1, s2, op0, op1=None, reads=(), writes=()):
        if op1 is None:
            self.c.op(eng, lambda e: e.tensor_scalar(out, in0, s1, None, op0), reads, writes)
        else:
            self.c.op(eng, lambda e: e.tensor_scalar(out, in0, s1, s2, op0, op1), reads, writes)

    def STT(self, out, in0, scalar, in1, op0, op1, reads=(), writes=()):
        self.c.op("dve", lambda e: e.scalar_tensor_tensor(out, in0, scalar, in1, op0, op1), reads, writes)

    def CP(self, eng, out, in_, reads=(), writes=()):
        if eng == "act":
            self.c.op("act", lambda e: e.activation(out, in_, AF.Copy), reads, writes)
        else:
            self.c.op(eng, lambda e: e.tensor_copy(out, in_), reads, writes)

    def SCAN(self, out, d0, d1, init, op0, op1, reads=(), writes=()):
        self.c.op("dve", lambda e: e.tensor_tensor_scan(out, d0, d1, init, op0, op1), reads, writes)

    def RECIP(self, out, in_, reads=(), writes=()):
        self.c.op("dve", lambda e: e.reciprocal(out, in_), reads, writes)

    def MEMSET(self, eng, ap, val, writes=()):
        self.c.op(eng, lambda e: e.memset(ap, val), (), writes)

    def dump(self, name, tile, ap, p, n, q="sp"):
        if not self.cfg.dbg or self.dry():
            return
        off = self.dbg_off
        self.dbg_aps[name] = (off, p, n)
        self.dbg_off += n
        self.c.dma(q, self.dbgt[0:p, off:off + n], ap, reads=[tile])


class DryCtx:
    def op(self, *a, **k):
        pass

    def dma(self, *a, **k):
        return None


class Grp:
    def __init__(self, kind, T, tok0, first, last, idx):
        self.kind, self.T, self.tok0, self.first, self.last, self.idx = kind, T, tok0, first, last, idx
        self.nt = T // 128


def _emit(self):
    cfg = self.cfg
    self.n_dense = self.n_small = self.n_o = 0
    self.setup()
    groups = []
    for i in range(cfg.n_pg):
        groups.append(Grp("p", cfg.TMAX, i * cfg.TMAX, i == 0, i == cfg.n_pg - 1, i))
    if cfg.sample:
        groups.append(Grp("s", 128, 0, True, True, cfg.n_pg))
    for g in groups:
        if g.kind == "s":
            allst = [t for row in self.SA + self.SC for (t, _) in row]
            self.MEMSET("dve", self.arena[:, 0:1], 0.0, writes=allst + self.SsF + self.SsB)
        self.load_x(g)
        for l in range(cfg.depth):
            hasA, hasB, hasC = ("A" in cfg.parts), ("B" in cfg.parts), ("C" in cfg.parts)
            if hasA:
                self.branch_A(g, l)
            gB = self.branch_B(g, l) if hasB else None
            if gB is not None:
                next(gB)
            if hasA:
                self.rms_finish(g, l, 0)
            if gB is not None:
                for _ in gB:
                    pass
            gC = self.branch_C(g, l) if hasC else None
            if gC is not None:
                next(gC)
            if hasB:
                self.rms_finish(g, l, 1, scale_row=self._b_aden)
            if gC is not None:
                for _ in gC:
                    pass
                self.rms_finish(g, l, 2)
            if "M" in cfg.parts:
                self.merge(g, l)
            if "F" in cfg.parts:
                self.ffn(g, l)
        self.store_y(g)


def _setup(self):
    c, cfg, d = self.c, self.cfg, self.d
    L = cfg.depth
    cst, cstb = self.cst, self.cstb
    c.dma("sp", cst[:], d["consts"], writes=[cst])
    o1, _ = CL["onecol"]
    o2, _ = CL["ones"]
    self.CP("dve", cstb[:, 0:16], cst[:, o1:o1 + 16], reads=[cst], writes=[cstb])
    self.CP("dve", cstb[:, 16:144], cst[:, o2:o2 + 128], reads=[cst], writes=[cstb])
    for i, nm in enumerate(("ident", "nident", "posP_incl", "posS_incl", "posP_strict_ts", "posS_strict_ts")):
        o3, _ = CL[nm]
        self.CP("dve", cstb[:, 144 + i * 128:144 + (i + 1) * 128], cst[:, o3:o3 + 128], reads=[cst], writes=[cstb])
    e = self.T32("su_e", 4 * L)
    ev = e[:, 0:4 * L].rearrange("p (l h) -> p l h", l=L)
    c.dma("sp", ev, d["lb_logits"].rearrange("l (h k) -> k l h", k=128), writes=[e], allow_slow_non_contiguous=True)
    self.ACT(e[:, 0:4 * L], e[:, 0:4 * L], AF.Exp, reads=[e], writes=[e])
    s = self.T32("su_s", 4)
    self.CP("dve", s[:, 0:4], ev[:, 0, :], reads=[e], writes=[s])
    for l in range(1, L):
        self.TT("dve", s[:, 0:4], s[:, 0:4], ev[:, l, :], ALU.add, reads=[s, e], writes=[s])
    self.RECIP(s[:, 0:4], s[:, 0:4], reads=[s], writes=[s])
    lbt, omlt, nomlt = self.lbt, self.omlt, self.nomlt
    self.MEMSET("dve", lbt[:, 0, :], 0.0, writes=[lbt])
    for l in range(1, L):
        w = self.T32("su_w", 4)
        self.TT("dve", w[:, 0:4], ev[:, l, :], s[:, 0:4], ALU.mult, reads=[e, s], writes=[w])
        self.TT("dve", lbt[:, l, :], lbt[:, l - 1, :], w[:, 0:4], ALU.add, reads=[lbt, w], writes=[lbt])
    self.TS("dve", omlt[:].rearrange("p l h -> p (l h)"), lbt[:].rearrange("p l h -> p (l h)"), -1.0, 1.0, ALU.mult, ALU.add, reads=[lbt], writes=[omlt])
    self.TS("dve", nomlt[:].rearrange("p l h -> p (l h)"), omlt[:].rearrange("p l h -> p (l h)"), -1.0, None, ALU.mult, reads=[omlt], writes=[nomlt])
    for t, nm in ((self.gA, "a_norm_g"), (self.gB, "b_norm_g"), (self.gC, "c_norm_g")):
        c.dma("sp", t[:], d[nm].rearrange("l (h k) -> k l h", k=128), writes=[t], allow_slow_non_contiguous=True)
    for l in range(L):
        for j in range(4):
            c.dma("sp", self.cw[:, l, :, j], d["conv_w"][l, j].rearrange("(c p) -> p c", p=128), writes=[self.cw], allow_slow_non_contiguous=True)
    pr = self.prow
    for i, nm in enumerate(("b_mi", "b_mf", "dt_bias", "a_log")):
        c.dma("sp", pr[0:4, i * L:(i + 1) * L], d[nm].rearrange("l h -> h l"), writes=[pr], allow_slow_non_contiguous=True)
    self.TS("dve", pr[0:4, L:2 * L], pr[0:4, L:2 * L], -1.0, None, ALU.mult, reads=[pr], writes=[pr])
    self.ACT(pr[0:4, 3 * L:4 * L], pr[0:4, 3 * L:4 * L], AF.Exp, reads=[pr], writes=[pr])
    self.TS("dve", pr[0:4, 3 * L:4 * L], pr[0:4, 3 * L:4 * L], -1.0, None, ALU.mult, reads=[pr], writes=[pr])
    if not self.dry():
        self.ws.emit_conversions()
    for l in range(L):
        for h in range(4):
            for grp in (self.SA, self.SC):
                f, b = grp[l][h]
                self.MEMSET("dve", f[:], 0.0, writes=[f])
                self.MEMSET("dve", b[:], 0.0, writes=[b])
            f, b, n = self.SB[l][h]
            self.MEMSET("dve", f[:], 0.0, writes=[f])
            self.MEMSET("dve", b[:], 0.0, writes=[b])
            self.MEMSET("dve", n[:], 0.0, writes=[n])
        self.MEMSET("dve", self.carryB[l][:], 0.0, writes=[self.carryB[l]])
        self.MEMSET("dve", self.hist[l][:], 0.0, writes=[self.hist[l]])


def _load_x(self, g):
    c, d = self.c, self.d
    src = d["xp"] if g.kind == "p" else d["xs"]
    for j in range(g.nt):
        c.dma("sp", self.h_tok[:, j, :], src[g.tok0 + j * 128:g.tok0 + (j + 1) * 128, :], writes=[self.h_tok])
    self.make_hT(g)


def _make_hT(self, g):
    ident = self.cs("ident")
    for j in range(g.nt):
        for half in range(2):
            bank = self.dense_bank()
            for q in range(4):
                k = half * 4 + q
                self.TR(bank[:, q * 128:(q + 1) * 128], self.h_tok[:, j, k * 128:(k + 1) * 128], ident,
                        reads=[self.h_tok, self.cst], writes=[bank])
            self.CP("act", self.hT[:, half * 4:(half + 1) * 4, j * 128:(j + 1) * 128],
                    bank[:].rearrange("p (a b) -> p a b", a=4), reads=[bank], writes=[self.hT])


def _store_y(self, g):
    c = self.c
    dst = self.o["yp"] if g.kind == "p" else self.o["ys"]
    for j in range(g.nt):
        c.dma("sp", dst[g.tok0 + j * 128:g.tok0 + (j + 1) * 128, :], self.h_tok[:, j, :], reads=[self.h_tok])


def _proj_fm(self, wv, ws, c0, ncol, T, bank=None):
    bank = bank or self.dense_bank()
    for k in range(8):
        self.MM(bank[0:ncol, 0:T], wv[:, k, c0:c0 + ncol], self.hT[:, k, 0:T], start=(k == 0), stop=(k == 7),
                reads=[ws, self.hT], writes=[bank])
    return bank


def _win(self, l, c0, n):
    src = self.d["w_in"][l].rearrange("(k p) n -> p k n", p=128)[:, :, c0:c0 + n]
    return self.ws.get(("w_in", l, c0, n), src, 8, n)


def _proj_tm(self, g, dst, dres, l, c0, ncols):
    for b0 in range(0, ncols, 256):
        wv, ws = self.win(l, c0 + b0, 256)
        for j in range(g.nt):
            bank = self.dense_bank()
            for k in range(8):
                self.MM(bank[:, 0:256], self.hT[:, k, j * 128:(j + 1) * 128], wv[:, k, :], start=(k == 0), stop=(k == 7),
                        reads=[ws, self.hT], writes=[bank])
            self.CP("act", dst[:, j, b0:b0 + 256], bank[:, 0:256], reads=[bank], writes=[dres])


Builder.emit = _emit
Builder.setup = _setup
Builder.load_x = _load_x
Builder.make_hT = _make_hT
Builder.store_y = _store_y
Builder.proj_fm = _proj_fm
Builder.win = _win
Builder.proj_tm = _proj_tm


def _rms_gate(self, g, l, br, h, obank, gs, gtile, first, last):
    T = g.T
    osq = self.T16(f"osq{h % 2}")
    self.ACT(osq[:, 0:T], obank[:, 0:T], AF.Square, reads=[obank], writes=[osq])
    rows = self.ps_rows[1]
    self.MM(rows[0:4, 0:T], self.cs("onecol", bf=True, c0=4 * h, n=4), osq[:, 0:T], start=first, stop=last,
            reads=[osq, self.cstb], writes=[rows])
    t1 = self.T32(f"t1_{h}")
    self.STT(t1[:, 0:T], obank[:, 0:T], gtile[:, l, h:h + 1], gs[:, 0:T], ALU.mult, ALU.mult,
             reads=[obank, gtile, gs], writes=[t1])


def _rms_finish(self, g, l, n, scale_row=None):
    T = g.T
    rows = self.ps_rows[1]
    r = self.R32("rms_r")
    if scale_row is None:
        self.ACT(r[0:4, 0:T], rows[0:4, 0:T], AF.Ln, reads=[rows], writes=[r], scale=1.0 / 128, bias=NORM_EPS)
        self.ACT(r[0:4, 0:T], r[0:4, 0:T], AF.Exp, reads=[r], writes=[r], scale=-0.5)
    else:
        t = self.R32("rms_t")
        self.TT("dve", t[0:4, 0:T], rows[0:4, 0:T], scale_row[0:4, 0:T], ALU.mult, reads=[rows, scale_row], writes=[t])
        self.TT("dve", t[0:4, 0:T], t[0:4, 0:T], scale_row[0:4, 0:T], ALU.mult, reads=[t, scale_row], writes=[t])
        self.ACT(r[0:4, 0:T], t[0:4, 0:T], AF.Ln, reads=[t], writes=[r], scale=1.0 / 128, bias=NORM_EPS)
        self.ACT(r[0:4, 0:T], r[0:4, 0:T], AF.Exp, reads=[r], writes=[r], scale=-0.5)
        self.TT("dve", r[0:4, 0:T], r[0:4, 0:T], scale_row[0:4, 0:T], ALU.mult, reads=[r, scale_row], writes=[r])
    for h in range(4):
        bank = self.dense_bank()
        self.MM(bank[:, 0:T], self.cs("sel", rows=4, c0=h * 128, n=128), r[0:4, 0:T], reads=[self.cst, r], writes=[bank])
        t1 = self.T32(f"t1_{h}")
        self.TT("dve", self.yT[n][:, h, 0:T], t1[:, 0:T], bank[:, 0:T], ALU.mult, reads=[t1, bank], writes=[self.yT[n]])
        self.dump(f"y{n}_{h}_l{l}_g{g.idx}", self.yT[n], self.yT[n][:, h, 0:T], 128, T, q="pool")


def _branch_A(self, g, l):
    c, cfg, d = self.c, self.cfg, self.d
    T, nt = g.T, g.nt
    samp = g.kind == "s"
    seglen = 8 if samp else 64
    spt = 128 // seglen
    nseg = T // seglen
    maskA = self.cs("mask8" if samp else "maskA64")
    rst = self.cs("rst8" if samp else "rst64", n=T)
    ident = self.cs("ident")
    rm = self.cs("rm16" if samp else "rm2")
    vA = self.T16("vtok", 512 * cfg.tg)
    vA3 = vA[:, :].rearrange("p (j n) -> p j n", n=512)
    self.proj_tm(g, vA3, vA, l, 1024, 512)
    for p in range(2):
        wq, wqs = self.win(l, 0 + p * 256, 256)
        wg, wgs = self.win(l, 1536 + p * 256, 256)
        wf, wfs = self.win(l, 512 + p * 256, 256)
        def head(hh):
            h = 2 * p + hh
            c0 = hh * 128
            hs = [0]

            def hsmall():
                pool = (self.ps_small[hh], self.ps_dense[hh])
                b_ = pool[hs[0] % 2]
                hs[0] += 1
                return b_[:, 0:128], b_
            if samp:
                c.dma("sp", self.SsF[hh][:, :, 0:128], d["st_a"][l, :, h].rearrange("s k v -> k s v"), writes=[self.SsF[hh]])
                self.CP("act", self.SsB[hh][:, :, :], self.SsF[hh][:, :, 0:128], reads=[self.SsF[hh]], writes=[self.SsB[hh]])
                yield
            psq = self.proj_fm(wq, wqs, c0, 128, T, bank=self.ps_dense[hh])
            qs = self.W(0, hh)
            self.ACT(qs[:, 0:T], psq[:, 0:T], AF.Silu, reads=[psq], writes=[qs])
            yield
            psg = self.proj_fm(wg, wgs, c0, 128, T, bank=self.ps_dense[hh])
            gs = self.W(7, hh)
            self.ACT(gs[:, 0:T], psg[:, 0:T], AF.Silu, reads=[psg], writes=[gs])
            yield
            psf = self.proj_fm(wf, wfs, c0, 128, T, bank=self.ps_dense[hh])
            sig = self.W(1, hh)
            self.ACT(sig[:, 0:T], psf[:, 0:T], AF.Sigmoid, reads=[psf], writes=[sig])
            yield
            f = self.W(2, hh)
            self.TS("dve", f[:, 0:T], sig[:, 0:T], self.omlt[:, l, h:h + 1], self.lbt[:, l, h:h + 1], ALU.mult, ALU.add,
                    reads=[sig, self.omlt, self.lbt], writes=[f])
            kk = self.W(3, hh)
            self.TS("dve", kk[:, 0:T], sig[:, 0:T], self.nomlt[:, l, h:h + 1], self.omlt[:, l, h:h + 1], ALU.mult, ALU.add,
                    reads=[sig, self.omlt, self.nomlt], writes=[kk])
            self.ACT(f[:, 0:T], f[:, 0:T], AF.Ln, reads=[f], writes=[f])
            yield
            b = self.W(4, hh)
            self.SCAN(b[:, 0:T], rst, f[:, 0:T], 0.0, ALU.mult, ALU.add, reads=[self.cst, f], writes=[b])
            yield
            E1 = sig
            self.ACT(E1[:, 0:T], b[:, 0:T], AF.Exp, reads=[b], writes=[E1])
            yield
            En = f
            self.ACT(En[:, 0:T], b[:, 0:T], AF.Exp, reads=[b], writes=[En], scale=-1.0)
            yield
            q2 = self.V(0, hh)
            self.TT("dve", q2[:, 0:T], qs[:, 0:T], E1[:, 0:T], ALU.mult, reads=[qs, E1], writes=[q2])
            yield
            kt = self.V(1, hh)
            self.TT("dve", kt[:, 0:T], kk[:, 0:T], En[:, 0:T], ALU.mult, reads=[kk, En], writes=[kt])
            yield
            b3 = b[:, 0:T].rearrange("p (s c) -> p s c", c=seglen)
            bl = b[:, 0:T].rearrange("p (s c) -> p s c", c=seglen)[:, :, seglen - 1:seglen]
            dd = self.W(5, hh)
            dd3 = dd[:, 0:T].rearrange("p (s c) -> p s c", c=seglen)
            self.TT("dve", dd3, b3, bl.to_broadcast([128, nseg, seglen]), ALU.subtract, reads=[b], writes=[dd])
            yield
            self.ACT(dd[:, 0:T], dd[:, 0:T], AF.Exp, reads=[dd], writes=[dd], scale=-1.0)
            yield
            khT = self.W(6, hh)
            self.TT("dve", khT[:, 0:T], kk[:, 0:T], dd[:, 0:T], ALU.mult, reads=[kk, dd], writes=[khT])
            yield
            ob = self.ps_o[hh]
            for j in range(nt):
                jc = slice(j * 128, (j + 1) * 128)
                pt, ptr = hsmall()
                self.TR(pt, khT[:, jc], ident, reads=[khT, self.cst], writes=[ptr])
                khs = self.T16(f"khsb_{hh}", 128)
                self.CP("dve", khs[:, :], pt, reads=[ptr], writes=[khs])
                yield
                pa, par = hsmall()
                self.MM(pa, kt[:, jc], q2[:, jc], reads=[kt, q2], writes=[par])
                attm = self.T16(f"c_attm_{hh}", 128)
                self.TT("dve", attm[:, :], pa, maskA, ALU.mult, reads=[par, self.cst], writes=[attm])
                yield
                vh = vA3[:, j, h * 128:(h + 1) * 128]
                self.MM(ob[:, jc], vh, attm[:, :], start=True, stop=False, reads=[vA, attm], writes=[ob])
                for i in range(spt):
                    seg = j * spt + i
                    sc = slice(seg * seglen, (seg + 1) * seglen)
                    if samp:
                        Sf, Sb, Sfr, Sbr = self.SsF[hh][:, seg, 0:128], self.SsB[hh][:, seg, :], self.SsF[hh], self.SsB[hh]
                    else:
                        Sft, Sbt = self.SA[l][h]
                        Sf, Sb, Sfr, Sbr = Sft[:, :], Sbt[:, :], Sft, Sbt
                    self.MM(ob[:, sc], Sb, q2[:, sc], start=False, stop=(i == spt - 1), reads=[Sbr, q2], writes=[ob])
                    khm = self.T16(f"khseg{i % 2}_{hh}", 128)
                    self.TS("dve", khm[:, :], khs[:, :], rm[:, i:i + 1], None, ALU.mult, reads=[khs, self.cst], writes=[khm])
                    pS, pSr = hsmall()
                    self.MM(pS, khm[:, :], vh, reads=[khm, vA], writes=[pSr])
                    self.STT(Sf, Sf, E1[:, (seg + 1) * seglen - 1:(seg + 1) * seglen], pS, ALU.mult, ALU.add,
                             reads=[Sfr, E1, pSr], writes=[Sfr])
                    if not samp:
                        self.CP("act", Sb, Sf, reads=[Sfr], writes=[Sbr])
                        yield
            self.rms_gate(g, l, 0, h, ob, gs, self.gA, first=(h == 0), last=(h == 3))
            if samp:
                c.dma("sp", self.o["oa_s"][l, :, h].rearrange("s k v -> k s v"), self.SsF[hh][:, :, 0:128], reads=[self.SsF[hh]])
            elif g.last:
                c.dma("sp", self.o["oa_p"][l, h], self.SA[l][h][0][:, :], reads=[self.SA[l][h][0]])
        self.interleave([head(0), head(1)])


Builder.rms_gate = _rms_gate
Builder.rms_finish = _rms_finish
Builder.branch_A = _branch_A


def _branch_B(self, g, l):
    c, cfg, d = self.c, self.cfg, self.d
    L = cfg.depth
    T, nt = g.T, g.nt
    samp = g.kind == "s"
    seglen = 8 if samp else 128
    spt = 128 // seglen
    nseg = T // seglen
    ident = self.cs("ident")
    pos = self.cs("posS_incl" if samp else "posP_incl")
    rm = self.cs("rm16")
    i4 = self.cs("i4", rows=4)
    pr = self.prow
    w4, w4s = self.win(l, 3584, 8)
    bank = self.dense_bank()
    for k in range(8):
        self.MM(bank[0:4, 0:T], w4[:, k, 0:4], self.hT[:, k, 0:T], start=(k == 0), stop=(k == 7), reads=[w4s, self.hT], writes=[bank])
    bi = self.R32("b_bi")
    self.ACT(bi[0:4, 0:T], bank[0:4, 0:T], AF.Identity, reads=[bank, pr], writes=[bi], bias=pr[0:4, l:l + 1])
    bank = self.dense_bank()
    for k in range(8):
        self.MM(bank[0:4, 0:T], w4[:, k, 4:8], self.hT[:, k, 0:T], start=(k == 0), stop=(k == 7), reads=[w4s, self.hT], writes=[bank])
    sp = self.R32("b_sp")
    self.ACT(sp[0:4, 0:T], bank[0:4, 0:T], AF.Exp, reads=[bank, pr], writes=[sp], scale=-1.0, bias=pr[0:4, L + l:L + l + 1])
    self.ACT(sp[0:4, 0:T], sp[0:4, 0:T], AF.Ln, reads=[sp], writes=[sp], bias=1.0)
    Bn = self.R32("b_Bn")
    cb = self.carryB[l]
    if samp:
        self.SCAN(Bn[0:4, 0:T], self.cs("rst8", rows=4, n=T), sp[0:4, 0:T], 0.0, ALU.mult, ALU.add, reads=[self.cst, sp], writes=[Bn])
    else:
        self.SCAN(Bn[0:4, 0:T], self.cs("ones", rows=4, n=T), sp[0:4, 0:T], cb[0:4, 0:1], ALU.mult, ALU.add,
                  reads=[self.cst, sp, cb], writes=[Bn])
    u = self.R32("b_u")
    self.TT("dve", u[0:4, 0:T], bi[0:4, 0:T], Bn[0:4, 0:T], ALU.add, reads=[bi, Bn], writes=[u])
    mu = self.R32("b_mu")
    si = self.R32("b_si", 16)
    if samp:
        c.dma("sp", self.m0row[0:4, 0:16], d["st_m"][l].rearrange("s h -> h s"), writes=[self.m0row], allow_slow_non_contiguous=True)
        for s in range(16):
            sl = slice(s * 8, (s + 1) * 8)
            self.SCAN(mu[0:4, sl], u[0:4, sl], u[0:4, sl], self.m0row[0:4, s:s + 1], ALU.max, ALU.max, reads=[u, self.m0row], writes=[mu])
        self.CP("dve", si[0:4, 0:16], self.m0row[0:4, 0:16], reads=[self.m0row], writes=[si])
    else:
        self.SCAN(mu[0:4, 0:T], u[0:4, 0:T], u[0:4, 0:T], cb[0:4, 1:2], ALU.max, ALU.max, reads=[u, cb], writes=[mu])
        self.CP("dve", si[0:4, 0:1], cb[0:4, 1:2], reads=[cb], writes=[si])
        for j in range(1, nt):
            self.CP("dve", si[0:4, j:j + 1], mu[0:4, j * 128 - 1:j * 128], reads=[mu], writes=[si])
    mu3 = mu[0:4, 0:T].rearrange("p (s c) -> p s c", c=seglen)
    u3 = u[0:4, 0:T].rearrange("p (s c) -> p s c", c=seglen)
    wint = self.R32("b_wint")
    wint3 = wint[0:4, 0:T].rearrange("p (s c) -> p s c", c=seglen)
    self.TT("dve", wint3, mu3, si[0:4, 0:nseg].unsqueeze(2).to_broadcast([4, nseg, seglen]), ALU.subtract, reads=[mu, si], writes=[wint])
    self.ACT(wint[0:4, 0:T], wint[0:4, 0:T], AF.Exp, reads=[wint], writes=[wint], scale=-1.0)
    mt = self.R32("b_mt")
    self.TT("dve", mt[0:4, 0:T], mu[0:4, 0:T], Bn[0:4, 0:T], ALU.subtract, reads=[mu, Bn], writes=[mt])
    emt = self.R32("b_emt")
    self.ACT(emt[0:4, 0:T], mt[0:4, 0:T], AF.Exp, reads=[mt], writes=[emt], scale=-1.0)
    muend = mu3[:, :, seglen - 1:seglen]
    wk = self.R32("b_wk")
    wk3 = wk[0:4, 0:T].rearrange("p (s c) -> p s c", c=seglen)
    self.TT("dve", wk3, u3, muend.to_broadcast([4, nseg, seglen]), ALU.subtract, reads=[u, mu], writes=[wk])
    self.ACT(wk[0:4, 0:T], wk[0:4, 0:T], AF.Exp, reads=[wk], writes=[wk])
    wold = self.R32("b_wold", 16)
    self.TT("dve", wold[0:4, 0:nseg].unsqueeze(2), si[0:4, 0:nseg].unsqueeze(2), muend, ALU.subtract, reads=[si, mu], writes=[wold])
    self.ACT(wold[0:4, 0:nseg], wold[0:4, 0:nseg], AF.Exp, reads=[wold], writes=[wold])
    if samp:
        c.dma("sp", self.o["om_s"][l].rearrange("s h -> h s"), mt[0:4, 0:T].rearrange("p (s c) -> p s c", c=8)[:, :, 7],
              reads=[mt], allow_slow_non_contiguous=True)
    else:
        if g.last:
            c.dma("sp", self.o["om_p"][l].rearrange("(h o) -> h o", o=1), mt[0:4, T - 1:T], reads=[mt])
        self.CP("dve", cb[0:4, 0:1], Bn[0:4, T - 1:T], reads=[Bn], writes=[cb])
        self.CP("dve", cb[0:4, 1:2], mu[0:4, T - 1:T], reads=[mu], writes=[cb])
    vB = self.T16("vtok", 512 * cfg.tg)
    vB3 = vB[:, :].rearrange("p (j n) -> p j n", n=512)
    self.proj_tm(g, vB3, vB, l, 2560, 512)
    kB = self.T16("b_ktok", 256 * cfg.tg)
    kB3 = kB[:, :].rearrange("p (j n) -> p j n", n=256)
    self.proj_tm(g, kB3, kB, l, 2304, 256)
    cols = self.T32("b_cols", 8 * cfg.tg)
    for j in range(nt):
        jc = slice(j * 128, (j + 1) * 128)
        pc, pcr = self.small()
        self.MM(pc[:, 0:4], u[0:4, jc], i4, reads=[u, self.cst], writes=[pcr])
        self.MM(pc[:, 4:8], wk[0:4, jc], i4, reads=[wk, self.cst], writes=[pcr])
        self.CP("dve", cols[:, j * 8:(j + 1) * 8], pc[:, 0:8], reads=[pcr], writes=[cols])
    wq, wqs = self.win(l, 2048, 256)
    wk_, wks = self.win(l, 2304, 256)
    rowsA = self.ps_rows[0]
    yield "pre"
    for p in range(2):
        wo, wos = self.win(l, 3072 + p * 256, 256)

        def head(hh):
            h = 2 * p + hh
            hs = [0]

            def hsmall():
                pool = (self.ps_small[hh], self.ps_dense[hh])
                b_ = pool[hs[0] % 2]
                hs[0] += 1
                return b_[:, 0:128], b_
            if samp:
                for s4 in range(4):
                    cld = self.T32(f"b_cld_{hh}", 4 * 64)
                    cld3 = cld[:, :].rearrange("p (s k) -> p s k", k=64)
                    c.dma("sp", cld3, d["st_c"][l, s4 * 4:(s4 + 1) * 4, h].rearrange("s v k -> v s k"), writes=[cld])
                    bk_ = self.ps_dense[hh]
                    for q in range(4):
                        self.TR(bk_[0:64, q * 128:(q + 1) * 128], cld3[:, q, :], ident, reads=[cld, self.cst], writes=[bk_])
                    self.CP("act", self.SsF[hh][0:64, s4 * 4:(s4 + 1) * 4, 0:128], bk_[0:64, :].rearrange("p (a b) -> p a b", a=4), reads=[bk_], writes=[self.SsF[hh]])
                    yield
                nld = self.T32(f"b_nld_{hh}", 64, p=16)
                c.dma("sp", nld[0:16, 0:64], d["st_n"][l, :, h, :], writes=[nld])
                bk_ = self.ps_dense[hh]
                self.TR(bk_[0:64, 0:16], nld[0:16, 0:64], ident[0:16, 0:16], reads=[nld, self.cst], writes=[bk_])
                self.CP("dve", self.SsF[hh][0:64, :, 128], bk_[0:64, 0:16], reads=[bk_], writes=[self.SsF[hh]])
                yield
                self.CP("act", self.SsB[hh][0:64, :, :], self.SsF[hh][0:64, :, 0:128], reads=[self.SsF[hh]], writes=[self.SsB[hh]])
                yield
                self.MEMSET("dve", self.SsN[hh][:], 0.0, writes=[self.SsN[hh]])
                self.CP("dve", self.SsN[hh][0:64, :, h], self.SsF[hh][0:64, :, 128], reads=[self.SsF[hh]], writes=[self.SsN[hh]])
                yield
            psq = self.proj_fm(wq, wqs, h * 64, 64, T, bank=self.ps_dense[hh])
            qT = self.T16(f"b_qT_{hh}", p=64)
            self.ACT(qT[0:64, 0:T], psq[0:64, 0:T], AF.Copy, reads=[psq], writes=[qT], scale=0.125)
            yield
            psk = self.proj_fm(wk_, wks, h * 64, 64, T, bank=self.ps_dense[hh])
            kT = self.T16(f"b_kT_{hh}", p=64)
            self.CP("act", kT[0:64, 0:T], psk[0:64, 0:T], reads=[psk], writes=[kT])
            yield
            bcb = self.ps_dense[hh]
            self.MM(bcb[:, 0:T], self.cs("sel", rows=4, c0=h * 128, n=128), wint[0:4, 0:T], reads=[self.cst, wint], writes=[bcb])
            qp = self.T16(f"b_qp_{hh}", p=64)
            self.TT("dve", qp[0:64, 0:T], qT[0:64, 0:T], bcb[0:64, 0:T], ALU.mult, reads=[qT, bcb], writes=[qp])
            yield
            bcw = self.ps_dense[hh]
            self.MM(bcw[:, 0:nseg], self.cs("sel", rows=4, c0=h * 128, n=128), wold[0:4, 0:nseg], reads=[self.cst, wold], writes=[bcw])
            woldbc = self.T32(f"b_woldbc_{hh}", 16)
            self.CP("dve", woldbc[:, 0:nseg], bcw[:, 0:nseg], reads=[bcw], writes=[woldbc])
            yield
            pso = self.proj_fm(wo, wos, (h % 2) * 128, 128, T, bank=self.ps_dense[hh])
            gs = self.W(7, hh)
            self.ACT(gs[:, 0:T], pso[:, 0:T], AF.Sigmoid, reads=[pso], writes=[gs])
            yield
            ob = self.ps_o[hh]
            for j in range(nt):
                jc = slice(j * 128, (j + 1) * 128)
                X, Xr = (self.ps_small[hh][:, 0:128], self.ps_small[hh])
                self.MM(X, self.cs("sel", rows=4, c0=h * 128, n=128), mu[0:4, jc], start=True, stop=False, reads=[self.cst, mu], writes=[Xr])
                self.MM(X, self.cs("ident", bf=True), self.cs("posS_incl" if samp else "posP_incl", bf=True), start=False, stop=True, reads=[self.cstb], writes=[Xr])
                E = self.T32(f"b_E_{hh}", 128)
                self.ACT(E[:, :], X, AF.Exp, reads=[Xr, cols], writes=[E], scale=-1.0, bias=cols[:, j * 8 + h:j * 8 + h + 1])
                yield
                KQ, KQr = (self.ps_dense[hh][:, 0:128], self.ps_dense[hh])
                self.MM(KQ, kT[0:64, jc], qT[0:64, jc], reads=[kT, qT], writes=[KQr])
                scm = self.T16(f"b_scm_{hh}", 128)
                self.TT("dve", scm[:, :], KQ, E[:, :], ALU.mult, reads=[KQr, E], writes=[scm])
                yield
                vh = vB3[:, j, h * 128:(h + 1) * 128]
                self.MM(ob[:, jc], vh, scm[:, :], start=True, stop=False, reads=[vB, scm], writes=[ob])
                self.MM(rowsA[0:4, jc], self.cs("onecol", bf=True, c0=4 * h, n=4), scm[:, :], start=(h == 0 and j == 0), stop=False,
                        reads=[self.cstb, scm], writes=[rowsA])
                wkv = self.T16(f"b_wkv_{hh}", 132)
                self.TS("dve", wkv[:, 0:128], vh, cols[:, j * 8 + 4 + h:j * 8 + 5 + h], None, ALU.mult, reads=[vB, cols], writes=[wkv])
                yield
                self.CP("dve", wkv[:, 128:129], cols[:, j * 8 + 4 + h:j * 8 + 5 + h], reads=[cols], writes=[wkv])
                yield
                for i in range(spt):
                    seg = j * spt + i
                    sc = slice(seg * seglen, (seg + 1) * seglen)
                    if samp:
                        Cf, Cb, Cn = self.SsF[hh][0:64, seg, :], self.SsB[hh][0:64, seg, :], self.SsN[hh][0:64, seg, :]
                        Cfr, Cbr, Cnr = self.SsF[hh], self.SsB[hh], self.SsN[hh]
                    else:
                        Cft, Cbt, Cnt = self.SB[l][h]
                        Cf, Cb, Cn, Cfr, Cbr, Cnr = Cft[:, :], Cbt[:, :], Cnt[:, :], Cft, Cbt, Cnt
                    last = (i == spt - 1)
                    self.MM(ob[:, sc], Cb, qp[0:64, sc], start=False, stop=last, reads=[Cbr, qp], writes=[ob])
                    self.MM(rowsA[0:4, sc], Cn, qp[0:64, sc], start=False, stop=(last and h == 3 and j == nt - 1), reads=[Cnr, qp], writes=[rowsA])
                    if spt > 1:
                        kkm = self.T16(f"b_kkm_{hh}", 64)
                        self.TS("dve", kkm[:, :], kB3[:, j, h * 64:(h + 1) * 64], rm[:, i:i + 1], None, ALU.mult, reads=[kB, self.cst], writes=[kkm])
                        yield
                        kk_ap, kk_r = kkm[:, :], kkm
                    else:
                        kk_ap, kk_r = kB3[:, j, h * 64:(h + 1) * 64], kB
                    pS = self.ps_dense[hh]
                    self.MM(pS[0:64, 0:129], kk_ap, wkv[:, 0:129], reads=[kk_r, wkv], writes=[pS])
                    self.STT(Cf, Cf, woldbc[0:64, seg:seg + 1], pS[0:64, 0:129], ALU.mult, ALU.add, reads=[Cfr, woldbc, pS], writes=[Cfr])
                    yield
                    if not samp:
                        self.CP("act", Cb, Cf[:, 0:128], reads=[Cfr], writes=[Cbr])
                        yield
                        self.CP("act", Cn[:, h:h + 1], Cf[:, 128:129], reads=[Cfr], writes=[Cnr])
                        yield
            self.rms_gate(g, l, 1, h, ob, gs, self.gB, first=(h == 0), last=(h == 3))
            if samp:
                for s4 in range(4):
                    cld = self.T32(f"b_cld_{hh}", 4 * 64)
                    cld3 = cld[:, :].rearrange("p (s k) -> p s k", k=64)
                    bk_ = self.ps_dense[hh]
                    for q in range(4):
                        self.TR(bk_[:, q * 64:(q + 1) * 64], self.SsF[hh][0:64, s4 * 4 + q, 0:128], ident[0:64, 0:64], reads=[self.SsF[hh], self.cst], writes=[bk_])
                    self.CP("act", cld3, bk_[:, 0:256].rearrange("p (a b) -> p a b", a=4), reads=[bk_], writes=[cld])
                    yield
                    c.dma("sp", self.o["oc_s"][l, s4 * 4:(s4 + 1) * 4, h].rearrange("s v k -> v s k"), cld3, reads=[cld])
                nst_ = self.T32(f"b_nst_{hh}", 16, p=64)
                self.CP("dve", nst_[0:64, 0:16], self.SsF[hh][0:64, :, 128], reads=[self.SsF[hh]], writes=[nst_])
                yield
                bk_ = self.ps_dense[hh]
                self.TR(bk_[0:16, 0:64], nst_[0:64, 0:16], ident[0:64, 0:64], reads=[nst_, self.cst], writes=[bk_])
                nld = self.T32(f"b_nld_{hh}", 64, p=16)
                self.CP("dve", nld[0:16, 0:64], bk_[0:16, 0:64], reads=[bk_], writes=[nld])
                yield
                c.dma("sp", self.o["on_s"][l, :, h, :], nld[0:16, 0:64], reads=[nld])
            elif g.last:
                Cft = self.SB[l][h][0]
                bk_ = self.ps_dense[hh]
                self.TR(bk_[:, 0:64], Cft[0:64, 0:128], ident[0:64, 0:64], reads=[Cft, self.cst], writes=[bk_])
                co = self.T32(f"b_co_{hh}", 64)
                self.CP("act", co[:, 0:64], bk_[:, 0:64], reads=[bk_], writes=[co])
                yield
                c.dma("sp", self.o["oc_p"][l, h], co[:, 0:64], reads=[co])
                c.dma("sp", self.o["on_p"][l, h].rearrange("(k o) -> k o", o=1), Cft[0:64, 128:129], reads=[Cft])
        self.interleave([head(0), head(1)])
    aden = self.R32("b_aden")
    self.ACT(aden[0:4, 0:T], rowsA[0:4, 0:T], AF.Abs, reads=[rowsA], writes=[aden])
    self.TT("dve", aden[0:4, 0:T], aden[0:4, 0:T], emt[0:4, 0:T], ALU.max, reads=[aden, emt], writes=[aden])
    self.RECIP(aden[0:4, 0:T], aden[0:4, 0:T], reads=[aden], writes=[aden])
    self._b_aden = aden


def _ones_row(self, T):
    return self.cs("ones", rows=4, n=128) if T <= 128 else self.onesrow[0:4, 0:T]


Builder.branch_B = _branch_B
Builder.ones_row = _ones_row


def _branch_C(self, g, l):
    c, cfg, d = self.c, self.cfg, self.d
    L = cfg.depth
    T, nt = g.T, g.nt
    samp = g.kind == "s"
    seglen = 8 if samp else 128
    spt = 128 // seglen
    nseg = T // seglen
    nsteps = 2 if samp else 6
    ident = self.cs("ident")
    nident = self.cs("nident")
    pos_strict = self.cs("posS_strict_ts" if samp else "posP_strict_ts")
    pos_incl = self.cs("posS_incl" if samp else "posP_incl")
    rm = self.cs("rm16")
    i4 = self.cs("i4", rows=4)
    ones_bf = self.cs("ones", bf=True, n=128)
    identb = self.cs("ident", bf=True)
    nidentb = self.cs("nident", bf=True)
    pos_strictb = self.cs("posS_strict_ts" if samp else "posP_strict_ts", bf=True)
    pos_inclb = self.cs("posS_incl" if samp else "posP_incl", bf=True)
    pr = self.prow
    cseg, clen = (16, 8) if samp else (1, T)
    XW = cseg * (clen + 3)
    w5, w5s = self.win(l, 5640, 8)
    bank = self.dense_bank()
    for k in range(8):
        self.MM(bank[0:4, 0:T], w5[:, k, 0:4], self.hT[:, k, 0:T], start=(k == 0), stop=(k == 7), reads=[w5s, self.hT], writes=[bank])
    beta = self.R32("b_bi")
    self.ACT(beta[0:4, 0:T], bank[0:4, 0:T], AF.Sigmoid, reads=[bank], writes=[beta])
    bank = self.dense_bank()
    for k in range(8):
        self.MM(bank[0:4, 0:T], w5[:, k, 4:8], self.hT[:, k, 0:T], start=(k == 0), stop=(k == 7), reads=[w5s, self.hT], writes=[bank])
    gg = self.R32("b_sp")
    self.ACT(gg[0:4, 0:T], bank[0:4, 0:T], AF.Exp, reads=[bank, pr], writes=[gg], bias=pr[0:4, 2 * L + l:2 * L + l + 1])
    self.ACT(gg[0:4, 0:T], gg[0:4, 0:T], AF.Ln, reads=[gg], writes=[gg], bias=1.0)
    self.TS("dve", gg[0:4, 0:T], gg[0:4, 0:T], pr[0:4, 3 * L + l:3 * L + l + 1], None, ALU.mult, reads=[gg, pr], writes=[gg])
    b = self.R32("b_Bn")
    self.SCAN(b[0:4, 0:T], self.cs("rst8" if samp else "rst128", rows=4, n=T), gg[0:4, 0:T], 0.0, ALU.mult, ALU.add,
              reads=[self.cst, gg], writes=[b])
    nb = self.R32("b_u")
    self.TS("dve", nb[0:4, 0:T], b[0:4, 0:T], -1.0, None, ALU.mult, reads=[b], writes=[nb])
    nbeta = self.R32("b_mu")
    self.TS("dve", nbeta[0:4, 0:T], beta[0:4, 0:T], -1.0, None, ALU.mult, reads=[beta], writes=[nbeta])
    eb = self.R32("b_wint")
    self.ACT(eb[0:4, 0:T], b[0:4, 0:T], AF.Exp, reads=[b], writes=[eb])
    bebe = self.R32("b_mt")
    self.TT("dve", bebe[0:4, 0:T], beta[0:4, 0:T], eb[0:4, 0:T], ALU.mult, reads=[beta, eb], writes=[bebe])
    ebl = self.R32("b_emt")
    b3 = b[0:4, 0:T].rearrange("p (s c) -> p s c", c=seglen)
    self.TT("dve", ebl[0:4, 0:T].rearrange("p (s c) -> p s c", c=seglen), b3, b3[:, :, seglen - 1:seglen].to_broadcast([4, nseg, seglen]),
            ALU.subtract, reads=[b], writes=[ebl])
    self.ACT(ebl[0:4, 0:T], ebl[0:4, 0:T], AF.Exp, reads=[ebl], writes=[ebl], scale=-1.0)
    eblast = self.R32("b_wold", 16)
    self.CP("dve", eblast[0:4, 0:nseg].unsqueeze(2), eb[0:4, 0:T].rearrange("p (s c) -> p s c", c=seglen)[:, :, seglen - 1:seglen],
            reads=[eb], writes=[eblast])
    cols = self.T32("c_cols", 24 * cfg.tg)
    rowlist = (b, nb, nbeta, beta, bebe, ebl)
    for j in range(nt):
        jc = slice(j * 128, (j + 1) * 128)
        pc, pcr = self.small()
        for qi, rw in enumerate(rowlist):
            self.MM(pc[:, qi * 4:(qi + 1) * 4], rw[0:4, jc], i4, reads=[rw, self.cst], writes=[pcr])
        self.CP("dve", cols[:, j * 24:(j + 1) * 24], pc[:, 0:24], reads=[pcr], writes=[cols])

    def col(j, qi, h):
        o = j * 24 + qi * 4 + h
        return cols[:, o:o + 1]
    yield "pre"
    for p in range(2):
        wcq, wcqs = self.win(l, 3592 + p * 256, 256)
        wck, wcks = self.win(l, 4104 + p * 256, 256)
        wcv, wcvs = self.win(l, 4616 + p * 256, 256)
        wcg, wcgs = self.win(l, 5128 + p * 256, 256)
        def head(hh):
            h = 2 * p + hh
            c0 = hh * 128
            hs = [0]

            def hsmall():
                pool = (self.ps_small[hh], self.ps_dense[hh])
                b_ = pool[hs[0] % 2]
                hs[0] += 1
                return b_[:, 0:128], b_
            if samp:
                c.dma("sp", self.SsF[hh][:, :, 0:128], d["st_g"][l, :, h].rearrange("s k v -> k s v"), writes=[self.SsF[hh]])
                self.CP("act", self.SsB[hh][:, :, :], self.SsF[hh][:, :, 0:128], reads=[self.SsF[hh]], writes=[self.SsB[hh]])
                yield
            outs = []
            for ci, (wv_, ws_) in enumerate(((wcq, wcqs), (wck, wcks), (wcv, wcvs))):
                chunk = ci * 4 + h
                ps = self.proj_fm(wv_, ws_, c0, 128, T, bank=self.ps_dense[hh])
                xe = self.W(0, hh)
                xe3 = xe[:, 0:XW].rearrange("p (s c) -> p s c", c=clen + 3)
                self.CP("act", xe3[:, :, 3:3 + clen], ps[:, 0:T].rearrange("p (s c) -> p s c", c=clen), reads=[ps], writes=[xe])
                yield
                if samp:
                    cvs = self.T32(f"c_cvs_{hh}", 128, p=48)
                    c.dma("sp", cvs[0:48, :], d["st_v"][l].rearrange("s j c -> (s j) c")[:, chunk * 128:(chunk + 1) * 128], writes=[cvs])
                    bk_ = self.ps_dense[hh]
                    self.TR(bk_[:, 0:48], cvs[0:48, 0:128], ident[0:48, 0:48], reads=[cvs, self.cst], writes=[bk_])
                    self.CP("dve", xe3[:, :, 0:3], bk_[:, 0:48].rearrange("p (s c) -> p s c", c=3), reads=[bk_], writes=[xe])
                    yield
                else:
                    self.CP("dve", xe3[:, :, 0:3], self.hist[l][:, chunk, :].unsqueeze(1), reads=[self.hist[l]], writes=[xe])
                    yield
                    self.CP("dve", self.hist[l][:, chunk, :].unsqueeze(1), xe3[:, :, clen:clen + 3], reads=[xe], writes=[self.hist[l]])
                    yield
                if samp or g.last:
                    nrow = 48 if samp else 3
                    xl = self.T32(f"c_xl_{hh}", 48)
                    self.CP("dve", xl[:, 0:nrow].rearrange("p (s c) -> p s c", c=3), xe3[:, :, clen:clen + 3], reads=[xe], writes=[xl])
                    yield
                    bk_ = self.ps_dense[hh]
                    self.TR(bk_[0:nrow, 0:128], xl[:, 0:nrow], ident, reads=[xl, self.cst], writes=[bk_])
                    cvo = self.T32(f"c_cvo_{hh}", 128, p=48)
                    self.CP("act", cvo[0:nrow, 0:128], bk_[0:nrow, 0:128], reads=[bk_], writes=[cvo])
                    yield
                    if samp:
                        c.dma("sp", self.o["ov_s"][l].rearrange("s j c -> (s j) c")[:, chunk * 128:(chunk + 1) * 128], cvo[0:48, 0:128], reads=[cvo])
                    else:
                        c.dma("sp", self.o["ov_p"][l][:, chunk * 128:(chunk + 1) * 128], cvo[0:3, 0:128], reads=[cvo])
                acc = self.W(1 + ci, hh)
                acc3 = acc[:, 0:T].rearrange("p (s c) -> p s c", c=clen)
                cw = self.cw
                self.TS("dve", acc3, xe3[:, :, 3:3 + clen], cw[:, l, chunk, 3:4], None, ALU.mult, reads=[xe, cw], writes=[acc])
                for tap in (2, 1, 0):
                    self.STT(acc3, xe3[:, :, tap:tap + clen], cw[:, l, chunk, tap:tap + 1], acc3, ALU.mult, ALU.add, reads=[xe, cw, acc], writes=[acc])
                    yield
                self.ACT(acc[:, 0:T], acc[:, 0:T], AF.Silu, reads=[acc], writes=[acc])
                yield
                outs.append(acc)
            cq, ck, cv = outs
            psg = self.proj_fm(wcg, wcgs, c0, 128, T, bank=self.ps_dense[hh])
            gs = self.W(7, hh)
            self.ACT(gs[:, 0:T], psg[:, 0:T], AF.Silu, reads=[psg], writes=[gs])
            yield
            sq = self.V(0, hh)
            self.ACT(sq[:, 0:T], cq[:, 0:T], AF.Square, reads=[cq], writes=[sq])
            yield
            bk_ = self.ps_dense[hh]
            self.MM(bk_[:, 0:T], ones_bf, sq[:, 0:T], reads=[self.cstb, sq], writes=[bk_])
            rq = self.W(4, hh)
            self.ACT(rq[:, 0:T], bk_[:, 0:T], AF.Ln, reads=[bk_], writes=[rq], bias=NORM_EPS)
            yield
            self.ACT(rq[:, 0:T], rq[:, 0:T], AF.Exp, reads=[rq], writes=[rq], scale=-0.5)
            yield
            q1 = self.V(2, hh)
            self.STT(q1[:, 0:T], cq[:, 0:T], 128.0 ** -0.5, rq[:, 0:T], ALU.mult, ALU.mult, reads=[cq, rq], writes=[q1])
            yield
            bk_ = self.ps_dense[hh]
            self.MM(bk_[:, 0:T], self.cs("sel", rows=4, c0=h * 128, n=128), eb[0:4, 0:T], reads=[self.cst, eb], writes=[bk_])
            q2 = self.V(3, hh)
            self.TT("dve", q2[:, 0:T], q1[:, 0:T], bk_[:, 0:T], ALU.mult, reads=[q1, bk_], writes=[q2])
            yield
            sq2 = self.V(1, hh)
            self.ACT(sq2[:, 0:T], ck[:, 0:T], AF.Square, reads=[ck], writes=[sq2])
            yield
            bk_ = self.ps_dense[hh]
            self.MM(bk_[:, 0:T], ones_bf, sq2[:, 0:T], reads=[self.cstb, sq2], writes=[bk_])
            rk = self.W(4, hh)
            self.ACT(rk[:, 0:T], bk_[:, 0:T], AF.Ln, reads=[bk_], writes=[rk], bias=NORM_EPS)
            yield
            self.ACT(rk[:, 0:T], rk[:, 0:T], AF.Exp, reads=[rk], writes=[rk], scale=-0.5)
            yield
            kn = ck
            self.TT("dve", kn[:, 0:T], ck[:, 0:T], rk[:, 0:T], ALU.mult, reads=[ck, rk], writes=[kn])
            yield
            knb = self.V(4, hh)
            self.CP("act", knb[:, 0:T], kn[:, 0:T], reads=[kn], writes=[knb])
            yield
            bk_ = self.ps_dense[hh]
            self.MM(bk_[:, 0:nseg], self.cs("sel", rows=4, c0=h * 128, n=128), eblast[0:4, 0:nseg], reads=[self.cst, eblast], writes=[bk_])
            decbc = self.T32(f"c_decbc_{hh}", 16)
            self.CP("dve", decbc[:, 0:nseg], bk_[:, 0:nseg], reads=[bk_], writes=[decbc])
            yield
            ob = self.ps_o[hh]
            selh = self.cs("sel", rows=4, c0=h * 128, n=128)
            bA, bB = self.ps_small[hh], self.ps_dense[hh]
            WN = nt * 128
            Qa = [self.T32(f"c_Qb{i}_{hh}", cfg.TMAX) for i in range(2)]
            QTa = [self.T32(f"c_QTb{i}_{hh}", cfg.TMAX) for i in range(2)]
            PTa = [self.T32(f"c_PTb{i}_{hh}", cfg.TMAX) for i in range(2)]
            Dmb = self.T32(f"c_Dmb_{hh}", cfg.TMAX)
            DmTb = self.T32(f"c_DmTb_{hh}", cfg.TMAX)
            tc_ = lambda j: slice(j * 128, (j + 1) * 128)
            for j in range(nt):
                self.MM(bA[:, tc_(j)], selh, b[0:4, tc_(j)], start=True, stop=False, reads=[self.cst, b], writes=[bA])
                self.MM(bA[:, tc_(j)], identb, pos_strictb, start=False, stop=True, reads=[self.cstb], writes=[bA])
            for j in range(nt):
                self.ACT(Dmb[:, tc_(j)], bA[:, tc_(j)], AF.Exp, reads=[bA, cols], writes=[Dmb], scale=-1.0, bias=col(j, 0, h))
            yield
            for j in range(nt):
                self.MM(bB[:, tc_(j)], selh, b[0:4, tc_(j)], start=True, stop=False, reads=[self.cst, b], writes=[bB])
                self.MM(bB[:, tc_(j)], nidentb, pos_inclb, start=False, stop=True, reads=[self.cstb], writes=[bB])
            for j in range(nt):
                self.ACT(DmTb[:, tc_(j)], bB[:, tc_(j)], AF.Exp, reads=[bB, cols], writes=[DmTb], bias=col(j, 1, h))
            yield
            for j in range(nt):
                self.MM(bA[:, tc_(j)], knb[:, tc_(j)], knb[:, tc_(j)], reads=[knb], writes=[bA])
            for j in range(nt):
                self.STT(Qa[0][:, tc_(j)], bA[:, tc_(j)], col(j, 2, h), Dmb[:, tc_(j)], ALU.mult, ALU.mult, reads=[bA, cols, Dmb], writes=[Qa[0]])
            yield
            for j in range(nt):
                self.TR(bB[:, tc_(j)], Qa[0][:, tc_(j)], ident, reads=[Qa[0], self.cst], writes=[bB])
            for j in range(nt):
                self.TT("dve", PTa[0][:, tc_(j)], bB[:, tc_(j)], ident, ALU.add, reads=[bB, self.cst], writes=[PTa[0]])
            self.CP("dve", QTa[0][:, 0:WN], bB[:, 0:WN], reads=[bB], writes=[QTa[0]])
            yield
            for stp in range(nsteps):
                cur, nxt = stp % 2, (stp + 1) % 2
                lastst = stp == nsteps - 1
                for j in range(nt):
                    self.MM(bA[:, tc_(j)], QTa[cur][:, tc_(j)], Qa[cur][:, tc_(j)], reads=[QTa[cur], Qa[cur]], writes=[bA])
                self.CP("act", Qa[nxt][:, 0:WN], bA[:, 0:WN], reads=[bA], writes=[Qa[nxt]])
                yield
                if not lastst:
                    for j in range(nt):
                        self.TR(bB[:, tc_(j)], Qa[nxt][:, tc_(j)], ident, reads=[Qa[nxt], self.cst], writes=[bB])
                    self.CP("dve", QTa[nxt][:, 0:WN], bB[:, 0:WN], reads=[bB], writes=[QTa[nxt]])
                    yield
                for j in range(nt):
                    self.MM(bA[:, tc_(j)], Qa[nxt][:, tc_(j)], PTa[cur][:, tc_(j)], reads=[Qa[nxt], PTa[cur]], writes=[bA])
                self.TT("dve", PTa[nxt][:, 0:WN], PTa[cur][:, 0:WN], bA[:, 0:WN], ALU.add, reads=[PTa[cur], bA], writes=[PTa[nxt]])
                yield
            PTf = PTa[nsteps % 2]
            for j in range(nt):
                jc = slice(j * 128, (j + 1) * 128)
                PT = self._View(PTf[:, jc], "ptv")
                PT.res = PTf.res
                DmT = self._View(DmTb[:, jc], "dmtv")
                DmT.res = DmTb.res
                aT, aTr = hsmall()
                self.MM(aT, knb[:, jc], q1[:, jc], reads=[knb, q1], writes=[aTr])
                attm = self.T16(f"c_attm_{hh}", 128)
                self.TT("dve", attm[:, :], aT, DmT[:, :], ALU.mult, reads=[aTr, DmT], writes=[attm])
                yield
                kt_, ktr = hsmall()
                self.TR(kt_, kn[:, jc], ident, reads=[kn, self.cst], writes=[ktr])
                kbe = self.T32(f"c_kbe_{hh}", 128)
                self.ACT(kbe[:, :], kt_, AF.Copy, reads=[ktr, cols], writes=[kbe], scale=col(j, 4, h))
                yield
                khat = self.T32(f"c_khat_{hh}", 128)
                self.ACT(khat[:, :], kt_, AF.Copy, reads=[ktr, cols], writes=[khat], scale=col(j, 5, h))
                yield
                vt_, vtr = hsmall()
                self.TR(vt_, cv[:, jc], ident, reads=[cv, self.cst], writes=[vtr])
                vb = self.T32(f"c_vb_{hh}", 128)
                self.ACT(vb[:, :], vt_, AF.Copy, reads=[vtr, cols], writes=[vb], scale=col(j, 3, h))
                yield
                WT, WTr = hsmall()
                self.MM(WT, kbe[:, :], PT[:, :], reads=[kbe, PT], writes=[WTr])
                nWT = self.T32(f"c_nWT_{hh}", 128)
                self.ACT(nWT[:, :], WT, AF.Copy, reads=[WTr], writes=[nWT], scale=-1.0)
                yield
                vnT, vnTr = hsmall()
                self.MM(vnT, vb[:, :], PT[:, :], start=True, stop=False, reads=[vb, PT], writes=[vnTr])
                for i in range(spt):
                    seg = j * spt + i
                    lc = slice(i * seglen, (i + 1) * seglen)
                    if samp:
                        Sf, Sfr = self.SsF[hh][:, seg, 0:128], self.SsF[hh]
                    else:
                        Sft = self.SC[l][h][0]
                        Sf, Sfr = Sft[:, :], Sft
                    self.MM(vnT[:, lc], Sf, nWT[:, lc], start=False, stop=(i == spt - 1), reads=[Sfr, nWT], writes=[vnTr])
                vnTs = self.T32(f"c_vnTs_{hh}", 128)
                self.CP("act", vnTs[:, :], vnT, reads=[vnTr], writes=[vnTs])
                yield
                vn_, vnr = hsmall()
                self.TR(vn_, vnTs[:, :], ident, reads=[vnTs, self.cst], writes=[vnr])
                vnb = self.T16(f"c_vnb_{hh}", 128)
                self.CP("act", vnb[:, :], vn_, reads=[vnr], writes=[vnb])
                yield
                self.MM(ob[:, jc], vnb[:, :], attm[:, :], start=True, stop=False, reads=[vnb, attm], writes=[ob])
                for i in range(spt):
                    seg = j * spt + i
                    sc = slice(seg * seglen, (seg + 1) * seglen)
                    if samp:
                        Sf, Sb, Sfr, Sbr = self.SsF[hh][:, seg, 0:128], self.SsB[hh][:, seg, :], self.SsF[hh], self.SsB[hh]
                    else:
                        Sft, Sbt = self.SC[l][h]
                        Sf, Sb, Sfr, Sbr = Sft[:, :], Sbt[:, :], Sft, Sbt
                    self.MM(ob[:, sc], Sb, q2[:, sc], start=False, stop=(i == spt - 1), reads=[Sbr, q2], writes=[ob])
                    if spt > 1:
                        khm = self.T16(f"khseg{i % 2}_{hh}", 128)
                        self.TS("dve", khm[:, :], khat[:, :], rm[:, i:i + 1], None, ALU.mult, reads=[khat, self.cst], writes=[khm])
                    else:
                        khm = self.T16(f"khseg0_{hh}", 128)
                        self.CP("dve", khm[:, :], khat[:, :], reads=[khat], writes=[khm])
                    pS, pSr = hsmall()
                    self.MM(pS, khm[:, :], vnb[:, :], reads=[khm, vnb], writes=[pSr])
                    self.STT(Sf, Sf, decbc[:, seg:seg + 1], pS, ALU.mult, ALU.add, reads=[Sfr, decbc, pSr], writes=[Sfr])
                    yield
                    if not samp:
                        self.CP("act", Sb, Sf, reads=[Sfr], writes=[Sbr])
                        yield
            self.rms_gate(g, l, 2, h, ob, gs, self.gC, first=(h == 0), last=(h == 3))
            if samp:
                c.dma("sp", self.o["og_s"][l, :, h].rearrange("s k v -> k s v"), self.SsF[hh][:, :, 0:128], reads=[self.SsF[hh]])
            elif g.last:
                c.dma("sp", self.o["og_p"][l, h], self.SC[l][h][0][:, :], reads=[self.SC[l][h][0]])

        self.interleave([head(0), head(1)])


Builder.branch_C = _branch_C


def _ln_prefetch(self, l, which, half):
    c, d = self.c, self.d
    gname, bname = ("ln1_g", "ln1_b") if which == 1 else ("ln2_g", "ln2_b")
    hs = slice(half * 512, (half + 1) * 512)
    c.dma("sp", self.lnp[0][:], d[gname][l][hs].partition_broadcast(128), writes=[self.lnp[0]])
    c.dma("sp", self.lnp[1][:], d[bname][l][hs].partition_broadcast(128), writes=[self.lnp[1]])


def _layer_norm(self, g, l, which):
    c, d = self.c, self.d
    st = self.T32("ln_st", 8 * self.cfg.tg)
    junk = self.big[:, 0:D // 2].bitcast(BF16)
    junkr = self.big
    tiles = range(g.nt)
    X = lambda j: self.h_tok[:, j, :]
    S = lambda j: st[:, j * 8:(j + 1) * 8]
    for j in tiles:
        self.ACT(junk, X(j), AF.Identity, reads=[self.h_tok], writes=[junkr, st], accum=S(j)[:, 0:1])
    for j in tiles:
        self.TS("dve", S(j)[:, 1:2], S(j)[:, 0:1], -1.0 / D, None, ALU.mult, reads=[st], writes=[st])
    for j in tiles:
        self.ACT(junk, X(j), AF.Square, reads=[self.h_tok, st], writes=[junkr, st], bias=S(j)[:, 1:2], accum=S(j)[:, 2:3])
    for j in tiles:
        self.ACT(S(j)[:, 3:4], S(j)[:, 2:3], AF.Ln, reads=[st], writes=[st], scale=1.0 / D, bias=LN_EPS)
    for j in tiles:
        self.ACT(S(j)[:, 3:4], S(j)[:, 3:4], AF.Exp, reads=[st], writes=[st], scale=-0.5)
    for j in tiles:
        self.TS("dve", X(j), X(j), S(j)[:, 1:2], S(j)[:, 3:4], ALU.add, ALU.mult, reads=[self.h_tok, st], writes=[self.h_tok])
    for half in range(2):
        hs = slice(half * 512, (half + 1) * 512)
        if half == 1:
            self.ln_prefetch(l, which, 1)
        for j in tiles:
            x = self.h_tok[:, j, hs]
            self.TT("dve", x, x, self.lnp[0][:], ALU.mult, reads=[self.h_tok, self.lnp[0]], writes=[self.h_tok])
        for j in tiles:
            x = self.h_tok[:, j, hs]
            self.TT("dve", x, x, self.lnp[1][:], ALU.add, reads=[self.h_tok, self.lnp[1]], writes=[self.h_tok])


def _merge(self, g, l):
    c, cfg, d = self.c, self.cfg, self.d
    self.dense_pool = self.ps_dense + self.ps_small + self.ps_rows + self.ps_o
    T, nt, TM = g.T, g.nt, cfg.TMAX
    self.ln_prefetch(l, 1, 0)
    big = self.big
    macc = big[:, 0:8 * TM].rearrange("p (k t) -> p k t", t=TM)
    mbf = big[:, 8 * TM:12 * TM].bitcast(BF16).rearrange("p (k t) -> p k t", t=TM)
    for n in range(3):
        wbr = d["w_branch"][l, n].rearrange("(k p) n -> p k n", p=128)
        for dh in range(2):
            wbv, wbs = self.ws.get(("w_branch", l, n, dh), wbr[:, :, dh * 512:(dh + 1) * 512], 4, 512)
            for mgb in range(2):
                wmg, wmgs = self.win(l, 5648 + n * 1024 + dh * 512 + mgb * 256, 256)
                for q in range(2):
                    j = dh * 4 + mgb * 2 + q
                    psm = self.proj_fm(wmg, wmgs, q * 128, 128, T)
                    sg = self.W(0, 0)
                    self.ACT(sg[:, 0:T], psm[:, 0:T], AF.Sigmoid, reads=[psm], writes=[sg])
                    psz = self.dense_bank()
                    for k in range(4):
                        self.MM(psz[:, 0:T], wbv[:, k, (mgb * 2 + q) * 128:(mgb * 2 + q + 1) * 128], self.yT[n][:, k, 0:T],
                                start=(k == 0), stop=(k == 3), reads=[wbs, self.yT[n]], writes=[psz])
                    if n == 0:
                        self.TT("dve", macc[:, j, 0:T], sg[:, 0:T], psz[:, 0:T], ALU.mult, reads=[sg, psz], writes=[big])
                    else:
                        prod = self.W(1, 0)
                        self.TT("dve", prod[:, 0:T], sg[:, 0:T], psz[:, 0:T], ALU.mult, reads=[sg, psz], writes=[prod])
                        if n == 1:
                            self.TT("dve", macc[:, j, 0:T], macc[:, j, 0:T], prod[:, 0:T], ALU.add, reads=[big, prod], writes=[big])
                        else:
                            self.TT("dve", mbf[:, j, 0:T], macc[:, j, 0:T], prod[:, 0:T], ALU.add, reads=[big, prod], writes=[big])
    wor = d["w_out"][l].rearrange("(k p) n -> p k n", p=128)
    for half in range(2):
        hs = slice(half * 512, (half + 1) * 512)
        w0, w0s = self.ws.get(("w_out", l, 0, half), wor[:, 0:4, hs], 4, 512)
        w1, w1s = self.ws.get(("w_out", l, 1, half), wor[:, 4:8, hs], 4, 512)
        for j in range(nt):
            bank = self.dense_bank()
            for k in range(8):
                wv, wsx = (w0, w0s) if k < 4 else (w1, w1s)
                self.MM(bank[:, 0:512], mbf[:, k, j * 128:(j + 1) * 128], wv[:, k % 4, :], start=(k == 0), stop=(k == 7),
                        reads=[big, wsx], writes=[bank])
            self.STT(self.h_tok[:, j, hs], self.h_tok[:, j, hs], ALPHA, bank[:, 0:512], ALU.mult, ALU.add,
                     reads=[self.h_tok, bank], writes=[self.h_tok])
    self.dense_pool = self.ps_dense
    self.layer_norm(g, l, 1)
    for j in range(nt):
        self.dump(f"h1_{j}_l{l}_g{g.idx}", self.h_tok, self.h_tok[:, j, :], 128, D)
    self.make_hT(g)


def _ffn(self, g, l):
    c, cfg, d = self.c, self.cfg, self.d
    self.dense_pool = self.ps_dense + self.ps_small + self.ps_rows
    T, nt, TM = g.T, g.nt, cfg.TMAX
    self.ln_prefetch(l, 2, 0)
    big = self.big
    aT = big[:, 0:11 * TM].bitcast(BF16).rearrange("p (k t) -> p k t", t=TM)
    wfi = d["w_ffn_in"][l].rearrange("(k p) n -> p k n", p=128)
    for part in range(2):
        for blk in range(11):
            wv, wsx = self.ws.get(("w_ffn_in", l, part, blk), wfi[:, :, part * FH + blk * 256:part * FH + (blk + 1) * 256], 8, 256)
            for q in range(2):
                j = blk * 2 + q
                ps = self.proj_fm(wv, wsx, q * 128, 128, T)
                if part == 0:
                    self.ACT(aT[:, j, 0:T], ps[:, 0:T], AF.Silu, reads=[ps], writes=[big])
                else:
                    self.TT("dve", aT[:, j, 0:T], aT[:, j, 0:T], ps[:, 0:T], ALU.mult, reads=[big, ps], writes=[big])
    wfo = d["w_ffn_out"][l].rearrange("(k p) n -> p k n", p=128)
    for half in range(2):
        hs = slice(half * 512, (half + 1) * 512)
        accs = [self.ps_o[j % 2] for j in range(nt)]
        assert nt <= 2
        for kb in range(6):
            k0 = kb * 4
            nk = min(4, 22 - k0)
            wv, wsx = self.ws.get(("w_ffn_out", l, kb, half), wfo[:, k0:k0 + nk, hs], nk, 512)
            for j in range(nt):
                for kk in range(nk):
                    k = k0 + kk
                    self.MM(accs[j][:, 0:512], aT[:, k, j * 128:(j + 1) * 128], wv[:, kk, :], start=(k == 0), stop=(k == 21),
                            reads=[big, wsx], writes=[accs[j]])
        for j in range(nt):
            self.STT(self.h_tok[:, j, hs], self.h_tok[:, j, hs], ALPHA, accs[j][:, 0:512], ALU.mult, ALU.add,
                     reads=[self.h_tok, accs[j]], writes=[self.h_tok])
    self.dense_pool = self.ps_dense
    self.layer_norm(g, l, 2)
    if l < cfg.depth - 1:
        self.make_hT(g)


Builder.layer_norm = _layer_norm
Builder.ln_prefetch = _ln_prefetch
Builder.merge = _merge
Builder.ffn = _ffn


_W_NAMES = ["w_in", "lb_logits", "a_norm_g", "b_mi", "b_mf", "b_norm_g", "conv_w", "a_log", "dt_bias",
            "c_norm_g", "w_branch", "w_out", "ln1_g", "ln1_b", "w_ffn_in", "w_ffn_out", "ln2_g", "ln2_b"]


def run_cfg(cfg, inputs, n_cores=8, trace=False):
    b = Builder(cfg)
    nc = b.build()
    consts = _make_consts()
    f = lambda a: np.ascontiguousarray(a, dtype=np.float32)
    in_maps = []
    for i in range(n_cores):
        s0, s1 = i * 16, (i + 1) * 16
        m = dict(
            xp=f(inputs["x_prompt"][i][:cfg.seq]),
            xs=f(inputs["x_sample"][s0:s1].reshape(128, D)),
            st_a=f(inputs["state_hgrn"][:cfg.depth, s0:s1]), st_c=f(inputs["state_mlstm_c"][:cfg.depth, s0:s1]),
            st_n=f(inputs["state_mlstm_n"][:cfg.depth, s0:s1]), st_m=f(inputs["state_mlstm_m"][:cfg.depth, s0:s1]),
            st_g=f(inputs["state_gdn"][:cfg.depth, s0:s1]), st_v=f(inputs["state_gdn_conv"][:cfg.depth, s0:s1]),
            consts=consts,
        )
        for nm in _W_NAMES:
            m[nm] = f(inputs[nm][:cfg.depth])
        in_maps.append(m)
    res = run_bass_kernel_spmd(nc, in_maps, core_ids=list(range(n_cores)), **({"trace": True} if trace else {}))
    return res, b


def kernel(**inputs):
    cfg = Cfg(depth=DEPTH, n_pg=8, tg=2, sample=True)
    res, b = run_cfg(cfg, inputs, 8)
    R = res.results
    cat = lambda k, ax: np.concatenate([r[k] for r in R], axis=ax)
    stk = lambda k: np.stack([r[k] for r in R], axis=1)
    y_p = np.stack([r["yp"] for r in R], axis=0)
    y_s = np.concatenate([r["ys"].reshape(16, 8, D) for r in R], axis=0)
    outs = (y_p, y_s,
            stk("oa_p"), cat("oa_s", 1), stk("oc_p"), cat("oc_s", 1), stk("on_p"), cat("on_s", 1),
            stk("om_p"), cat("om_s", 1), stk("og_p"), cat("og_s", 1), stk("ov_p"), cat("ov_s", 1))
    return tuple(np.ascontiguousarray(o, dtype=np.float32) for o in outs)
```
